# Optimizing a Trainium2 kernel written in Bass

```python
import math
import jax, jax.numpy as jnp
from jax import lax
import numpy as np

D_MODEL = 1024
BATCH = 4
SEQ = 8192
DEPTH = 4

GRID_W = 64
CTX_LEN = 256
N_MIXERS = 3
D_FF = 2816
N_MOD = 9
EPS = 1e-6

SSD_D_INNER = 2 * D_MODEL
SSD_HEAD_DIM = 64
SSD_N_HEADS = SSD_D_INNER // SSD_HEAD_DIM
SSD_N_GROUPS = 4
SSD_D_STATE = 128
SSD_CONV_W = 5
SSD_CHUNK = 128
SSD_GN = SSD_N_GROUPS * SSD_D_STATE
SSD_CONV_DIM = SSD_D_INNER + 2 * SSD_GN
SSD_IN_DIM = SSD_D_INNER + SSD_CONV_DIM + 2 * SSD_N_HEADS

POOL_WINDOWS = (2, 4, 8, 16)
POOL_GROUPS = 4
POOL_GROUP_DIM = D_MODEL // POOL_GROUPS

HY_ORDER = 2
HY_SHORT_W = 3
HY_EMB_DIM = 33
HY_BANDS = (HY_EMB_DIM - 1) // 2
HY_FILTER_HIDDEN = 64
HY_MAX_DECAY = math.log(1e-2) / 0.3
HY_MIN_DECAY = math.log(1e-2) / 1.5

kernel_name = "hybrid_ssd_pool_hyena_prefix_dit"


def rmsnorm(x, g):
    xf = x.astype(jnp.float32)
    y = xf * lax.rsqrt(jnp.mean(xf * xf, axis=-1, keepdims=True) + EPS)
    return (y * g.astype(jnp.float32)).astype(x.dtype)


def modulate(x, g, shift, scale):
    return rmsnorm(x, g) * (1 + scale) + shift


def swiglu(u, wg, wu, wd):
    return (jax.nn.silu(u @ wg) * (u @ wu)) @ wd


def dwconv_centred(u, w, b):
    K = w.shape[0]
    P = K // 2
    L = u.shape[1]
    up = jnp.pad(u, ((0, 0), (P, P), (0, 0)))
    out = b
    for k in range(K):
        out = out + w[k] * up[:, k:k + L]
    return out


def ssd_scan(x, dt, A, B, C, h0, with_output):
    b, l, H, P = x.shape
    G, N = B.shape[2], B.shape[3]
    E = H // G
    Q = SSD_CHUNK
    nc = l // Q
    xdt = (x.astype(jnp.float32) * dt[..., None]).reshape(b, nc, Q, G, E, P)
    Bc = B.astype(jnp.float32).reshape(b, nc, Q, G, N)
    Cc = C.astype(jnp.float32).reshape(b, nc, Q, G, N)
    a_cum = jnp.cumsum((dt * A).reshape(b, nc, Q, G, E), axis=2)
    a_last = a_cum[:, :, -1]
    decay_to_end = jnp.exp(a_last[:, :, None] - a_cum)
    states = jnp.einsum('bcsgn,bcsge,bcsgep->bcgepn', Bc, decay_to_end, xdt)

    def step(h, inp):
        st, al = inp
        return h * jnp.exp(al)[..., None, None] + st, h

    h_final, h_in = lax.scan(step, h0, (jnp.moveaxis(states, 1, 0), jnp.moveaxis(a_last, 1, 0)))
    if not with_output:
        return None, h_final
    h_in = jnp.moveaxis(h_in, 0, 1)
    seg = a_cum[:, :, :, None] - a_cum[:, :, None, :]
    mask = jnp.tril(jnp.ones((Q, Q), dtype=bool))[:, :, None, None]
    Lmat = jnp.exp(jnp.where(mask, seg, -jnp.inf))
    cb = jnp.einsum('bclgn,bcsgn->bclsg', Cc, Bc)
    y_diag = jnp.einsum('bclsg,bclsge,bcsgep->bclgep', cb, Lmat, xdt)
    y_off = jnp.einsum('bclgn,bcgepn,bclge->bclgep', Cc, h_in, jnp.exp(a_cum))
    y = (y_diag + y_off).reshape(b, l, H, P)
    return y.astype(x.dtype), h_final


def ssd_mixer(u_lat, u_ctx, w_in, conv_w, conv_b, a_log, dt_bias, d_skip, norm_g, w_out, need_ctx_out):
    H, P, G, N = SSD_N_HEADS, SSD_HEAD_DIM, SSD_N_GROUPS, SSD_D_STATE
    A = -jnp.exp(a_log.astype(jnp.float32))

    def project(u):
        b, l, _ = u.shape
        zxbcdt = u @ w_in
        z = zxbcdt[..., :SSD_D_INNER]
        xbc = jax.nn.silu(dwconv_centred(zxbcdt[..., SSD_D_INNER:SSD_D_INNER + SSD_CONV_DIM], conv_w, conv_b))
        dt_raw = zxbcdt[..., SSD_D_INNER + SSD_CONV_DIM:]
        xs = xbc[..., :SSD_D_INNER].reshape(b, l, H, P)
        Bm = xbc[..., SSD_D_INNER:SSD_D_INNER + SSD_GN].reshape(b, l, G, N)
        Cm = xbc[..., SSD_D_INNER + SSD_GN:].reshape(b, l, G, N)
        dt = jax.nn.softplus(dt_raw.reshape(b, l, 2, H).astype(jnp.float32) + dt_bias.astype(jnp.float32))
        return z, xs, Bm, Cm, dt

    def flip(t, d):
        return t[:, ::-1] if d == 1 else t

    zc, xc, Bc, Cc, dtc = project(u_ctx)
    zl, xl, Bl, Cl, dtl = project(u_lat)
    b = u_lat.shape[0]
    h0 = jnp.zeros((b, G, H // G, P, N), jnp.float32)
    y_lat = d_skip[:, None] * xl
    y_ctx = d_skip[:, None] * xc if need_ctx_out else None
    for d in range(2):
        yc, hc = ssd_scan(flip(xc, d), flip(dtc[:, :, d], d), A[d], flip(Bc, d), flip(Cc, d), h0, need_ctx_out)
        yl, _ = ssd_scan(flip(xl, d), flip(dtl[:, :, d], d), A[d], flip(Bl, d), flip(Cl, d), hc, True)
        y_lat = y_lat + flip(yl, d)
        if need_ctx_out:
            y_ctx = y_ctx + flip(yc, d)

    def gate_norm_out(y, z):
        b_, l_ = y.shape[:2]
        gy = (y.reshape(b_, l_, SSD_D_INNER) * jax.nn.silu(z)).reshape(b_, l_, G, SSD_D_INNER // G)
        gy = rmsnorm(gy, norm_g.reshape(G, SSD_D_INNER // G))
        return gy.reshape(b_, l_, SSD_D_INNER) @ w_out

    out_lat = gate_norm_out(y_lat, zl)
    out_ctx = gate_norm_out(y_ctx, zc) if need_ctx_out else None
    return out_lat, out_ctx


def box_sum(v, w, axis):
    L = v.shape[axis]
    cs = jnp.cumsum(v, axis=axis)
    pad = [(0, 0)] * v.ndim
    pad[axis] = (1, 0)
    cs = jnp.pad(cs, pad)
    pos = jnp.arange(L)
    hi = jnp.minimum(pos + (w - w // 2), L)
    lo = jnp.maximum(pos - w // 2, 0)
    s = jnp.take(cs, hi, axis=axis) - jnp.take(cs, lo, axis=axis)
    return s, (hi - lo).astype(jnp.float32)


def pool_grid(v, w):
    b, l, C = v.shape
    rows = l // GRID_W
    v4 = v.astype(jnp.float32).reshape(b, rows, GRID_W, C)
    s, cnt_c = box_sum(v4, w, 2)
    s, cnt_r = box_sum(s, w, 1)
    mean = s / (cnt_r[:, None, None] * cnt_c[None, :, None])
    return mean.reshape(b, l, C).astype(v.dtype) - v


def pool_seq(v, w):
    s, cnt = box_sum(v.astype(jnp.float32), w, 1)
    return (s / cnt[None, :, None]).astype(v.dtype) - v


def pool_mixer(u_lat, u_ctx, w, bias, scale, need_ctx_out):
    def mix(u, pool_fn):
        b, l, _ = u.shape
        grp = u.reshape(b, l, POOL_GROUPS, POOL_GROUP_DIM)
        pooled = jnp.stack([pool_fn(grp[:, :, g], POOL_WINDOWS[g]) for g in range(POOL_GROUPS)], axis=2)
        y = jnp.einsum('blgc,gcd->blgd', pooled, w) + bias
        return y.reshape(b, l, D_MODEL) * scale

    out_lat = mix(u_lat, pool_grid)
    out_ctx = mix(u_ctx, pool_seq) if need_ctx_out else None
    return out_lat, out_ctx


def hyena_filter_fft(L, fw1, fb1, fw2, fb2, fw3, ffreq):
    f32 = jnp.float32
    pos = jnp.arange(L, dtype=f32)
    t = jnp.linspace(0.0, 1.0, L, dtype=f32)
    wpos = 2.0 * math.pi * pos / L
    f = jnp.linspace(1e-4, HY_BANDS - 1, HY_BANDS, dtype=f32)
    ang = wpos[:, None] * f[None, :]
    z = jnp.concatenate([t[:, None], jnp.cos(ang), -jnp.sin(ang)], axis=-1)
    h = jnp.sin(ffreq[0].astype(f32) * (z @ fw1.astype(f32) + fb1.astype(f32)))
    h = jnp.sin(ffreq[1].astype(f32) * (h @ fw2.astype(f32) + fb2.astype(f32)))
    h = (h @ fw3.astype(f32)).reshape(L, HY_ORDER, 2, D_MODEL)
    deltas = jnp.abs(jnp.linspace(HY_MIN_DECAY, HY_MAX_DECAY, D_MODEL, dtype=f32))
    h = h * jnp.exp(-t[:, None, None, None] * deltas)
    h = h * lax.rsqrt(jnp.sum(h * h, axis=(0, 2), keepdims=True) + EPS)
    kern = jnp.concatenate([h[:, :, 0], jnp.zeros((1, HY_ORDER, D_MODEL), f32), h[:0:-1, :, 1]], axis=0)
    return jnp.fft.rfft(kern, axis=0)


def fftconv(u, kf, bias):
    L = u.shape[1]
    uf = u.astype(jnp.float32)
    y = jnp.fft.irfft(jnp.fft.rfft(uf, n=2 * L, axis=1) * kf, n=2 * L, axis=1)[:, :L]
    return (y + uf * bias.astype(jnp.float32)).astype(u.dtype)


def hyena_mixer(u_lat, u_ctx, w_in, conv_w, conv_b, fw1, fb1, fw2, fb2, fw3, ffreq, hbias, w_out, need_ctx_out):
    def run(u):
        kf = hyena_filter_fft(u.shape[1], fw1, fb1, fw2, fb2, fw3, ffreq)
        p = dwconv_centred(u @ w_in, conv_w, conv_b)
        v, x1, x2 = jnp.split(p, 3, axis=-1)
        z = x1 * fftconv(v, kf[:, 0], hbias[0])
        y = x2 * fftconv(z, kf[:, 1], hbias[1])
        return y @ w_out

    out_lat = run(u_lat)
    out_ctx = run(u_ctx) if need_ctx_out else None
    return out_lat, out_ctx


def n_layers_of(kind):
    return len(range(kind, DEPTH, N_MIXERS))


def setup_inputs(seed: int = 0) -> dict:
    key = jax.random.key(seed)
    ks = jax.random.split(key, 32)
    f32 = jnp.float32

    def nrm(k, shape, scale):
        return jax.random.normal(k, shape, f32) * scale

    nA, nB, nC = n_layers_of(0), n_layers_of(1), n_layers_of(2)
    D, F = D_MODEL, D_FF
    H = SSD_N_HEADS
    dt0 = jnp.exp(jax.random.uniform(ks[12], (nA, 2, H), f32, math.log(1e-3), math.log(1e-1)))
    return {
        "x": nrm(ks[0], (BATCH, SEQ, D), 1.0),
        "c": nrm(ks[1], (BATCH, D), 1.0),
        "ctx": nrm(ks[2], (BATCH, CTX_LEN, D), 1.0),
        "c_ctx": nrm(ks[3], (D,), 1.0),
        "ada_w": nrm(ks[4], (DEPTH, D, N_MOD * D), 0.5 * D ** -0.5),
        "ada_b": nrm(ks[5], (DEPTH, N_MOD * D), 0.02),
        "norm_g": 1.0 + nrm(ks[6], (DEPTH, 3, D), 0.05),
        "ffn_w_gate": nrm(ks[7], (DEPTH, 2, D, F), D ** -0.5),
        "ffn_w_up": nrm(ks[8], (DEPTH, 2, D, F), D ** -0.5),
        "ffn_w_down": nrm(ks[9], (DEPTH, 2, F, D), F ** -0.5),
        "ssd_w_in": nrm(ks[10], (nA, D, SSD_IN_DIM), D ** -0.5),
        "ssd_conv_w": nrm(ks[11], (nA, SSD_CONV_W, SSD_CONV_DIM), SSD_CONV_W ** -0.5),
        "ssd_conv_b": nrm(ks[13], (nA, SSD_CONV_DIM), 0.02),
        "ssd_a_log": jnp.log(jax.random.uniform(ks[14], (nA, 2, H), f32, 1.0, 16.0)),
        "ssd_dt_bias": dt0 + jnp.log(-jnp.expm1(-dt0)),
        "ssd_d": 1.0 + nrm(ks[15], (nA, H), 0.1),
        "ssd_norm_g": 1.0 + nrm(ks[16], (nA, SSD_D_INNER), 0.05),
        "ssd_w_out": nrm(ks[17], (nA, SSD_D_INNER, D), SSD_D_INNER ** -0.5),
        "pool_w": nrm(ks[18], (nB, POOL_GROUPS, POOL_GROUP_DIM, POOL_GROUP_DIM), POOL_GROUP_DIM ** -0.5),
        "pool_b": nrm(ks[19], (nB, POOL_GROUPS, POOL_GROUP_DIM), 0.02),
        "pool_scale": 1.0 + nrm(ks[20], (nB, D), 0.1),
        "hy_w_in": nrm(ks[21], (nC, D, 3 * D), D ** -0.5),
        "hy_conv_w": nrm(ks[22], (nC, HY_SHORT_W, 3 * D), HY_SHORT_W ** -0.5),
        "hy_conv_b": nrm(ks[23], (nC, 3 * D), 0.02),
        "hy_filt_w1": nrm(ks[24], (nC, HY_EMB_DIM, HY_FILTER_HIDDEN), HY_EMB_DIM ** -0.5),
        "hy_filt_b1": nrm(ks[25], (nC, HY_FILTER_HIDDEN), 0.1),
        "hy_filt_w2": nrm(ks[26], (nC, HY_FILTER_HIDDEN, HY_FILTER_HIDDEN), HY_FILTER_HIDDEN ** -0.5),
        "hy_filt_b2": nrm(ks[27], (nC, HY_FILTER_HIDDEN), 0.1),
        "hy_filt_w3": nrm(ks[28], (nC, HY_FILTER_HIDDEN, HY_ORDER * 2 * D), HY_FILTER_HIDDEN ** -0.5),
        "hy_filt_freq": 1.0 + nrm(ks[29], (nC, 2, HY_FILTER_HIDDEN), 0.1),
        "hy_bias": nrm(ks[30], (nC, HY_ORDER, D), 0.1),
        "hy_w_out": nrm(ks[31], (nC, D, D), D ** -0.5),
        "final_g": 1.0 + nrm(jax.random.fold_in(key, 99), (D,), 0.05),
    }


def reference(x, c, ctx, c_ctx, ada_w, ada_b, norm_g, ffn_w_gate, ffn_w_up, ffn_w_down,
              ssd_w_in, ssd_conv_w, ssd_conv_b, ssd_a_log, ssd_dt_bias, ssd_d, ssd_norm_g, ssd_w_out,
              pool_w, pool_b, pool_scale,
              hy_w_in, hy_conv_w, hy_conv_b, hy_filt_w1, hy_filt_b1, hy_filt_w2, hy_filt_b2, hy_filt_w3,
              hy_filt_freq, hy_bias, hy_w_out, final_g):
    h = ctx
    sc = jax.nn.silu(c)
    scc = jax.nn.silu(c_ctx)
    for i in range(DEPTH):
        kind = i % N_MIXERS
        j = i // N_MIXERS
        last = i == DEPTH - 1
        ctx_in_needed = (not last) or kind == 0
        m = (sc @ ada_w[i] + ada_b[i]).reshape(-1, N_MOD, D_MODEL)[:, :, None, :]
        mc = (scc @ ada_w[i] + ada_b[i]).reshape(N_MOD, D_MODEL)[:, None, None, :]

        x = x + 0.5 * m[:, 2] * swiglu(modulate(x, norm_g[i, 0], m[:, 0], m[:, 1]),
                                       ffn_w_gate[i, 0], ffn_w_up[i, 0], ffn_w_down[i, 0])
        if ctx_in_needed:
            h = h + 0.5 * mc[2] * swiglu(modulate(h, norm_g[i, 0], mc[0], mc[1]),
                                         ffn_w_gate[i, 0], ffn_w_up[i, 0], ffn_w_down[i, 0])

        u = modulate(x, norm_g[i, 1], m[:, 3], m[:, 4])
        uc = modulate(h, norm_g[i, 1], mc[3], mc[4]) if ctx_in_needed else None
        if kind == 0:
            y, yc = ssd_mixer(u, uc, ssd_w_in[j], ssd_conv_w[j], ssd_conv_b[j], ssd_a_log[j], ssd_dt_bias[j],
                              ssd_d[j], ssd_norm_g[j], ssd_w_out[j], not last)
        elif kind == 1:
            y, yc = pool_mixer(u, uc, pool_w[j], pool_b[j], pool_scale[j], not last)
        else:
            y, yc = hyena_mixer(u, uc, hy_w_in[j], hy_conv_w[j], hy_conv_b[j], hy_filt_w1[j], hy_filt_b1[j],
                                hy_filt_w2[j], hy_filt_b2[j], hy_filt_w3[j], hy_filt_freq[j], hy_bias[j],
                                hy_w_out[j], not last)
        x = x + m[:, 5] * y
        if not last:
            h = h + mc[5] * yc

        x = x + 0.5 * m[:, 8] * swiglu(modulate(x, norm_g[i, 2], m[:, 6], m[:, 7]),
                                       ffn_w_gate[i, 1], ffn_w_up[i, 1], ffn_w_down[i, 1])
        if not last:
            h = h + 0.5 * mc[8] * swiglu(modulate(h, norm_g[i, 2], mc[6], mc[7]),
                                         ffn_w_gate[i, 1], ffn_w_up[i, 1], ffn_w_down[i, 1])
    return rmsnorm(x, final_g)
```

```python
import contextlib
import math
import numpy as np
import concourse.bass as bass
import concourse.mybir as mybir
from concourse.bass_utils import run_bass_kernel_spmd

F32 = mybir.dt.float32
BF16 = mybir.dt.bfloat16
AF = mybir.ActivationFunctionType
ALU = mybir.AluOpType
AX = mybir.AxisListType

D = 1024
DFF = 2816
NMOD = 9
DEPTH = 4
CTX = 256
EPS = 1e-6
KC = 8
FC = 22
GRID_W = 64


class Buf:
    def __init__(self, t, name=""):
        self.t = t
        self.w = None
        self.r = {}
        self.name = name

    def __getitem__(self, idx):
        return self.t[idx]


class Sched:
    ENGS = ["pe", "act", "dve", "pool", "sp"]

    def __init__(self, nc, st, nds=40):
        self.nc = nc
        self.sem = {e: st.enter_context(nc.semaphore("sem_" + e)) for e in self.ENGS}
        self.cnt = {e: 0 for e in self.ENGS}
        self.prog = {e: [] for e in self.ENGS}
        self.known = {e: {} for e in self.ENGS}
        self.dsem = [st.enter_context(nc.semaphore("dsem%d" % i)) for i in range(nds)]
        self.dcnt = [0] * nds
        self.dnext = 0

    def _need(self, eng, dep, waits):
        if dep is None:
            return
        kind, a, v = dep
        if kind == "E":
            if a == eng and a == "pe":
                return
            key = ("E", a)
        else:
            key = ("D", a)
        if self.known[eng].get(key, 0) >= v:
            return
        self.known[eng][key] = v
        waits.append((key, v))

    def _deps(self, eng, reads, writes):
        waits = []
        for b in reads:
            self._need(eng, b.w, waits)
        for b in writes:
            self._need(eng, b.w, waits)
            for d in b.r.values():
                self._need(eng, d, waits)
        return waits

    def op(self, eng, fn, reads=(), writes=()):
        waits = self._deps(eng, reads, writes)
        self.cnt[eng] += 1
        dep = ("E", eng, self.cnt[eng])
        self.prog[eng].append((waits, fn, dep))
        for b in writes:
            b.w = dep
            b.r = {}
        for b in reads:
            if b not in writes:
                b.r[eng] = dep

    def dma(self, q, out, in_, reads=(), writes=(), **kw):
        waits = self._deps(q, reads, writes)
        s = self.dnext
        self.dnext = (self.dnext + 1) % len(self.dsem)
        if self.dcnt[s] > 0:
            self._need(q, ("D", s, self.dcnt[s]), waits)
        self.dcnt[s] += 16
        dep = ("D", s, self.dcnt[s])
        self.prog[q].append((waits, (lambda e: e.dma_start(out=out, in_=in_, **kw)), dep))
        for b in writes:
            b.w = dep
            b.r = {}
        for b in reads:
            if b not in writes:
                b.r["dma%d" % s] = dep

    def barrier(self):
        for e in self.ENGS:
            waits = []
            for f in self.ENGS:
                if f != e and self.cnt[f] > 0:
                    self._need(e, ("E", f, self.cnt[f]), waits)
            for s in range(len(self.dsem)):
                if self.dcnt[s] > 0:
                    self._need(e, ("D", s, self.dcnt[s]), waits)
            if waits:
                self.prog[e].append((waits, None, None))

    def emit(self, block):
        decos = dict(pe=block.tensor, act=block.scalar, dve=block.vector, pool=block.gpsimd, sp=block.sync)
        for name in self.ENGS:
            prog = self.prog[name]

            def body(e, prog=prog, name=name):
                for waits, fn, dep in prog:
                    for key, v in waits:
                        sem = self.sem[key[1]] if key[0] == "E" else self.dsem[key[1]]
                        e.wait_ge(sem, v)
                    if fn is None:
                        continue
                    ins = fn(e)
                    if dep[0] == "E":
                        ins.then_inc(self.sem[name], 1)
                    else:
                        ins.then_inc(self.dsem[dep[1]], 16)

            decos[name](body)


class Ctx:
    pass


def build_program(L, layers=DEPTH, mixers=(0, 1, 2), dbg_ctx=False):
    NT = CTX + L
    NG = NT // 256
    nc = bass.Bass("TRN2", target_bir_lowering=False)
    P = Ctx()
    P.nc = nc
    P.dbg_ctx = dbg_ctx
    P.L = L
    P.NT = NT
    P.NG = NG

    def din(name, shape, dt=F32):
        return nc.dram_tensor(name, list(shape), dt, kind="ExternalInput").ap()

    def dscr(name, shape, dt=F32):
        return nc.dram_tensor(name, list(shape), dt, kind="Internal").ap()

    I = {}
    I["x"] = din("x", [L, D])
    I["ctx"] = din("ctx", [CTX, D])
    I["c"] = din("c", [KC, 128])
    I["c_ctx"] = din("c_ctx", [KC, 128])
    I["ada_w"] = din("ada_w", [DEPTH, D, NMOD * D])
    I["ada_b"] = din("ada_b", [DEPTH, NMOD * KC, 128])
    I["norm_g"] = din("norm_g", [DEPTH, 3 * KC, 128])
    I["norm_g_rows"] = din("norm_g_rows", [DEPTH, 3, D])
    I["ffn_w_gate"] = din("ffn_w_gate", [DEPTH, 2, D, DFF])
    I["ffn_w_up"] = din("ffn_w_up", [DEPTH, 2, D, DFF])
    I["ffn_w_down"] = din("ffn_w_down", [DEPTH, 2, DFF, D])
    I["final_g"] = din("final_g", [1, D])
    I["ident"] = din("ident", [128, 128])
    NK = L // 128
    I["poolP"] = din("poolP", [19, 128, 128], BF16)
    I["poolPc"] = din("poolPc", [16, 128, 128], BF16)
    I["pool_inv"] = din("pool_inv", [128, NK, 4])
    I["pool_invc"] = din("pool_invc", [128, 2, 4])
    I["pool_w"] = din("pool_w", [1, 4, 256, 256])
    I["pool_b"] = din("pool_b", [1, D])
    I["pool_scale"] = din("pool_scale", [1, D])
    I["ssd_w_in"] = din("ssd_w_in", [2, D, SSD_IN])
    I["ssd_conv_w"] = din("ssd_conv_w", [2, 5, 3072])
    I["ssd_conv_b"] = din("ssd_conv_b", [2, 3072])
    I["ssd_a_log"] = din("ssd_a_log", [2, 64])
    I["ssd_dt_bias"] = din("ssd_dt_bias", [2, 64])
    I["ssd_d"] = din("ssd_d", [2, 32])
    I["ssd_norm_g"] = din("ssd_norm_g", [2, 2048])
    I["ssd_w_out"] = din("ssd_w_out", [2, 2048, D])
    I["ssd_U"] = din("ssd_U", [6, 128, 128])
    NCH = NT // 128
    P.UT = dscr("UT", [KC, 128, NT], BF16)
    P.SZ = dscr("SZ", [NT, 2048])
    P.YP = dscr("YP", [NT, 2048])
    P.SBs = dscr("SBs", [NCH, 128, 2048])
    P.CT = dscr("CT", [NCH, 128, 512], BF16)
    P.EB = dscr("EB", [NCH, 128, 64])
    I["hy_w_in"] = din("hy_w_in", [1, D, 3072])
    I["hy_conv_w"] = din("hy_conv_w", [1, 3, 3072])
    I["hy_conv_b"] = din("hy_conv_b", [1, 3072])
    I["hy_filt_w1"] = din("hy_filt_w1", [1, 33, 64])
    I["hy_filt_w2"] = din("hy_filt_w2", [1, 64, 64])
    I["hy_filt_w3"] = din("hy_filt_w3", [1, 64, 4096])
    I["hy_filt_fb"] = din("hy_filt_fb", [1, 4, 128])
    I["hy_bias"] = din("hy_bias", [1, 2, D])
    I["hy_w_out"] = din("hy_w_out", [1, D, D])
    I["hy_F128"] = din("hy_F128", [4, 128, 128], BF16)
    I["hy_ndel"] = din("hy_ndel", [128, KC])
    I["hy_DF"] = din("hy_DF", [2, CTX, 2 * CTX], BF16)
    I["hy_DG"] = din("hy_DG", [2, 2 * CTX, CTX], BF16)
    P.HV = dscr("HV", [KC, 128, NT], BF16)
    P.HX1 = dscr("HX1", [KC, 128, NT])
    P.HX2 = dscr("HX2", [KC, 128, NT])
    P.HY = dscr("HY", [KC, 128, NT])
    for tag, Lf in (("c", CTX), ("l", L)):
        NA_ = 2 * Lf // 128
        NZ_ = Lf // 128
        dd = {}
        dd["FA"] = din("hy_FA" + tag, [2, NZ_, NA_], BF16)
        dd["TW"] = din("hy_TW" + tag, [2, 128, NA_])
        dd["IT"] = din("hy_IT" + tag, [2, NA_, 128])
        dd["CS"] = din("hy_CS" + tag, [3, NA_, NZ_], BF16)
        dd["zT"] = din("hy_zT" + tag, [33, Lf])
        dd["trow"] = din("hy_trow" + tag, [1, Lf])
        dd["KS"] = dscr("hy_KS" + tag, [2, 2, KC, 128, Lf], BF16)
        dd["RN"] = dscr("hy_RN" + tag, [2, D])
        dd["KF"] = dscr("hy_KF" + tag, [2, 2, 128, D, NA_], BF16)
        setattr(P, "hyc_" + tag, dd)
    P.U32 = dscr("U32", [NT, D])
    P.U16 = dscr("U16", [NT, D], BF16)
    out = nc.dram_tensor("out", [L, D], F32, kind="ExternalOutput").ap()
    X = dscr("X", [NT, D])
    Mrows = dscr("Mrows", [2, NMOD * D])
    P.I = I
    P.X = X
    P.Mrows = Mrows
    P.out = out

    with contextlib.ExitStack() as st0:
        S = Sched(nc, st0)
        P.S = S

        uid = [0]

        def sb(st, name, shape, dt=F32):
            uid[0] += 1
            nm = "s%d_%s" % (uid[0], name)
            return Buf(st.enter_context(nc.sbuf_tensor(nm, list(shape), dt)), nm)

        def ps(st, name, dt=F32, cols=512):
            uid[0] += 1
            nm = "p%d_%s" % (uid[0], name)
            return Buf(st.enter_context(nc.psum_tensor(nm, [128, cols], dt)), nm)

        P.sb = sb
        P.ps = ps
        ident = sb(st0, "ident", [128, 128])
        scT = sb(st0, "scT", [128, KC, 2])
        modc = sb(st0, "modc", [128, NMOD * KC, 2])
        gcol = sb(st0, "gcol", [128, 3 * KC])
        neghalf = sb(st0, "neghalf", [128, 16])
        P.ident, P.scT, P.modc, P.gcol, P.neghalf = ident, scT, modc, gcol, neghalf
        P.neghalf16 = neghalf
        P.Xb = [Buf(None, "X%d" % g) for g in range(NG)]
        P.Mb = Buf(None, "Mrows")

        S.dma("sp", ident[:], I["ident"][:, :], writes=[ident])
        S.op("dve", lambda e: e.memset(neghalf[:], -0.5), writes=[neghalf])

        with contextlib.ExitStack() as st:
            crow = sb(st, "crow", [8, 2, 128])
            pT = ps(st, "pT")
            S.dma("sp", crow[:, 0, :], I["c"][:, :], writes=[crow])
            S.dma("sp", crow[:, 1, :], I["c_ctx"][:, :], writes=[crow])
            for j in range(2):
                S.op("pe", lambda e, j=j: e.transpose(pT[:, j * 8:(j + 1) * 8], crow[:, j, :], ident[0:8, 0:8]),
                     reads=[crow, ident], writes=[pT])
            for j in range(2):
                S.op("act", lambda e, j=j: e.activation(out=scT[:, :, j], in_=pT[:, j * 8:(j + 1) * 8], func=AF.Silu),
                     reads=[pT], writes=[scT])
            S.barrier()

        for i in range(layers):
            kind = i % 3
            jj = i // 3
            last = i == DEPTH - 1
            adaln_phase(P, i)
            ffn_phase(P, i, 0, first=(i == 0), do_ctx=True)
            if kind == 0 and 0 in mixers:
                ssd_phase(P, i, jj, last)
            if kind == 1 and 1 in mixers:
                pool_phase(P, i, jj)
            if kind == 2 and 2 in mixers:
                hyena_phase(P, i, jj)
            ffn_phase(P, i, 1, first=False, do_ctx=not last)
        final_phase(P)
        S.barrier()
        with nc.Block() as block:
            S.emit(block)
    return nc


def adaln_phase(P, i):
    nc, S, I = P.nc, P.S, P.I
    ident, scT, modc, gcol = P.ident, P.scT, P.modc, P.gcol
    with contextlib.ExitStack() as st:
        brow = P.sb(st, "brow", [72, 128])
        grow = P.sb(st, "grow", [24, 128])
        bcol = P.sb(st, "bcol", [128, 72])
        wbuf = [P.sb(st, "adaw%d" % k, [128, KC, 1024]) for k in range(2)]
        mrow = P.sb(st, "mrow", [72, 2, 128])
        pT = P.ps(st, "pTa")
        pM = P.ps(st, "pMa")
        pR = P.ps(st, "pRa")
        S.dma("sp", brow[:], I["ada_b"][i, :, :], writes=[brow])
        S.dma("sp", grow[:], I["norm_g"][i, :, :], writes=[grow])
        S.op("pe", lambda e: e.transpose(pT[:, 0:72], brow[:], ident[0:72, 0:72]), reads=[brow, ident], writes=[pT])
        S.op("pe", lambda e: e.transpose(pT[:, 128:152], grow[:], ident[0:24, 0:24]), reads=[grow, ident], writes=[pT])
        S.op("dve", lambda e: e.tensor_copy(out=bcol[:], in_=pT[:, 0:72]), reads=[pT], writes=[bcol])
        S.op("dve", lambda e: e.tensor_copy(out=gcol[:], in_=pT[:, 128:152]), reads=[pT], writes=[gcol])
        for m in range(NMOD):
            wb = wbuf[m % 2]
            S.dma("sp", wb[:], I["ada_w"][i, :, m * 1024:(m + 1) * 1024].rearrange("(kc p) n -> p kc n", p=128),
                  writes=[wb])
            for cc in range(KC):
                col = m * KC + cc
                for kc in range(KC):
                    S.op("pe", lambda e, wb=wb, cc=cc, kc=kc, col=col: e.matmul(
                        pM[:, col * 2:col * 2 + 2], lhsT=wb[:, kc, cc * 128:(cc + 1) * 128], rhs=scT[:, kc, :],
                        start=(kc == 0), stop=(kc == KC - 1)), reads=[wb, scT], writes=[pM])
        for j in range(2):
            S.op("dve", lambda e, j=j: e.tensor_tensor(
                out=modc[:, :, j], in0=pM[:, :].rearrange("p (c j) -> p c j", j=2)[:, 0:72, j], in1=bcol[:], op=ALU.add),
                reads=[pM, bcol], writes=[modc])
        for j in range(2):
            S.op("pe", lambda e, j=j: e.transpose(pR[0:72, j * 128:(j + 1) * 128], modc[:, :, j], ident[:, :]),
                 reads=[modc, ident], writes=[pR])
        S.op("dve", lambda e: e.tensor_copy(out=mrow[:, :, :], in_=pR[0:72, 0:256].rearrange("p (j c) -> p j c", j=2)),
             reads=[pR], writes=[mrow])
        for j in range(2):
            S.dma("sp", P.Mrows[j, :].rearrange("(c p) -> c p", p=128), mrow[:, j, :], reads=[mrow], writes=[P.Mb])
        S.barrier()


def rstd_ops(P, xt, nsub, stats, mv, t1, t2, rstd):
    S = P.S
    for s in range(nsub):
        for hh in range(2):
            S.op("dve", lambda e, s=s, hh=hh: e.bn_stats(out=stats[:, s, hh, :], in_=xt[:, s, hh * 512:(hh + 1) * 512]),
                 reads=[xt], writes=[stats])
        S.op("dve", lambda e, s=s: e.bn_aggr(out=mv[:, s, :], in_=stats[:, s, :, :].rearrange("p a b -> p (a b)")),
             reads=[stats], writes=[mv])
    S.op("dve", lambda e: e.tensor_tensor(out=t1[:, 0:nsub], in0=mv[:, 0:nsub, 0], in1=mv[:, 0:nsub, 0], op=ALU.mult),
         reads=[mv], writes=[t1])
    S.op("dve", lambda e: e.scalar_tensor_tensor(out=t2[:, 0:nsub], in0=t1[:, 0:nsub], scalar=EPS, in1=mv[:, 0:nsub, 1],
                                                 op0=ALU.add, op1=ALU.add), reads=[t1, mv], writes=[t2])
    S.op("pool", lambda e: e.tensor_tensor(out=rstd[:, 0:nsub], in0=t2[:, 0:nsub], in1=P.neghalf[:, 0:nsub], op=ALU.pow),
         reads=[t2, P.neghalf], writes=[rstd])


def group_src(P, g, first):
    if first:
        if g == 0:
            return P.I["ctx"][:, :].rearrange("(s p) d -> p s d", p=128)
        return P.I["x"][(g - 1) * 256:g * 256, :].rearrange("(s p) d -> p s d", p=128)
    return P.X[g * 256:(g + 1) * 256, :].rearrange("(s p) d -> p s d", p=128)


def ffn_phase(P, i, which, first, do_ctx):
    nc, S, I = P.nc, P.S, P.I
    ident, modc, gcol = P.ident, P.modc, P.gcol
    nidx = 0 if which == 0 else 2
    m_shift, m_scale, m_gate = (0, 1, 2) if which == 0 else (6, 7, 8)
    with contextlib.ExitStack() as st:
        sb, ps = P.sb, P.ps
        wg = sb(st, "wg", [128, KC, DFF], BF16)
        wu = sb(st, "wu", [128, KC, DFF], BF16)
        wd = sb(st, "wd", [128, FC, D], BF16)
        stg = [sb(st, "stg%d" % k, [128, 1408]) for k in range(2)]
        xb = [sb(st, "xt%d" % k, [128, 2, D]) for k in range(2)]
        xn = sb(st, "xn", [128, 2, D])
        uT = [sb(st, "uT%d" % k, [128, KC, 256], BF16) for k in range(2)]
        hT = sb(st, "hT", [128, FC, 256], BF16)
        sg = [sb(st, "sg%d" % k, [128, 256]) for k in range(2)]
        ep = [sb(st, "ep%d" % k, [128, 512]) for k in range(2)]
        gbc = sb(st, "gbc", [128, D])
        acol = sb(st, "acol", [128, KC, 2])
        stats = sb(st, "stats", [128, 2, 2, 6])
        mv = sb(st, "mv", [128, 2, 2])
        t1 = sb(st, "t1", [128, 2])
        t2 = sb(st, "t2", [128, 2])
        rstd = sb(st, "rstd", [128, 2])
        psT = [ps(st, "psT%d" % k) for k in range(2)]
        psGU = [ps(st, "psGU%d" % k) for k in range(2)]
        psD = [ps(st, "psD%d" % k) for k in range(4)]

        for j in range(2):
            S.op("dve", lambda e, j=j: e.scalar_tensor_tensor(
                out=acol[:, :, j], in0=modc[:, m_scale * KC:(m_scale + 1) * KC, j], scalar=1.0,
                in1=gcol[:, nidx * KC:(nidx + 1) * KC], op0=ALU.add, op1=ALU.mult), reads=[modc, gcol], writes=[acol])

        conv_engs = ["pool", "dve", "act"]
        k = 0
        for (wsrc, wdst) in ((I["ffn_w_gate"], wg), (I["ffn_w_up"], wu)):
            for kc in range(KC):
                for hh in range(2):
                    sg_ = stg[k % 2]
                    S.dma("sp", sg_[:, :], wsrc[i, which, kc * 128:(kc + 1) * 128, hh * 1408:(hh + 1) * 1408], writes=[sg_])
                    eng = conv_engs[k % 3]
                    dst = wdst
                    if eng == "act":
                        S.op(eng, lambda e, sg_=sg_, dst=dst, kc=kc, hh=hh: e.copy(
                            out=dst[:, kc, hh * 1408:(hh + 1) * 1408], in_=sg_[:, :]), reads=[sg_], writes=[dst])
                    else:
                        S.op(eng, lambda e, sg_=sg_, dst=dst, kc=kc, hh=hh: e.tensor_copy(
                            out=dst[:, kc, hh * 1408:(hh + 1) * 1408], in_=sg_[:, :]), reads=[sg_], writes=[dst])
                    k += 1
        for fc in range(FC):
            sg_ = stg[k % 2]
            S.dma("sp", sg_[:, 0:D], I["ffn_w_down"][i, which, fc * 128:(fc + 1) * 128, :], writes=[sg_])
            eng = conv_engs[k % 3]
            if eng == "act":
                S.op(eng, lambda e, sg_=sg_, fc=fc: e.copy(out=wd[:, fc, :], in_=sg_[:, 0:D]), reads=[sg_], writes=[wd])
            else:
                S.op(eng, lambda e, sg_=sg_, fc=fc: e.tensor_copy(out=wd[:, fc, :], in_=sg_[:, 0:D]), reads=[sg_], writes=[wd])
            k += 1

        groups = list(range(P.NG)) if do_ctx else list(range(1, P.NG))

        def prep(gi, g):
            j = 1 if g == 0 else 0
            xt = xb[gi % 2]
            S.dma("sp", xt[:], group_src(P, g, first), reads=[P.Xb[g]], writes=[xt])
            rstd_ops(P, xt, 2, stats, mv, t1, t2, rstd)
            for s in range(2):
                S.op("pool", lambda e, s=s, xt=xt: e.tensor_scalar(
                    out=xn[:, s, :], in0=xt[:, s, :], scalar1=rstd[:, s:s + 1], scalar2=0.0, op0=ALU.mult, op1=ALU.add),
                    reads=[xt, rstd], writes=[xn])
            u = uT[gi % 2]
            for kc in range(KC):
                pb = psT[(kc // 2) % 2]
                for s in range(2):
                    c0 = (kc % 2) * 256 + s * 128
                    S.op("pe", lambda e, pb=pb, c0=c0, s=s, kc=kc: e.transpose(
                        pb[:, c0:c0 + 128], xn[:, s, kc * 128:(kc + 1) * 128], ident[:, :]), reads=[xn, ident], writes=[pb])
                c1 = (kc % 2) * 256
                S.op("act", lambda e, pb=pb, c1=c1, kc=kc, u=u, j=j: e.activation(
                    out=u[:, kc, :], in_=pb[:, c1:c1 + 256], func=AF.Identity,
                    scale=acol[:, kc, j:j + 1], bias=modc[:, m_shift * KC + kc, j:j + 1]),
                    reads=[pb, acol, modc], writes=[u])

        cur_j = None
        prep(0, groups[0])
        for gi, g in enumerate(groups):
            j = 1 if g == 0 else 0
            if j != cur_j:
                S.dma("sp", gbc[:, :], P.Mrows[j, m_gate * D:(m_gate + 1) * D].partition_broadcast(128),
                      reads=[P.Mb], writes=[gbc])
                cur_j = j
            xt = xb[gi % 2]
            u = uT[gi % 2]
            for fc in range(FC):
                pb = psGU[fc % 2]
                for (w_, c0) in ((wg, 0), (wu, 256)):
                    for kc in range(KC):
                        S.op("pe", lambda e, pb=pb, w_=w_, c0=c0, kc=kc, fc=fc, u=u: e.matmul(
                            pb[:, c0:c0 + 256], lhsT=w_[:, kc, fc * 128:(fc + 1) * 128], rhs=u[:, kc, :],
                            start=(kc == 0), stop=(kc == KC - 1)), reads=[w_, u], writes=[pb])
                sgb = sg[fc % 2]
                S.op("act", lambda e, pb=pb, sgb=sgb: e.activation(out=sgb[:, :], in_=pb[:, 0:256], func=AF.Silu),
                     reads=[pb], writes=[sgb])
                S.op("dve", lambda e, pb=pb, sgb=sgb, fc=fc: e.tensor_tensor(
                    out=hT[:, fc, :], in0=sgb[:, :], in1=pb[:, 256:512], op=ALU.mult), reads=[pb, sgb], writes=[hT])
            if gi + 1 < len(groups):
                prep(gi + 1, groups[gi + 1])
            for s in range(2):
                for hh in range(2):
                    pb = psD[s * 2 + hh]
                    for fc in range(FC):
                        S.op("pe", lambda e, pb=pb, s=s, hh=hh, fc=fc: e.matmul(
                            pb[:, :], lhsT=hT[:, fc, s * 128:(s + 1) * 128], rhs=wd[:, fc, hh * 512:(hh + 1) * 512],
                            start=(fc == 0), stop=(fc == FC - 1)), reads=[hT, wd], writes=[pb])
                    epb = ep[hh]
                    S.op("dve", lambda e, pb=pb, epb=epb, hh=hh: e.scalar_tensor_tensor(
                        out=epb[:, :], in0=pb[:, :], scalar=0.5, in1=gbc[:, hh * 512:(hh + 1) * 512],
                        op0=ALU.mult, op1=ALU.mult), reads=[pb, gbc], writes=[epb])
                    S.op("pool", lambda e, epb=epb, xt=xt, s=s, hh=hh: e.tensor_tensor(
                        out=xt[:, s, hh * 512:(hh + 1) * 512], in0=epb[:, :], in1=xt[:, s, hh * 512:(hh + 1) * 512],
                        op=ALU.add), reads=[epb, xt], writes=[xt])
            S.dma("pool", P.X[g * 256:(g + 1) * 256, :].rearrange("(s p) d -> p s d", p=128), xt[:],
                  reads=[xt], writes=[P.Xb[g]])
        S.barrier()


def final_phase(P):
    nc, S, I = P.nc, P.S, P.I
    with contextlib.ExitStack() as st:
        sb = P.sb
        xb = [sb(st, "fx%d" % k, [128, 2, D]) for k in range(2)]
        yb = [sb(st, "fy%d" % k, [128, 2, D]) for k in range(2)]
        gbc = sb(st, "fg", [128, D])
        stats = sb(st, "fstats", [128, 2, 2, 6])
        mv = sb(st, "fmv", [128, 2, 2])
        t1 = sb(st, "ft1", [128, 2])
        t2 = sb(st, "ft2", [128, 2])
        rstd = sb(st, "frstd", [128, 2])
        S.dma("sp", gbc[:, :], I["final_g"][0, :].partition_broadcast(128), writes=[gbc])
        glist = list(range(1, P.NG))
        if getattr(P, "dbg_ctx", False):
            glist = [0] + glist[1:]
        for g in glist:
            xt = xb[g % 2]
            yt = yb[g % 2]
            S.dma("sp", xt[:], group_src(P, g, False), reads=[P.Xb[g]], writes=[xt])
            rstd_ops(P, xt, 2, stats, mv, t1, t2, rstd)
            for s in range(2):
                S.op("dve", lambda e, s=s, xt=xt, yt=yt: e.scalar_tensor_tensor(
                    out=yt[:, s, :], in0=xt[:, s, :], scalar=rstd[:, s:s + 1], in1=gbc[:, :], op0=ALU.mult, op1=ALU.mult),
                    reads=[xt, rstd, gbc], writes=[yt])
            go = max(g - 1, 0)
            S.dma("pool", P.out[go * 256:(go + 1) * 256, :].rearrange("(s p) d -> p s d", p=128), yt[:], reads=[yt])
        S.barrier()


_CACHE = {}


def host_inputs(inputs, b, L, shared):
    f = np.float32
    m = dict(shared)
    m["x"] = np.ascontiguousarray(inputs["x"][b, :L], dtype=f)
    m["ctx"] = np.ascontiguousarray(inputs["ctx"][b], dtype=f)
    m["c"] = np.ascontiguousarray(inputs["c"][b].reshape(KC, 128), dtype=f)
    return m


def shared_inputs(inputs, L):
    f = np.float32
    m = {}
    m["c_ctx"] = np.ascontiguousarray(inputs["c_ctx"].reshape(KC, 128), dtype=f)
    m["ada_w"] = np.ascontiguousarray(inputs["ada_w"], dtype=f)
    m["ada_b"] = np.ascontiguousarray(inputs["ada_b"].reshape(DEPTH, NMOD * KC, 128), dtype=f)
    m["norm_g"] = np.ascontiguousarray(inputs["norm_g"].reshape(DEPTH, 3 * KC, 128), dtype=f)
    m["norm_g_rows"] = np.ascontiguousarray(inputs["norm_g"], dtype=f)
    m["ffn_w_gate"] = np.ascontiguousarray(inputs["ffn_w_gate"], dtype=f)
    m["ffn_w_up"] = np.ascontiguousarray(inputs["ffn_w_up"], dtype=f)
    m["ffn_w_down"] = np.ascontiguousarray(inputs["ffn_w_down"], dtype=f)
    m["final_g"] = np.ascontiguousarray(inputs["final_g"].reshape(1, D), dtype=f)
    m["ident"] = np.eye(128, dtype=f)
    pm, pinv, pcm, pcinv = pool_consts(L)
    m["poolP"], m["pool_inv"], m["poolPc"], m["pool_invc"] = pm, pinv, pcm, pcinv
    m["pool_w"] = np.ascontiguousarray(inputs["pool_w"], dtype=f)
    m["pool_b"] = np.ascontiguousarray(inputs["pool_b"].reshape(1, D), dtype=f)
    m["pool_scale"] = np.ascontiguousarray(inputs["pool_scale"].reshape(1, D), dtype=f)
    m["ssd_w_in"] = np.ascontiguousarray(inputs["ssd_w_in"], dtype=f)
    m["ssd_conv_w"] = np.ascontiguousarray(inputs["ssd_conv_w"], dtype=f)
    m["ssd_conv_b"] = np.ascontiguousarray(inputs["ssd_conv_b"], dtype=f)
    m["ssd_a_log"] = np.ascontiguousarray(inputs["ssd_a_log"].reshape(2, 64), dtype=f)
    m["ssd_dt_bias"] = np.ascontiguousarray(inputs["ssd_dt_bias"].reshape(2, 64), dtype=f)
    m["ssd_d"] = np.ascontiguousarray(inputs["ssd_d"], dtype=f)
    m["ssd_norm_g"] = np.ascontiguousarray(inputs["ssd_norm_g"], dtype=f)
    m["ssd_w_out"] = np.ascontiguousarray(inputs["ssd_w_out"], dtype=f)
    m["ssd_U"] = ssd_consts()
    m["hy_w_in"] = np.ascontiguousarray(inputs["hy_w_in"], dtype=f)
    m["hy_conv_w"] = np.ascontiguousarray(inputs["hy_conv_w"], dtype=f)
    m["hy_conv_b"] = np.ascontiguousarray(inputs["hy_conv_b"], dtype=f)
    m["hy_filt_w1"] = np.ascontiguousarray(inputs["hy_filt_w1"], dtype=f)
    m["hy_filt_w2"] = np.ascontiguousarray(inputs["hy_filt_w2"], dtype=f)
    m["hy_filt_w3"] = np.ascontiguousarray(inputs["hy_filt_w3"], dtype=f)
    fbm = np.zeros((1, 4, 128), f)
    fbm[0, 0, :64] = inputs["hy_filt_freq"][0, 0]
    fbm[0, 1, :64] = inputs["hy_filt_freq"][0, 1]
    fbm[0, 2, :64] = inputs["hy_filt_b1"][0]
    fbm[0, 3, :64] = inputs["hy_filt_b2"][0]
    m["hy_filt_fb"] = fbm
    m["hy_bias"] = np.ascontiguousarray(inputs["hy_bias"], dtype=f)
    m["hy_w_out"] = np.ascontiguousarray(inputs["hy_w_out"], dtype=f)
    F128, ndel = hy_shared_consts()
    m["hy_F128"], m["hy_ndel"] = F128, ndel
    m["hy_DF"], m["hy_DG"] = hy_dense_consts()
    for tag, Lf in (("c", CTX), ("l", L)):
        hc = hy_consts(Lf)
        for k_ in ("FA", "TW", "IT", "CS", "zT", "trow"):
            m["hy_" + k_ + tag] = hc[k_]
    return m


def run(inputs, L, layers=DEPTH, mixers=(0, 1, 2), trace=False, ncores=8, dbg_ctx=False):
    mixers = tuple(mixers)
    key = (L, layers, mixers, dbg_ctx)
    if key not in _CACHE:
        _CACHE[key] = build_program(L, layers, mixers, dbg_ctx)
    nc = _CACHE[key]
    B = inputs["x"].shape[0]
    shared = shared_inputs(inputs, L)
    in_maps = [host_inputs(inputs, c % B, L, shared) for c in range(ncores)]
    res = run_bass_kernel_spmd(nc, in_maps, core_ids=list(range(ncores)))
    out = np.stack([np.asarray(res.results[b]["out"]) for b in range(min(B, ncores))], axis=0)
    return out.astype(np.float32)


def kernel(**inputs):
    return run(inputs, inputs["x"].shape[1])


POOL_W = (2, 4, 8, 16)
POOL_RANGES = ((-1, 0), (-1, 1), (-2, 2), (-4, 4))
POOL_BASE = (0, 2, 5, 10)


def pool_consts(L):
    import ml_dtypes
    rows = L // GRID_W
    NK = L // 128
    p = np.arange(128)
    rl, cc = p // 64, p % 64
    mats = np.zeros((19, 128, 128), np.float32)
    for wi, w in enumerate(POOL_W):
        lo, hi = POOL_RANGES[wi]
        for dl in range(lo, hi + 1):
            dr = 2 * dl + rl[:, None] - rl[None, :]
            dc = cc[:, None] - cc[None, :]
            ok = (dr >= -(w // 2)) & (dr <= w // 2 - 1) & (dc >= -(w // 2)) & (dc <= w // 2 - 1)
            mats[POOL_BASE[wi] + dl - lo] = ok
    inv = np.zeros((128, NK, 4), np.float32)
    for wi, w in enumerate(POOL_W):
        for k in range(NK):
            r = 2 * k + rl
            cr = np.minimum(r + w - w // 2, rows) - np.maximum(r - w // 2, 0)
            ccnt = np.minimum(cc + w - w // 2, GRID_W) - np.maximum(cc - w // 2, 0)
            inv[:, k, wi] = 1.0 / (cr * ccnt).astype(np.float32)
    cm = np.zeros((16, 128, 128), np.float32)
    cinv = np.zeros((128, 2, 4), np.float32)
    for wi, w in enumerate(POOL_W):
        for kin in range(2):
            for ko in range(2):
                tin = kin * 128 + p[:, None]
                to = ko * 128 + p[None, :]
                dt_ = tin - to
                cm[wi * 4 + kin * 2 + ko] = (dt_ >= -(w // 2)) & (dt_ <= w // 2 - 1)
        for ko in range(2):
            t = ko * 128 + p
            cnt = np.minimum(t + w - w // 2, CTX) - np.maximum(t - w // 2, 0)
            cinv[:, ko, wi] = 1.0 / cnt.astype(np.float32)
    bf = ml_dtypes.bfloat16
    return mats.astype(bf), inv, cm.astype(bf), cinv


def mixer_u_pass(P, i, st, U32, U16, Ub):
    S, I = P.S, P.I
    sb = P.sb
    abc = sb(st, "abc", [128, D])
    bbc = sb(st, "bbc", [128, D])
    gb = sb(st, "gb", [128, D])
    xb = [sb(st, "ux%d" % k, [128, 2, D]) for k in range(2)]
    ub = [sb(st, "uu%d" % k, [128, 2, D]) for k in range(2)]
    u16 = [sb(st, "uh%d" % k, [128, 2, D], BF16) for k in range(2)]
    stats = sb(st, "ustats", [128, 2, 2, 6])
    mv = sb(st, "umv", [128, 2, 2])
    t1 = sb(st, "ut1", [128, 2])
    t2 = sb(st, "ut2", [128, 2])
    rstd = sb(st, "urstd", [128, 2])
    S.dma("sp", gb[:, :], I["norm_g_rows"][i, 1, :].partition_broadcast(128), writes=[gb])
    cur_j = None
    for g in range(P.NG):
        j = 1 if g == 0 else 0
        if j != cur_j:
            S.dma("sp", abc[:, :], P.Mrows[j, 4 * D:5 * D].partition_broadcast(128), reads=[P.Mb], writes=[abc])
            S.dma("sp", bbc[:, :], P.Mrows[j, 3 * D:4 * D].partition_broadcast(128), reads=[P.Mb], writes=[bbc])
            S.op("pool", lambda e: e.tensor_scalar(
                out=abc[:, :], in0=abc[:, :], scalar1=1.0, scalar2=1.0, op0=ALU.add, op1=ALU.mult),
                reads=[abc], writes=[abc])
            S.op("pool", lambda e: e.tensor_tensor(out=abc[:, :], in0=abc[:, :], in1=gb[:, :], op=ALU.mult),
                 reads=[abc, gb], writes=[abc])
            cur_j = j
        xt, ut, uh = xb[g % 2], ub[g % 2], u16[g % 2]
        S.dma("sp", xt[:], group_src(P, g, False), reads=[P.Xb[g]], writes=[xt])
        rstd_ops(P, xt, 2, stats, mv, t1, t2, rstd)
        for s in range(2):
            S.op("dve", lambda e, s=s, xt=xt, ut=ut: e.scalar_tensor_tensor(
                out=ut[:, s, :], in0=xt[:, s, :], scalar=rstd[:, s:s + 1], in1=abc[:, :], op0=ALU.mult, op1=ALU.mult),
                reads=[xt, rstd, abc], writes=[ut])
            S.op("pool", lambda e, s=s, ut=ut: e.tensor_tensor(out=ut[:, s, :], in0=ut[:, s, :], in1=bbc[:, :], op=ALU.add),
                 reads=[ut, bbc], writes=[ut])
            if U16 is not None:
                S.op("act", lambda e, s=s, ut=ut, uh=uh: e.copy(out=uh[:, s, :], in_=ut[:, s, :]), reads=[ut], writes=[uh])
        if U32 is not None:
            S.dma("pool", U32[g * 256:(g + 1) * 256, :].rearrange("(s p) d -> p s d", p=128), ut[:], reads=[ut], writes=[Ub[g]])
        if U16 is not None:
            S.dma("pool", U16[g * 256:(g + 1) * 256, :].rearrange("(s p) d -> p s d", p=128), uh[:], reads=[uh], writes=[Ub[g]])


def pool_phase(P, i, jj):
    nc, S, I = P.nc, P.S, P.I
    ident = P.ident
    NK = P.L // 128
    Ub = [Buf(None, "U%d" % g) for g in range(P.NG)]
    with contextlib.ExitStack() as st:
        mixer_u_pass(P, i, st, P.U32, P.U16, Ub)
        S.barrier()
    with contextlib.ExitStack() as st:
        sb, ps = P.sb, P.ps
        NR = 10
        ring = [sb(st, "ring%d" % k, [128, D], BF16) for k in range(NR)]
        u32 = [sb(st, "pu%d" % k, [128, D]) for k in range(2)]
        xb = [sb(st, "px%d" % k, [128, D]) for k in range(2)]
        pooled = sb(st, "pooled", [128, D])
        pT16 = sb(st, "pT16", [128, KC, 128], BF16)
        ep = [sb(st, "pep%d" % k, [128, 512]) for k in range(2)]
        Pm = sb(st, "Pm", [128, 19, 128], BF16)
        Pc = sb(st, "Pc", [128, 16, 128], BF16)
        invl = sb(st, "invl", [128, NK, 4])
        invc = sb(st, "invc", [128, 2, 4])
        pw32 = sb(st, "pw32", [128, 4, 2, 256])
        pw = sb(st, "pw", [128, 4, 2, 256], BF16)
        sbc = sb(st, "sbc", [128, D])
        bsbc = sb(st, "bsbc", [128, D])
        psc = sb(st, "psc", [128, D])
        pP = [ps(st, "pP%d" % k) for k in range(2)]
        psT = [ps(st, "ppT%d" % k) for k in range(2)]
        psY = [ps(st, "pY%d" % k) for k in range(2)]
        S.dma("sp", Pm[:], I["poolP"][:, :, :].rearrange("m p q -> p m q"), writes=[Pm])
        S.dma("sp", Pc[:], I["poolPc"][:, :, :].rearrange("m p q -> p m q"), writes=[Pc])
        S.dma("sp", invl[:], I["pool_inv"][:, :, :], writes=[invl])
        S.dma("sp", invc[:], I["pool_invc"][:, :, :], writes=[invc])
        S.dma("sp", pw32[:], I["pool_w"][jj, :, :, :].rearrange("g (cc p) d -> p g cc d", p=128), writes=[pw32])
        S.op("dve", lambda e: e.tensor_copy(out=pw[:], in_=pw32[:]), reads=[pw32], writes=[pw])
        S.dma("sp", psc[:, :], I["pool_scale"][jj, :].partition_broadcast(128), writes=[psc])

        def run_tiles(ntiles, row0, j, mats_of, inv_t, Mt):
            S.dma("sp", sbc[:, :], P.Mrows[j, 5 * D:6 * D].partition_broadcast(128), reads=[P.Mb], writes=[sbc])
            S.dma("sp", bsbc[:, :], I["pool_b"][jj, :].partition_broadcast(128), writes=[bsbc])
            S.op("pool", lambda e: e.tensor_tensor(out=sbc[:, :], in0=sbc[:, :], in1=psc[:, :], op=ALU.mult),
                 reads=[sbc, psc], writes=[sbc])
            S.op("pool", lambda e: e.tensor_tensor(out=bsbc[:, :], in0=bsbc[:, :], in1=sbc[:, :], op=ALU.mult),
                 reads=[sbc, bsbc], writes=[bsbc])
            loaded = set()

            def ensure(t):
                if t in loaded or t < 0 or t >= ntiles:
                    return
                slot = ring[t % NR]
                r0 = row0 + t * 128
                S.dma("sp", slot[:, :], P.U16[r0:r0 + 128, :], reads=[Ub[r0 // 256]], writes=[slot])
                loaded.add(t)

            for k in range(ntiles):
                for t in range(k - 4, k + 5):
                    ensure(t)
                r0 = row0 + k * 128
                ut, xt = u32[k % 2], xb[k % 2]
                S.dma("sp", ut[:, :], P.U32[r0:r0 + 128, :], reads=[Ub[r0 // 256]], writes=[ut])
                S.dma("sp", xt[:, :], P.X[r0:r0 + 128, :], reads=[P.Xb[r0 // 256]], writes=[xt])
                for gi in range(4):
                    pb = pP[gi // 2]
                    c0 = (gi % 2) * 256
                    lst = [(mi, t) for (mi, t) in mats_of(k, gi) if 0 <= t < ntiles]
                    for n, (mi, t) in enumerate(lst):
                        slot = ring[t % NR]
                        S.op("pe", lambda e, pb=pb, c0=c0, mi=mi, slot=slot, gi=gi, n=n, ln=len(lst): e.matmul(
                            pb[:, c0:c0 + 256], lhsT=Mt[:, mi, :], rhs=slot[:, gi * 256:(gi + 1) * 256],
                            start=(n == 0), stop=(n == ln - 1)), reads=[Mt, slot], writes=[pb])
                    S.op("dve", lambda e, pb=pb, c0=c0, gi=gi, k=k, ut=ut: e.scalar_tensor_tensor(
                        out=pooled[:, gi * 256:(gi + 1) * 256], in0=pb[:, c0:c0 + 256], scalar=inv_t[:, k, gi:gi + 1],
                        in1=ut[:, gi * 256:(gi + 1) * 256], op0=ALU.mult, op1=ALU.subtract),
                        reads=[pb, inv_t, ut], writes=[pooled])
                for cb in range(KC):
                    pb = psT[cb // 4]
                    S.op("pe", lambda e, pb=pb, cb=cb: e.transpose(
                        pb[:, (cb % 4) * 128:(cb % 4 + 1) * 128], pooled[:, cb * 128:(cb + 1) * 128], ident[:, :]),
                        reads=[pooled, ident], writes=[pb])
                for hh in range(2):
                    S.op("act", lambda e, hh=hh: e.copy(
                        out=pT16[:, hh * 4:(hh + 1) * 4, :], in_=psT[hh][:, :].rearrange("p (a b) -> p a b", a=4)),
                        reads=[psT[hh]], writes=[pT16])
                for gi in range(4):
                    pb = psY[gi // 2]
                    c0 = (gi % 2) * 256
                    for cc in range(2):
                        S.op("pe", lambda e, pb=pb, c0=c0, gi=gi, cc=cc: e.matmul(
                            pb[:, c0:c0 + 256], lhsT=pT16[:, gi * 2 + cc, :], rhs=pw[:, gi, cc, :],
                            start=(cc == 0), stop=(cc == 1)), reads=[pT16, pw], writes=[pb])
                for hh in range(2):
                    epb = ep[hh]
                    S.op("dve", lambda e, hh=hh, epb=epb: e.tensor_tensor(
                        out=epb[:, :], in0=psY[hh][:, :], in1=sbc[:, hh * 512:(hh + 1) * 512], op=ALU.mult),
                        reads=[psY[hh], sbc], writes=[epb])
                    S.op("pool", lambda e, hh=hh, epb=epb, xt=xt: e.tensor_tensor(
                        out=xt[:, hh * 512:(hh + 1) * 512], in0=epb[:, :], in1=xt[:, hh * 512:(hh + 1) * 512], op=ALU.add),
                        reads=[epb, xt], writes=[xt])
                S.op("pool", lambda e, xt=xt: e.tensor_tensor(out=xt[:, :], in0=xt[:, :], in1=bsbc[:, :], op=ALU.add),
                     reads=[xt, bsbc], writes=[xt])
                S.dma("pool", P.X[r0:r0 + 128, :], xt[:, :], reads=[xt], writes=[P.Xb[r0 // 256]])

        def mats_ctx(k, gi):
            return [(gi * 4 + kin * 2 + k, kin) for kin in range(2)]

        def mats_lat(k, gi):
            lo, hi = POOL_RANGES[gi]
            return [(POOL_BASE[gi] + dl - lo, k + dl) for dl in range(lo, hi + 1)]

        run_tiles(2, 0, 1, mats_ctx, invc, Pc)
        S.barrier()
        run_tiles(NK, CTX, 0, mats_lat, invl, Pm)
        S.barrier()


SSD_H = 32
SSD_IN = 5184


def ssd_consts():
    j = np.arange(128)
    f = np.float32
    c = np.zeros((6, 128, 128), f)
    c[0] = (j[:, None] <= j[None, :])
    c[1] = (j[:, None] >= j[None, :])
    c[2] = (j[:, None] > j[None, :])
    c[3] = (j[:, None] < j[None, :])
    c[4] = 1.0
    return c


def mixer_uT_pass(P, i, st, UT, UTb):
    S, I = P.S, P.I
    sb, ps = P.sb, P.ps
    ident, modc, gcol = P.ident, P.modc, P.gcol
    xb = [sb(st, "tx%d" % k, [128, 2, D]) for k in range(2)]
    xn = sb(st, "txn", [128, 2, D])
    uT = [sb(st, "tuT%d" % k, [128, KC, 256], BF16) for k in range(2)]
    acol = sb(st, "tacol", [128, KC, 2])
    stats = sb(st, "tstats", [128, 2, 2, 6])
    mv = sb(st, "tmv", [128, 2, 2])
    t1 = sb(st, "tt1", [128, 2])
    t2 = sb(st, "tt2", [128, 2])
    rstd = sb(st, "trstd", [128, 2])
    psT = [ps(st, "tpsT%d" % k) for k in range(2)]
    for j in range(2):
        S.op("dve", lambda e, j=j: e.scalar_tensor_tensor(
            out=acol[:, :, j], in0=modc[:, 4 * KC:5 * KC, j], scalar=1.0, in1=gcol[:, KC:2 * KC],
            op0=ALU.add, op1=ALU.mult), reads=[modc, gcol], writes=[acol])
    for g in range(P.NG):
        j = 1 if g == 0 else 0
        xt = xb[g % 2]
        S.dma("sp", xt[:], group_src(P, g, False), reads=[P.Xb[g]], writes=[xt])
        rstd_ops(P, xt, 2, stats, mv, t1, t2, rstd)
        for s in range(2):
            S.op("pool", lambda e, s=s, xt=xt: e.tensor_scalar(
                out=xn[:, s, :], in0=xt[:, s, :], scalar1=rstd[:, s:s + 1], scalar2=0.0, op0=ALU.mult, op1=ALU.add),
                reads=[xt, rstd], writes=[xn])
        u = uT[g % 2]
        for kc in range(KC):
            pb = psT[(kc // 2) % 2]
            for s in range(2):
                c0 = (kc % 2) * 256 + s * 128
                S.op("pe", lambda e, pb=pb, c0=c0, s=s, kc=kc: e.transpose(
                    pb[:, c0:c0 + 128], xn[:, s, kc * 128:(kc + 1) * 128], ident[:, :]), reads=[xn, ident], writes=[pb])
            c1 = (kc % 2) * 256
            S.op("act", lambda e, pb=pb, c1=c1, kc=kc, u=u, j=j: e.activation(
                out=u[:, kc, :], in_=pb[:, c1:c1 + 256], func=AF.Identity,
                scale=acol[:, kc, j:j + 1], bias=modc[:, 3 * KC + kc, j:j + 1]),
                reads=[pb, acol, modc], writes=[u])
        S.dma("pool", UT[:, :, g * 256:(g + 1) * 256].rearrange("kc p t -> p kc t"), u[:, :, :], reads=[u], writes=[UTb[g]])


def load_cols(P, st, pbank, dram_rows, nrows, name):
    S = P.S
    rows = P.sb(st, name + "_r", [nrows, 128])
    cols = P.sb(st, name + "_c", [128, nrows])
    S.dma("sp", rows[:, :], dram_rows, writes=[rows])
    S.op("pe", lambda e: e.transpose(pbank[:, 0:nrows], rows[:, :], P.ident[0:nrows, 0:nrows]),
         reads=[rows, P.ident], writes=[pbank])
    S.op("dve", lambda e: e.tensor_copy(out=cols[:, :], in_=pbank[:, 0:nrows]), reads=[pbank], writes=[cols])
    return cols


def ssd_phase(P, i, jj, last):
    nc, S, I = P.nc, P.S, P.I
    sb, ps = P.sb, P.ps
    ident = P.ident
    NB = P.NG
    NCH = 2 * NB
    UTb = [Buf(None, "UT%d" % g) for g in range(NB)]
    SZb = [Buf(None, "SZ%d" % c) for c in range(NCH)]
    YPb = [Buf(None, "YP%d" % c) for c in range(NCH)]
    SBb = [Buf(None, "SB%d" % c) for c in range(NCH)]
    CTb = [Buf(None, "CT%d" % c) for c in range(NCH)]
    EBb = [Buf(None, "EB%d" % c) for c in range(NCH)]
    with contextlib.ExitStack() as st:
        mixer_uT_pass(P, i, st, P.UT, UTb)
        S.barrier()

    with contextlib.ExitStack() as st:
        win = sb(st, "win", [128, KC, SSD_IN], BF16)
        with contextlib.ExitStack() as st2:
            stg = [sb(st2, "wstg%d" % k, [128, 1296]) for k in range(2)]
            k = 0
            engs = ["pool", "dve", "act"]
            for kc in range(KC):
                for q in range(4):
                    sg_ = stg[k % 2]
                    S.dma("sp", sg_[:, :], I["ssd_w_in"][jj, kc * 128:(kc + 1) * 128, q * 1296:(q + 1) * 1296], writes=[sg_])
                    eng = engs[k % 3]
                    if eng == "act":
                        S.op(eng, lambda e, sg_=sg_, kc=kc, q=q: e.copy(out=win[:, kc, q * 1296:(q + 1) * 1296], in_=sg_[:, :]),
                             reads=[sg_], writes=[win])
                    else:
                        S.op(eng, lambda e, sg_=sg_, kc=kc, q=q: e.tensor_copy(out=win[:, kc, q * 1296:(q + 1) * 1296], in_=sg_[:, :]),
                             reads=[sg_], writes=[win])
                    k += 1
            S.barrier()
        segb = [ps(st, "bA%d" % k) for k in range(4)]
        bankA = segb
        bankY = ps(st, "bY")
        bankO = ps(st, "bO")
        bankC = ps(st, "bC")
        bankS = ps(st, "bS")
        cw = load_cols(P, st, bankS, I["ssd_conv_w"][jj, :, :].rearrange("k (c p) -> (k c) p", p=128), 120, "cw")
        cbias = load_cols(P, st, bankC, I["ssd_conv_b"][jj, :].rearrange("(c p) -> c p", p=128), 24, "cb")
        U = sb(st, "Uc", [128, 5, 128])
        S.dma("sp", U[:, :, :], I["ssd_U"][0:5, :, :].rearrange("m p q -> p m q"), writes=[U])
        Abc = sb(st, "Abc", [128, 64])
        dtb = sb(st, "dtb", [128, 64])
        Dbc = sb(st, "Dbc", [128, 32])
        S.dma("sp", Abc[:, :], I["ssd_a_log"][jj, :].partition_broadcast(128), writes=[Abc])
        S.dma("sp", dtb[:, :], I["ssd_dt_bias"][jj, :].partition_broadcast(128), writes=[dtb])
        S.dma("sp", Dbc[:, :], I["ssd_d"][jj, :].partition_broadcast(128), writes=[Dbc])
        S.op("act", lambda e: e.activation(out=Abc[:, :], in_=Abc[:, :], func=AF.Exp), reads=[Abc], writes=[Abc])
        S.op("dve", lambda e: e.tensor_scalar(out=Abc[:, :], in0=Abc[:, :], scalar1=-1.0, scalar2=None, op0=ALU.mult),
             reads=[Abc], writes=[Abc])

        uw = [sb(st, "uw%d" % k, [128, KC, 260], BF16) for k in range(2)]
        acc = [sb(st, "acc%d" % k, [128, 256]) for k in range(4)]
        Bf = sb(st, "Bf", [128, 4, 256], BF16)
        Cf = sb(st, "Cf", [128, 4, 256], BF16)
        xtok = sb(st, "xtok", [128, 2, 2048])
        Btok = sb(st, "Btok", [128, 2, 512], BF16)
        szb = [sb(st, "szb%d" % k, [128, 512]) for k in range(2)]
        dtt = sb(st, "dtt", [128, 2, 64])
        dt1 = sb(st, "dt1", [128, 2, 64])
        dt2 = sb(st, "dt2", [128, 2, 64])
        aa = sb(st, "aa", [128, 2, 64])
        acum = sb(st, "acum", [128, 64])
        eacum = sb(st, "eacum", [128, 64])
        dec = sb(st, "dec", [128, 64])
        eatot = sb(st, "eatot", [128, 64])
        ebst = sb(st, "ebst", [128, 64])
        xdt = [sb(st, "xdt%d" % d, [128, 2048], BF16) for d in range(2)]
        xdd = [sb(st, "xdd%d" % d, [128, 2048], BF16) for d in range(2)]
        lseg = [sb(st, "lseg%d" % k, [128, 512]) for k in range(4)]
        Eb = [sb(st, "Eb%d" % k, [128, 512]) for k in range(3)]
        MT = [sb(st, "MT%d" % k, [128, 4, 128], BF16) for k in range(16)]
        cbm = sb(st, "cbm", [128, 2, 4, 128])
        ysb = [sb(st, "ysb%d" % k, [128, 512]) for k in range(2)]
        ytmp = [sb(st, "ytmp%d" % k, [128, 512]) for k in range(2)]
        hf = sb(st, "hf", [128, 4, 512])
        hf16 = sb(st, "hf16", [128, 4, 512], BF16)
        sbst = [sb(st, "sbst%d" % k, [128, 512]) for k in range(2)]
        S.op("pool", lambda e: e.memset(hf[:], 0.0), writes=[hf])
        S.op("pool", lambda e: e.memset(hf16[:], 0.0), writes=[hf16])

        for b in range(NB):
            t0 = b * 256
            seq_lo, seq_hi = (0, CTX) if b == 0 else (CTX, P.NT)
            lo, hi = max(t0 - 2, seq_lo), min(t0 + 258, seq_hi)
            w_ = uw[b % 2]
            if lo != t0 - 2 or hi != t0 + 258:
                S.op("pool", lambda e, w_=w_: e.memset(w_[:], 0.0), writes=[w_])
            S.dma("sp", w_[:, :, lo - (t0 - 2):hi - (t0 - 2)], P.UT[:, :, lo:hi].rearrange("kc p t -> p kc t"),
                  reads=[UTb[g] for g in range(max(b - 1, 0), min(b + 2, NB))], writes=[w_])
            for part in range(3):
                for c8 in range(8):
                    cch = part * 8 + c8
                    pb = bankA[cch % 4]
                    for kc in range(KC):
                        S.op("pe", lambda e, pb=pb, kc=kc, cch=cch, w_=w_: e.matmul(
                            pb[:, 0:260], lhsT=win[:, kc, 2048 + cch * 128:2048 + (cch + 1) * 128], rhs=w_[:, kc, :],
                            start=(kc == 0), stop=(kc == KC - 1)), reads=[win, w_], writes=[pb])
                    pre = pb
                    a_ = acc[cch % 4]
                    S.op("dve", lambda e, a_=a_, pre=pre, cch=cch: e.tensor_scalar(
                        out=a_[:, :], in0=pre[:, 0:256], scalar1=cw[:, cch:cch + 1], scalar2=cbias[:, cch:cch + 1],
                        op0=ALU.mult, op1=ALU.add), reads=[pre, cw, cbias], writes=[a_])
                    for kk in range(1, 5):
                        S.op("dve", lambda e, a_=a_, pre=pre, cch=cch, kk=kk: e.scalar_tensor_tensor(
                            out=a_[:, :], in0=pre[:, kk:kk + 256], scalar=cw[:, kk * 24 + cch:kk * 24 + cch + 1], in1=a_[:, :],
                            op0=ALU.mult, op1=ALU.add), reads=[pre, cw, a_], writes=[a_])
                    if cch < 16:
                        for s in range(2):
                            S.op("pe", lambda e, a_=a_, s=s: e.transpose(
                                bankS[:, s * 128:(s + 1) * 128], a_[:, s * 128:(s + 1) * 128], ident[:, :]),
                                reads=[a_, ident], writes=[bankS])
                        S.op("act", lambda e, cch=cch: e.activation(
                            out=xtok[:, :, cch * 128:(cch + 1) * 128], in_=bankS[:, 0:256].rearrange("p (s c) -> p s c", s=2),
                            func=AF.Silu), reads=[bankS], writes=[xtok])
                    elif cch < 20:
                        gq = cch - 16
                        S.op("act", lambda e, a_=a_, gq=gq: e.activation(out=Bf[:, gq, :], in_=a_[:, :], func=AF.Silu),
                             reads=[a_], writes=[Bf])
                        for s in range(2):
                            S.op("pe", lambda e, a_=a_, s=s: e.transpose(
                                bankS[:, s * 128:(s + 1) * 128], a_[:, s * 128:(s + 1) * 128], ident[:, :]),
                                reads=[a_, ident], writes=[bankS])
                        S.op("act", lambda e, gq=gq: e.activation(
                            out=Btok[:, :, gq * 128:(gq + 1) * 128], in_=bankS[:, 0:256].rearrange("p (s c) -> p s c", s=2),
                            func=AF.Silu), reads=[bankS], writes=[Btok])
                    else:
                        gq = cch - 20
                        S.op("act", lambda e, a_=a_, gq=gq: e.activation(out=Cf[:, gq, :], in_=a_[:, :], func=AF.Silu),
                             reads=[a_], writes=[Cf])
            for s in range(2):
                for kc in range(KC):
                    S.op("pe", lambda e, s=s, kc=kc, w_=w_: e.matmul(
                        bankS[:, 256 + s * 64:256 + (s + 1) * 64], lhsT=w_[:, kc, 2 + s * 128:2 + (s + 1) * 128],
                        rhs=win[:, kc, 5120:5184], start=(kc == 0), stop=(kc == KC - 1)), reads=[win, w_], writes=[bankS])
            S.op("dve", lambda e: e.tensor_tensor(
                out=dtt[:, :, :], in0=bankS[:, 256:384].rearrange("p (s c) -> p s c", s=2),
                in1=dtb[:, :].unsqueeze(1).to_broadcast([128, 2, 64]), op=ALU.add), reads=[bankS, dtb], writes=[dtt])
            S.op("dve", lambda e: e.scalar_tensor_tensor(out=dt1[:, :, :], in0=dtt[:, :, :], scalar=-1.0, in1=dtt[:, :, :],
                                                         op0=ALU.mult, op1=ALU.max), reads=[dtt], writes=[dt1])
            S.op("act", lambda e: e.activation(out=dt1[:, :, :], in_=dt1[:, :, :], func=AF.Exp, scale=-1.0), reads=[dt1], writes=[dt1])
            S.op("act", lambda e: e.activation(out=dt1[:, :, :], in_=dt1[:, :, :], func=AF.Ln, bias=1.0), reads=[dt1], writes=[dt1])
            S.op("dve", lambda e: e.scalar_tensor_tensor(out=dt2[:, :, :], in0=dtt[:, :, :], scalar=0.0, in1=dt1[:, :, :],
                                                         op0=ALU.max, op1=ALU.add), reads=[dtt, dt1], writes=[dt2])
            S.op("dve", lambda e: e.tensor_tensor(out=aa[:, :, :], in0=dt2[:, :, :],
                                                  in1=Abc[:, :].unsqueeze(1).to_broadcast([128, 2, 64]), op=ALU.mult),
                 reads=[dt2, Abc], writes=[aa])
            for s in range(2):
                c = 2 * b + s
                for q in range(4):
                    pb = bankA[q % 4]
                    for kc in range(KC):
                        S.op("pe", lambda e, pb=pb, s=s, kc=kc, q=q, w_=w_: e.matmul(
                            pb[:, :], lhsT=w_[:, kc, 2 + s * 128:2 + (s + 1) * 128], rhs=win[:, kc, q * 512:(q + 1) * 512],
                            start=(kc == 0), stop=(kc == KC - 1)), reads=[win, w_], writes=[pb])
                    zb = szb[q % 2]
                    S.op("act", lambda e, pb=pb, zb=zb: e.activation(out=zb[:, :], in_=pb[:, :], func=AF.Silu), reads=[pb], writes=[zb])
                    S.dma("pool", P.SZ[c * 128:(c + 1) * 128, q * 512:(q + 1) * 512], zb[:, :], reads=[zb], writes=[SZb[c]])
            for s in range(2):
                c = 2 * b + s
                a_s = aa[:, s, :]
                S.op("pe", lambda e, a_s=a_s: e.matmul(bankC[:, 0:32], lhsT=U[:, 0, :], rhs=a_s[:, 0:32], start=True, stop=True),
                     reads=[U, aa], writes=[bankC])
                S.op("pe", lambda e, a_s=a_s: e.matmul(bankC[:, 32:64], lhsT=U[:, 1, :], rhs=a_s[:, 32:64], start=True, stop=True),
                     reads=[U, aa], writes=[bankC])
                S.op("pe", lambda e, a_s=a_s: e.matmul(bankC[:, 64:128], lhsT=U[:, 4, :], rhs=a_s[:, 0:64], start=True, stop=True),
                     reads=[U, aa], writes=[bankC])
                S.op("dve", lambda e: e.tensor_copy(out=acum[:, :], in_=bankC[:, 0:64]), reads=[bankC], writes=[acum])
                S.op("dve", lambda e: e.tensor_tensor(out=dec[:, :], in0=bankC[:, 64:128], in1=acum[:, :], op=ALU.subtract),
                     reads=[bankC, acum], writes=[dec])
                S.op("act", lambda e: e.activation(out=dec[:, :], in_=dec[:, :], func=AF.Exp), reads=[dec], writes=[dec])
                S.op("act", lambda e: e.activation(out=eacum[:, :], in_=acum[:, :], func=AF.Exp), reads=[acum], writes=[eacum])
                S.op("act", lambda e: e.activation(out=eatot[:, :], in_=bankC[:, 64:128], func=AF.Exp), reads=[bankC], writes=[eatot])
                S.op("pool", lambda e: e.tensor_copy(out=ebst[:, 0:32], in_=eacum[:, 32:64]), reads=[eacum], writes=[ebst])
                S.op("pool", lambda e: e.tensor_copy(out=ebst[:, 32:64], in_=eatot[:, 32:64]), reads=[eatot], writes=[ebst])
                S.dma("pool", P.EB[c, :, :], ebst[:, :], reads=[ebst], writes=[EBb[c]])
                S.dma("pool", P.CT[c, :, :].rearrange("p (g l) -> p g l", g=4), Cf[:, :, s * 128:(s + 1) * 128], reads=[Cf], writes=[CTb[c]])
                for d in range(2):
                    S.op("dve", lambda e, d=d, s=s: e.tensor_tensor(
                        out=xdt[d][:, :].rearrange("p (h q) -> p h q", h=32), in0=xtok[:, s, :].rearrange("p (h q) -> p h q", h=32),
                        in1=dt2[:, s, d * 32:(d + 1) * 32].unsqueeze(2).to_broadcast([128, 32, 64]), op=ALU.mult),
                        reads=[xtok, dt2], writes=[xdt[d]])
                    S.op("pool", lambda e, d=d: e.tensor_tensor(
                        out=xdd[d][:, :].rearrange("p (h q) -> p h q", h=32), in0=xdt[d][:, :].rearrange("p (h q) -> p h q", h=32),
                        in1=dec[:, d * 32:(d + 1) * 32].unsqueeze(2).to_broadcast([128, 32, 64]), op=ALU.mult),
                        reads=[xdt[d], dec], writes=[xdd[d]])
                for gq in range(4):
                    S.op("pe", lambda e, gq=gq, s=s: e.matmul(
                        bankS[:, gq * 128:(gq + 1) * 128],
                        lhsT=Bf[:, gq, s * 128:(s + 1) * 128], rhs=Cf[:, gq, s * 128:(s + 1) * 128], start=True, stop=True),
                        reads=[Bf, Cf], writes=[bankS])
                for d in range(2):
                    S.op("dve", lambda e, d=d: e.tensor_tensor(
                        out=cbm[:, d, :, :], in0=bankS[:, :].rearrange("p (g l) -> p g l", g=4),
                        in1=U[:, d, :].unsqueeze(1).to_broadcast([128, 4, 128]), op=ALU.mult), reads=[bankS, U], writes=[cbm])
                for k16 in range(16):
                    u, d = k16 // 2, k16 % 2
                    gq, hq = u // 2, u % 2
                    h0 = gq * 8 + hq * 4
                    R_ = lseg[k16 % 4]
                    ba = segb[k16 % 4]
                    for hh in range(4):
                        S.op("dve", lambda e, R_=R_, d=d, h0=h0, hh=hh, s=s: e.tensor_scalar(
                            out=R_[:, hh * 128:(hh + 1) * 128], in0=U[:, d, :],
                            scalar1=aa[:, s, d * 32 + h0 + hh:d * 32 + h0 + hh + 1], scalar2=None, op0=ALU.mult),
                            reads=[U, aa], writes=[R_])
                    S.op("pe", lambda e, ba=ba, R_=R_, d=d: e.matmul(
                        ba[:, :], lhsT=U[:, 2 + d, :], rhs=R_[:, :], start=True, stop=True), reads=[R_, U], writes=[ba])
                    E_ = Eb[k16 % 3]
                    M_ = MT[k16]
                    S.op("act", lambda e, ba=ba, E_=E_: e.activation(out=E_[:, :], in_=ba[:, :], func=AF.Exp),
                         reads=[ba], writes=[E_])
                    S.op("dve" if k16 % 3 == 1 else "pool", lambda e, E_=E_, M_=M_, d=d, gq=gq: e.tensor_tensor(
                        out=M_[:, :, :], in0=E_[:, :].rearrange("p (h l) -> p h l", h=4),
                        in1=cbm[:, d, gq, :].unsqueeze(1).to_broadcast([128, 4, 128]), op=ALU.mult),
                        reads=[E_, cbm], writes=[M_])

                def S2(u):
                    gq, hq = u // 2, u % 2
                    by = bankY
                    for hh in range(4):
                        h = gq * 8 + hq * 4 + hh
                        col = (hq * 4 + hh) * 64
                        for d in range(2):
                            M_ = MT[u * 2 + d]
                            S.op("pe", lambda e, col=col, d=d, hh=hh, h=h, M_=M_: e.matmul(
                                by[:, col:col + 64], lhsT=M_[:, hh, :], rhs=xdt[d][:, h * 64:(h + 1) * 64],
                                start=(d == 0), stop=(d == 1)), reads=[M_, xdt[d]], writes=[by])
                    if hq == 0:
                        return
                    bo = bankO
                    S.op("pe", lambda e, gq=gq, s=s: e.matmul(
                        bo[:, :], lhsT=Cf[:, gq, s * 128:(s + 1) * 128], rhs=hf16[:, gq, :], start=True, stop=True),
                        reads=[Cf, hf16], writes=[bo])
                    yt_, ys_ = ytmp[gq % 2], ysb[gq % 2]
                    S.op("dve", lambda e, yt_=yt_, gq=gq: e.tensor_tensor(
                        out=yt_[:, :].rearrange("p (h q) -> p h q", h=8), in0=bo[:, :].rearrange("p (h q) -> p h q", h=8),
                        in1=eacum[:, gq * 8:(gq + 1) * 8].unsqueeze(2).to_broadcast([128, 8, 64]), op=ALU.mult),
                        reads=[bo, eacum], writes=[yt_])
                    S.op("dve", lambda e, yt_=yt_: e.tensor_tensor(out=yt_[:, :], in0=yt_[:, :], in1=by[:, :], op=ALU.add),
                         reads=[by, yt_], writes=[yt_])
                    S.op("pool", lambda e, ys_=ys_, gq=gq, s=s: e.tensor_tensor(
                        out=ys_[:, :].rearrange("p (h q) -> p h q", h=8),
                        in0=xtok[:, s, gq * 512:(gq + 1) * 512].rearrange("p (h q) -> p h q", h=8),
                        in1=Dbc[:, gq * 8:(gq + 1) * 8].unsqueeze(2).to_broadcast([128, 8, 64]), op=ALU.mult),
                        reads=[xtok, Dbc], writes=[ys_])
                    S.op("pool", lambda e, ys_=ys_, yt_=yt_: e.tensor_tensor(out=ys_[:, :], in0=ys_[:, :], in1=yt_[:, :], op=ALU.add),
                         reads=[ys_, yt_], writes=[ys_])
                    S.dma("pool", P.YP[c * 128:(c + 1) * 128, gq * 512:(gq + 1) * 512], ys_[:, :], reads=[ys_], writes=[YPb[c]])
                    for d in range(2):
                        S.op("pe", lambda e, d=d, gq=gq, s=s: e.matmul(
                            bankC[:, :], lhsT=Btok[:, s, gq * 128:(gq + 1) * 128], rhs=xdd[d][:, gq * 512:(gq + 1) * 512],
                            start=True, stop=True), reads=[Btok, xdd[d]], writes=[bankC])
                        if d == 0:
                            S.op("dve", lambda e, gq=gq: e.tensor_tensor(
                                out=hf[:, gq, :].rearrange("p (h q) -> p h q", h=8), in0=hf[:, gq, :].rearrange("p (h q) -> p h q", h=8),
                                in1=eatot[:, gq * 8:(gq + 1) * 8].unsqueeze(2).to_broadcast([128, 8, 64]), op=ALU.mult),
                                reads=[hf, eatot], writes=[hf])
                            S.op("dve", lambda e, gq=gq: e.tensor_tensor(out=hf[:, gq, :], in0=hf[:, gq, :], in1=bankC[:, :], op=ALU.add),
                                 reads=[hf, bankC], writes=[hf])
                            S.op("act", lambda e, gq=gq: e.copy(out=hf16[:, gq, :], in_=hf[:, gq, :]), reads=[hf], writes=[hf16])
                        else:
                            sb_ = sbst[gq % 2]
                            S.op("act", lambda e, sb_=sb_: e.copy(out=sb_[:, :], in_=bankC[:, :]), reads=[bankC], writes=[sb_])
                            S.dma("pool", P.SBs[c, :, gq * 512:(gq + 1) * 512], sb_[:, :], reads=[sb_], writes=[SBb[c]])

                for u in range(8):
                    S2(u)
        S.barrier()

    with contextlib.ExitStack() as st:
        wout = sb(st, "wout", [128, 16, D], BF16)
        with contextlib.ExitStack() as st2:
            stg = [sb(st2, "ostg%d" % k, [128, D]) for k in range(2)]
            engs = ["pool", "dve", "act"]
            for kc in range(16):
                sg_ = stg[kc % 2]
                S.dma("sp", sg_[:, :], I["ssd_w_out"][jj, kc * 128:(kc + 1) * 128, :], writes=[sg_])
                eng = engs[kc % 3]
                if eng == "act":
                    S.op(eng, lambda e, sg_=sg_, kc=kc: e.copy(out=wout[:, kc, :], in_=sg_[:, :]), reads=[sg_], writes=[wout])
                else:
                    S.op(eng, lambda e, sg_=sg_, kc=kc: e.tensor_copy(out=wout[:, kc, :], in_=sg_[:, :]), reads=[sg_], writes=[wout])
            S.barrier()
        bankO = [ps(st, "qO%d" % k) for k in range(2)]
        bankT = [ps(st, "qT%d" % k) for k in range(2)]
        bankX = [ps(st, "qX%d" % k) for k in range(2)]
        hb = sb(st, "hb", [128, 4, 512])
        hb16 = sb(st, "hb16", [128, 4, 512], BF16)
        ct = [sb(st, "ct%d" % k, [128, 4, 128], BF16) for k in range(2)]
        eb = [sb(st, "eb%d" % k, [128, 64]) for k in range(2)]
        sbl = [sb(st, "sbl%d" % k, [128, 2048]) for k in range(2)]
        yp = [sb(st, "yp%d" % k, [128, 2048]) for k in range(2)]
        szl = [sb(st, "szl%d" % k, [128, 2048]) for k in range(2)]
        xl = [sb(st, "xl%d" % k, [128, D]) for k in range(2)]
        ytmp = [sb(st, "qyt%d" % k, [128, 512]) for k in range(2)]
        gyT = sb(st, "gyT", [128, 16, 128], BF16)
        gnb = sb(st, "gnb", [128, 2048])
        m5bc = sb(st, "m5bc", [128, D])
        ep = [sb(st, "qep%d" % k, [128, 512]) for k in range(2)]
        stats = sb(st, "qstats", [128, 4, 6])
        mv = sb(st, "qmv", [128, 4, 2])
        t1 = sb(st, "qt1", [128, 4])
        t2 = sb(st, "qt2", [128, 4])
        rstd = sb(st, "qrstd", [128, 4])
        S.dma("sp", gnb[:, :], I["ssd_norm_g"][jj, :].partition_broadcast(128), writes=[gnb])
        S.op("pool", lambda e: e.memset(hb[:], 0.0), writes=[hb])
        S.op("pool", lambda e: e.memset(hb16[:], 0.0), writes=[hb16])
        order = [1, 0] + list(range(NCH - 1, 1, -1))
        cur_j = None
        for n, c in enumerate(order):
            j = 1 if c < 2 else 0
            skip_out = last and c < 2
            if j != cur_j:
                S.dma("sp", m5bc[:, :], P.Mrows[j, 5 * D:6 * D].partition_broadcast(128), reads=[P.Mb], writes=[m5bc])
                cur_j = j
            ct_, eb_, sbl_, yp_, szl_, xl_ = ct[n % 2], eb[n % 2], sbl[n % 2], yp[n % 2], szl[n % 2], xl[n % 2]
            S.dma("sp", eb_[:, :], P.EB[c, :, :], reads=[EBb[c]], writes=[eb_])
            S.dma("sp", sbl_[:, :], P.SBs[c, :, :], reads=[SBb[c]], writes=[sbl_])
            if not skip_out:
                S.dma("sp", ct_[:, :, :], P.CT[c, :, :].rearrange("p (g l) -> p g l", g=4), reads=[CTb[c]], writes=[ct_])
                S.dma("sp", yp_[:, :], P.YP[c * 128:(c + 1) * 128, :], reads=[YPb[c]], writes=[yp_])
                S.dma("sp", szl_[:, :], P.SZ[c * 128:(c + 1) * 128, :], reads=[SZb[c]], writes=[szl_])
                S.dma("sp", xl_[:, :], P.X[c * 128:(c + 1) * 128, :], reads=[P.Xb[c // 2]], writes=[xl_])
                for gq in range(4):
                    bo = bankO[gq % 2]
                    S.op("pe", lambda e, bo=bo, gq=gq, ct_=ct_: e.matmul(
                        bo[:, :], lhsT=ct_[:, gq, :], rhs=hb16[:, gq, :], start=True, stop=True), reads=[ct_, hb16], writes=[bo])
                    yt_ = ytmp[gq % 2]
                    S.op("dve", lambda e, bo=bo, yt_=yt_, gq=gq, eb_=eb_: e.tensor_tensor(
                        out=yt_[:, :].rearrange("p (h q) -> p h q", h=8), in0=bo[:, :].rearrange("p (h q) -> p h q", h=8),
                        in1=eb_[:, gq * 8:(gq + 1) * 8].unsqueeze(2).to_broadcast([128, 8, 64]), op=ALU.mult),
                        reads=[bo, eb_], writes=[yt_])
                    S.op("pool", lambda e, yt_=yt_, yp_=yp_, gq=gq: e.tensor_tensor(
                        out=yp_[:, gq * 512:(gq + 1) * 512], in0=yp_[:, gq * 512:(gq + 1) * 512], in1=yt_[:, :], op=ALU.add),
                        reads=[yt_, yp_], writes=[yp_])
                S.op("dve", lambda e, yp_=yp_, szl_=szl_: e.tensor_tensor(out=yp_[:, :], in0=yp_[:, :], in1=szl_[:, :], op=ALU.mult),
                     reads=[yp_, szl_], writes=[yp_])
                for gq in range(4):
                    S.op("dve", lambda e, gq=gq, yp_=yp_: e.bn_stats(out=stats[:, gq, :], in_=yp_[:, gq * 512:(gq + 1) * 512]),
                         reads=[yp_], writes=[stats])
                    S.op("dve", lambda e, gq=gq: e.bn_aggr(out=mv[:, gq, :], in_=stats[:, gq, :]), reads=[stats], writes=[mv])
                S.op("dve", lambda e: e.tensor_tensor(out=t1[:, :], in0=mv[:, :, 0], in1=mv[:, :, 0], op=ALU.mult), reads=[mv], writes=[t1])
                S.op("dve", lambda e: e.scalar_tensor_tensor(out=t2[:, :], in0=t1[:, :], scalar=EPS, in1=mv[:, :, 1],
                                                             op0=ALU.add, op1=ALU.add), reads=[t1, mv], writes=[t2])
                S.op("pool", lambda e: e.tensor_tensor(out=rstd[:, :], in0=t2[:, :], in1=P.neghalf[:, 0:4], op=ALU.pow),
                     reads=[t2, P.neghalf], writes=[rstd])
                for gq in range(4):
                    S.op("dve", lambda e, gq=gq, yp_=yp_: e.scalar_tensor_tensor(
                        out=yp_[:, gq * 512:(gq + 1) * 512], in0=yp_[:, gq * 512:(gq + 1) * 512], scalar=rstd[:, gq:gq + 1],
                        in1=gnb[:, gq * 512:(gq + 1) * 512], op0=ALU.mult, op1=ALU.mult), reads=[yp_, rstd, gnb], writes=[yp_])
                for kc in range(16):
                    pb = bankT[(kc // 4) % 2]
                    S.op("pe", lambda e, pb=pb, kc=kc, yp_=yp_: e.transpose(
                        pb[:, (kc % 4) * 128:(kc % 4 + 1) * 128], yp_[:, kc * 128:(kc + 1) * 128], ident[:, :]),
                        reads=[yp_, ident], writes=[pb])
                    if kc % 4 == 3:
                        S.op("act", lambda e, pb=pb, kc=kc: e.copy(
                            out=gyT[:, kc - 3:kc + 1, :], in_=pb[:, :].rearrange("p (a b) -> p a b", a=4)), reads=[pb], writes=[gyT])
                for hh in range(2):
                    pb = bankX[hh]
                    for kc in range(16):
                        S.op("pe", lambda e, pb=pb, kc=kc, hh=hh: e.matmul(
                            pb[:, :], lhsT=gyT[:, kc, :], rhs=wout[:, kc, hh * 512:(hh + 1) * 512],
                            start=(kc == 0), stop=(kc == 15)), reads=[gyT, wout], writes=[pb])
                    epb = ep[hh]
                    S.op("dve", lambda e, pb=pb, epb=epb, hh=hh: e.tensor_tensor(
                        out=epb[:, :], in0=pb[:, :], in1=m5bc[:, hh * 512:(hh + 1) * 512], op=ALU.mult), reads=[pb, m5bc], writes=[epb])
                    S.op("pool", lambda e, epb=epb, xl_=xl_, hh=hh: e.tensor_tensor(
                        out=xl_[:, hh * 512:(hh + 1) * 512], in0=epb[:, :], in1=xl_[:, hh * 512:(hh + 1) * 512], op=ALU.add),
                        reads=[epb, xl_], writes=[xl_])
                S.dma("pool", P.X[c * 128:(c + 1) * 128, :], xl_[:, :], reads=[xl_], writes=[P.Xb[c // 2]])
            S.op("dve", lambda e, eb_=eb_: e.tensor_tensor(
                out=hb[:, :, :].rearrange("p g (h q) -> p (g h) q", h=8), in0=hb[:, :, :].rearrange("p g (h q) -> p (g h) q", h=8),
                in1=eb_[:, 32:64].unsqueeze(2).to_broadcast([128, 32, 64]), op=ALU.mult), reads=[hb, eb_], writes=[hb])
            S.op("pool", lambda e, sbl_=sbl_: e.tensor_tensor(
                out=hb[:, :, :].rearrange("p g c -> p (g c)"), in0=hb[:, :, :].rearrange("p g c -> p (g c)"), in1=sbl_[:, :], op=ALU.add),
                reads=[hb, sbl_], writes=[hb])
            S.op("act", lambda e: e.copy(out=hb16[:, :, :], in_=hb[:, :, :]), reads=[hb], writes=[hb16])
        S.barrier()


HY_BANDS = 16
HY_MAX_DECAY = math.log(1e-2) / 0.3
HY_MIN_DECAY = math.log(1e-2) / 1.5


def hy_consts(Lf):
    import ml_dtypes
    bf = ml_dtypes.bfloat16
    f = np.float32
    N = 2 * Lf
    NA = N // 128
    NZ = Lf // 128
    a = np.arange(NZ)[:, None].astype(np.float64)
    c = np.arange(NA)[None, :].astype(np.float64)
    b = np.arange(128)[:, None].astype(np.float64)
    th = 2 * np.pi * a * c / NA
    FA = np.stack([np.cos(th), -np.sin(th)]).astype(bf)
    tw = 2 * np.pi * b * c / N
    TW = np.stack([np.cos(tw), -np.sin(tw)]).astype(f)
    IT = np.stack([np.cos(tw).T, np.sin(tw).T]).astype(f)
    CS = np.stack([np.cos(th).T / N, -np.sin(th).T / N, -np.cos(th).T / N]).astype(bf)
    pos = np.arange(Lf, dtype=f)
    t = np.linspace(0.0, 1.0, Lf, dtype=f)
    wpos = (f(2.0 * math.pi) * pos / f(Lf)).astype(f)
    fr = np.linspace(1e-4, HY_BANDS - 1, HY_BANDS, dtype=f)
    ang = wpos[:, None] * fr[None, :]
    z = np.concatenate([t[:, None], np.cos(ang), -np.sin(ang)], axis=-1).astype(f)
    return dict(FA=FA, TW=TW, IT=IT, CS=CS, zT=np.ascontiguousarray(z.T), trow=t.reshape(1, Lf).copy())


def hy_shared_consts():
    import ml_dtypes
    bf = ml_dtypes.bfloat16
    b = np.arange(128)[:, None].astype(np.float64)
    e = np.arange(128)[None, :].astype(np.float64)
    th = 2 * np.pi * b * e / 128
    F = np.stack([np.cos(th), -np.sin(th), np.sin(th), -np.cos(th)]).astype(bf)
    deltas = np.abs(np.linspace(HY_MIN_DECAY, HY_MAX_DECAY, D, dtype=np.float32))
    ndel = np.ascontiguousarray((-deltas).reshape(KC, 128).T.astype(np.float32))
    return F, ndel


def hy_dense_consts():
    import ml_dtypes
    bf = ml_dtypes.bfloat16
    N = 2 * CTX
    t = np.arange(CTX)[:, None].astype(np.float64)
    k = np.arange(N)[None, :].astype(np.float64)
    th = 2 * np.pi * t * k / N
    DF = np.stack([np.cos(th), -np.sin(th)]).astype(bf)
    DG = np.stack([np.cos(th).T / N, -np.sin(th).T / N]).astype(bf)
    return DF, DG


def hyena_phase(P, i, jj):
    nc, S, I = P.nc, P.S, P.I
    sb, ps = P.sb, P.ps
    ident = P.ident
    NB = P.NG
    UTb = [Buf(None, "hUT%d" % g) for g in range(NB)]
    Vb = Buf(None, "hV")
    with contextlib.ExitStack() as st:
        mixer_uT_pass(P, i, st, P.UT, UTb)
        S.barrier()

    with contextlib.ExitStack() as st:
        win = sb(st, "hwin", [128, KC, 3072], BF16)
        with contextlib.ExitStack() as st2:
            stg = [sb(st2, "hstg%d" % k, [128, 1536]) for k in range(2)]
            engs = ["pool", "dve", "act"]
            k = 0
            for kc in range(KC):
                for q in range(2):
                    sg_ = stg[k % 2]
                    S.dma("sp", sg_[:, :], I["hy_w_in"][jj, kc * 128:(kc + 1) * 128, q * 1536:(q + 1) * 1536], writes=[sg_])
                    eng = engs[k % 3]
                    if eng == "act":
                        S.op(eng, lambda e, sg_=sg_, kc=kc, q=q: e.copy(out=win[:, kc, q * 1536:(q + 1) * 1536], in_=sg_[:, :]),
                             reads=[sg_], writes=[win])
                    else:
                        S.op(eng, lambda e, sg_=sg_, kc=kc, q=q: e.tensor_copy(out=win[:, kc, q * 1536:(q + 1) * 1536], in_=sg_[:, :]),
                             reads=[sg_], writes=[win])
                    k += 1
            S.barrier()
        bankA = [ps(st, "hA%d" % k) for k in range(2)]
        bankS = ps(st, "hS")
        cw = load_cols(P, st, bankS, I["hy_conv_w"][jj, :, :].rearrange("k (c p) -> (k c) p", p=128), 72, "hcw")
        cbias = load_cols(P, st, bankS, I["hy_conv_b"][jj, :].rearrange("(c p) -> c p", p=128), 24, "hcb")
        uw = [sb(st, "huw%d" % k, [128, KC, 258], BF16) for k in range(2)]
        pre = [sb(st, "hpre%d" % k, [128, 258]) for k in range(2)]
        acc = [sb(st, "hacc%d" % k, [128, 256]) for k in range(2)]
        vb16 = [sb(st, "hvb%d" % k, [128, 256], BF16) for k in range(2)]
        for b in range(NB):
            t0 = b * 256
            seq_lo, seq_hi = (0, CTX) if b == 0 else (CTX, P.NT)
            lo, hi = max(t0 - 1, seq_lo), min(t0 + 257, seq_hi)
            w_ = uw[b % 2]
            if lo != t0 - 1 or hi != t0 + 257:
                S.op("pool", lambda e, w_=w_: e.memset(w_[:], 0.0), writes=[w_])
            S.dma("sp", w_[:, :, lo - (t0 - 1):hi - (t0 - 1)], P.UT[:, :, lo:hi].rearrange("kc p t -> p kc t"),
                  reads=[UTb[g] for g in range(max(b - 1, 0), min(b + 2, NB))], writes=[w_])
            for cch in range(24):
                pb = bankA[cch % 2]
                for kc in range(KC):
                    S.op("pe", lambda e, pb=pb, kc=kc, cch=cch, w_=w_: e.matmul(
                        pb[:, 0:258], lhsT=win[:, kc, cch * 128:(cch + 1) * 128], rhs=w_[:, kc, :],
                        start=(kc == 0), stop=(kc == KC - 1)), reads=[win, w_], writes=[pb])
                pr, a_ = pre[cch % 2], acc[cch % 2]
                S.op("act", lambda e, pb=pb, pr=pr: e.copy(out=pr[:, :], in_=pb[:, 0:258]), reads=[pb], writes=[pr])
                S.op("dve", lambda e, a_=a_, pr=pr, cch=cch: e.tensor_scalar(
                    out=a_[:, :], in0=pr[:, 0:256], scalar1=cw[:, cch:cch + 1], scalar2=cbias[:, cch:cch + 1],
                    op0=ALU.mult, op1=ALU.add), reads=[pr, cw, cbias], writes=[a_])
                for kk in range(1, 3):
                    S.op("dve", lambda e, a_=a_, pr=pr, cch=cch, kk=kk: e.scalar_tensor_tensor(
                        out=a_[:, :], in0=pr[:, kk:kk + 256], scalar=cw[:, kk * 24 + cch:kk * 24 + cch + 1], in1=a_[:, :],
                        op0=ALU.mult, op1=ALU.add), reads=[pr, cw, a_], writes=[a_])
                if cch < 8:
                    v_ = vb16[cch % 2]
                    S.op("act", lambda e, a_=a_, v_=v_: e.copy(out=v_[:, :], in_=a_[:, :]), reads=[a_], writes=[v_])
                    S.dma("pool", P.HV[cch, :, t0:t0 + 256], v_[:, :], reads=[v_])
                elif cch < 16:
                    S.dma("pool", P.HX1[cch - 8, :, t0:t0 + 256], a_[:, :], reads=[a_])
                else:
                    S.dma("pool", P.HX2[cch - 16, :, t0:t0 + 256], a_[:, :], reads=[a_])
        S.barrier()

    classes = [("c", CTX, 0, P.hyc_c), ("l", P.L, CTX, P.hyc_l)]

    with contextlib.ExitStack() as st:
        w1 = sb(st, "fw1", [33, 64])
        w2 = sb(st, "fw2", [64, 64])
        w3f = sb(st, "fw3f", [64, 4096])
        w3 = sb(st, "fw3", [64, 4096], BF16)
        ndel = sb(st, "ndel", [128, KC])
        bank1 = ps(st, "f1")
        bank2 = ps(st, "f2")
        bankH = [ps(st, "fH%d" % k) for k in range(2)]
        S.dma("sp", w1[:, :], I["hy_filt_w1"][jj, :, :], writes=[w1])
        S.dma("sp", w2[:, :], I["hy_filt_w2"][jj, :, :], writes=[w2])
        S.dma("sp", w3f[:, :], I["hy_filt_w3"][jj, :, :], writes=[w3f])
        S.op("dve", lambda e: e.tensor_copy(out=w3[:, :], in_=w3f[:, :]), reads=[w3f], writes=[w3])
        S.dma("sp", ndel[:, :], I["hy_ndel"][:, :], writes=[ndel])
        fcol = load_cols(P, st, bank1, I["hy_filt_fb"][jj, :, :], 4, "fcol")
        fb = sb(st, "ffb", [64, 2])
        S.op("dve", lambda e: e.tensor_tensor(out=fb[:, :], in0=fcol[0:64, 0:2], in1=fcol[0:64, 2:4], op=ALU.mult),
             reads=[fcol], writes=[fb])
        TWO_PI = 2.0 * math.pi

        def gen_filters(tag, Lf, base, Dd):
            nblk = max(Lf // 512, 1)
            bw = min(Lf, 512)
            zT = sb(st, "zT" + tag, [33, Lf])
            tbc = sb(st, "tbc" + tag, [128, Lf])
            S.dma("sp", zT[:, :], Dd["zT"][:, :], writes=[zT])
            S.dma("sp", tbc[:, :], Dd["trow"][0, :].partition_broadcast(128), writes=[tbc])
            stats = sb(st, "fstats" + tag, [128, 2, KC, 2 * nblk, 6])
            mv = sb(st, "fmv" + tag, [128, 2, KC, 2])
            arg = sb(st, "farg" + tag, [64, bw])
            wr = sb(st, "fwr" + tag, [64, bw])
            hid = sb(st, "fhid" + tag, [64, bw])
            hid2 = sb(st, "fhid2" + tag, [64, bw], BF16)
            dcy = sb(st, "fdcy" + tag, [128, KC, bw])
            h0 = [sb(st, "fh0%s%d" % (tag, k), [128, bw]) for k in range(2)]
            h1 = [sb(st, "fh1%s%d" % (tag, k), [128, bw]) for k in range(2)]
            so = [sb(st, "fso%s%d" % (tag, k), [128, bw], BF16) for k in range(2)]
            do = [sb(st, "fdo%s%d" % (tag, k), [128, bw], BF16) for k in range(2)]

            def sin_layer(src_bank, fi, dst):
                S.op("dve", lambda e: e.tensor_scalar(out=arg[:, :], in0=src_bank[0:64, 0:bw], scalar1=fcol[0:64, fi:fi + 1],
                                                      scalar2=fb[:, fi:fi + 1], op0=ALU.mult, op1=ALU.add),
                     reads=[src_bank, fcol, fb], writes=[arg])
                for _ in range(2):
                    S.op("dve", lambda e: e.tensor_scalar(out=wr[:, :], in0=arg[:, :], scalar1=math.pi, scalar2=-TWO_PI,
                                                          op0=ALU.is_gt, op1=ALU.mult), reads=[arg], writes=[wr])
                    S.op("dve", lambda e: e.tensor_tensor(out=arg[:, :], in0=arg[:, :], in1=wr[:, :], op=ALU.add),
                         reads=[arg, wr], writes=[arg])
                    S.op("dve", lambda e: e.tensor_scalar(out=wr[:, :], in0=arg[:, :], scalar1=-math.pi, scalar2=TWO_PI,
                                                          op0=ALU.is_lt, op1=ALU.mult), reads=[arg], writes=[wr])
                    S.op("dve", lambda e: e.tensor_tensor(out=arg[:, :], in0=arg[:, :], in1=wr[:, :], op=ALU.add),
                         reads=[arg, wr], writes=[arg])
                S.op("act", lambda e: e.activation(out=dst[:, :], in_=arg[:, :], func=AF.Sin), reads=[arg], writes=[dst])

            for lb in range(nblk):
                l0 = lb * bw
                S.op("pe", lambda e, l0=l0: e.matmul(bank1[0:64, 0:bw], lhsT=w1[:, :], rhs=zT[:, l0:l0 + bw], start=True, stop=True),
                     reads=[w1, zT], writes=[bank1])
                sin_layer(bank1, 0, hid)
                S.op("pe", lambda e: e.matmul(bank2[0:64, 0:bw], lhsT=w2[:, :], rhs=hid[:, :], start=True, stop=True),
                     reads=[w2, hid], writes=[bank2])
                sin_layer(bank2, 1, hid2)
                for dc in range(KC):
                    S.op("act", lambda e, dc=dc, l0=l0: e.activation(out=dcy[:, dc, :], in_=tbc[:, l0:l0 + bw], func=AF.Exp,
                                                                      scale=ndel[:, dc:dc + 1]), reads=[tbc, ndel], writes=[dcy])
                n = 0
                for filt in range(2):
                    for dc in range(KC):
                        hh = [h0[n % 2], h1[n % 2]]
                        for dr in range(2):
                            col = ((filt * 2 + dr) * KC + dc) * 128
                            pb = bankH[dr]
                            S.op("pe", lambda e, pb=pb, col=col: e.matmul(
                                pb[:, 0:bw], lhsT=w3[:, col:col + 128], rhs=hid2[:, :], start=True, stop=True),
                                reads=[w3, hid2], writes=[pb])
                            S.op("dve", lambda e, pb=pb, dr=dr, dc=dc, hh=hh: e.tensor_tensor(
                                out=hh[dr][:, :], in0=pb[:, 0:bw], in1=dcy[:, dc, :], op=ALU.mult), reads=[pb, dcy], writes=[hh[dr]])
                            S.op("dve", lambda e, dr=dr, dc=dc, hh=hh, filt=filt, lb=lb: e.bn_stats(
                                out=stats[:, filt, dc, dr * nblk + lb, :], in_=hh[dr][:, :]), reads=[hh[dr]], writes=[stats])
                        if lb == 0:
                            S.op("pool", lambda e, hh=hh: e.memset(hh[1][:, 0:1], 0.0), reads=[hh[1]], writes=[hh[1]])
                        s_, d_ = so[n % 2], do[n % 2]
                        S.op("pool", lambda e, hh=hh, s_=s_: e.tensor_tensor(out=s_[:, :], in0=hh[0][:, :], in1=hh[1][:, :], op=ALU.add),
                             reads=hh, writes=[s_])
                        S.op("pool", lambda e, hh=hh, d_=d_: e.tensor_tensor(out=d_[:, :], in0=hh[0][:, :], in1=hh[1][:, :], op=ALU.subtract),
                             reads=hh, writes=[d_])
                        S.dma("sp", Dd["KS"][filt, 0, dc, :, l0:l0 + bw], s_[:, :], reads=[s_])
                        S.dma("sp", Dd["KS"][filt, 1, dc, :, l0:l0 + bw], d_[:, :], reads=[d_])
                        n += 1
            for filt in range(2):
                for dc in range(KC):
                    S.op("dve", lambda e, filt=filt, dc=dc: e.bn_aggr(
                        out=mv[:, filt, dc, :], in_=stats[:, filt, dc, :, :].rearrange("p a b -> p (a b)")),
                        reads=[stats], writes=[mv])
            ssq = sb(st, "fssq" + tag, [128, 16])
            rn = sb(st, "frn" + tag, [128, 16])
            rrow = sb(st, "frrow" + tag, [16, 128])
            mvv = mv[:, :, :, :].rearrange("p f c t -> p (f c) t")
            S.op("dve", lambda e: e.tensor_tensor(out=ssq[:, :], in0=mvv[:, :, 0], in1=mvv[:, :, 0], op=ALU.mult), reads=[mv], writes=[ssq])
            S.op("dve", lambda e: e.tensor_tensor(out=ssq[:, :], in0=ssq[:, :], in1=mvv[:, :, 1], op=ALU.add), reads=[ssq, mv], writes=[ssq])
            S.op("dve", lambda e: e.tensor_scalar(out=ssq[:, :], in0=ssq[:, :], scalar1=float(2 * Lf), scalar2=EPS,
                                                  op0=ALU.mult, op1=ALU.add), reads=[ssq], writes=[ssq])
            S.op("pool", lambda e: e.tensor_tensor(out=rn[:, :], in0=ssq[:, :], in1=P.neghalf16[:, 0:16], op=ALU.pow),
                 reads=[ssq, P.neghalf16], writes=[rn])
            S.op("pe", lambda e: e.transpose(bank1[0:16, 0:128], rn[:, :], ident[:, :]), reads=[rn, ident], writes=[bank1])
            S.op("dve", lambda e: e.tensor_copy(out=rrow[:, :], in_=bank1[0:16, 0:128]), reads=[bank1], writes=[rrow])
            S.dma("sp", Dd["RN"][:, :].rearrange("f (c p) -> (f c) p", p=128), rrow[:, :], reads=[rrow])

        for cl in classes:
            gen_filters(*cl)
        S.barrier()

    NSLOT = 4
    with contextlib.ExitStack() as st:
        F128 = sb(st, "F128", [128, 4, 128], BF16)
        S.dma("sp", F128[:, :, :], I["hy_F128"][:, :, :].rearrange("m p q -> p m q"), writes=[F128])
        bias_bc = sb(st, "hbias", [128, 2, D])
        for filt in range(2):
            S.dma("sp", bias_bc[:, filt, :], I["hy_bias"][jj, filt, :].partition_broadcast(128), writes=[bias_bc])
        bankR = [ps(st, "hpR%d" % k) for k in range(NSLOT)]
        bankI = [ps(st, "hpI%d" % k) for k in range(NSLOT)]
        mts = [[sb(st, "hm%d_%d" % (s_, k), [128, 512]) for k in range(4)] for s_ in range(NSLOT)]

        def cmul(mt, Ar, Ai, Cr, Ci, Or, Oi, rA, np_, shape3):
            W_ = shape3[0] * shape3[1]

            def v(buf):
                return buf[0:np_, 0:W_].rearrange("p (c x) -> p c x", c=shape3[0])
            combos = ((Ar, Cr), (Ai, Ci), (Ar, Ci), (Ai, Cr))
            for k_, (A_, C_) in enumerate(combos):
                S.op("dve", lambda e, k_=k_, A_=A_, C_=C_: e.tensor_tensor(out=v(mt[k_]), in0=v(A_), in1=C_, op=ALU.mult),
                     reads=[A_] + rA, writes=[mt[k_]])
            S.op("pool", lambda e: e.tensor_tensor(out=v(Or), in0=v(mt[0]), in1=v(mt[1]), op=ALU.subtract),
                 reads=[mt[0], mt[1]], writes=[Or])
            S.op("pool", lambda e: e.tensor_tensor(out=v(Oi), in0=v(mt[2]), in1=v(mt[3]), op=ALU.add),
                 reads=[mt[2], mt[3]], writes=[Oi])

        def do_lat(tag, Lf, base, Dd):
            N = 2 * Lf
            NA = N // 128
            NZ = Lf // 128
            CB = 4
            G = CB * NA
            NGR = D // CB
            with contextlib.ExitStack() as stc:
                FA = sb(stc, "FA" + tag, [NZ, 2, NA], BF16)
                TW = sb(stc, "TW" + tag, [128, 2, NA])
                IT = sb(stc, "IT" + tag, [NA, 2, 128])
                CS = sb(stc, "CS" + tag, [NA, 3, NZ], BF16)
                rnb = sb(stc, "rnb" + tag, [128, 2, D])
                S.dma("sp", FA[:, :, :], Dd["FA"][:, :, :].rearrange("m p q -> p m q"), writes=[FA])
                S.dma("sp", TW[:, :, :], Dd["TW"][:, :, :].rearrange("m p q -> p m q"), writes=[TW])
                S.dma("sp", IT[:, :, :], Dd["IT"][:, :, :].rearrange("m p q -> p m q"), writes=[IT])
                S.dma("sp", CS[:, :, :], Dd["CS"][:, :, :].rearrange("m p q -> p m q"), writes=[CS])
                for filt in range(2):
                    S.dma("sp", rnb[:, filt, :], Dd["RN"][filt, :].partition_broadcast(128), writes=[rnb])
                TWr = TW[:, 0, :].unsqueeze(1).to_broadcast([128, CB, NA])
                TWi = TW[:, 1, :].unsqueeze(1).to_broadcast([128, CB, NA])
                ITr = IT[:, 0, :].unsqueeze(1).to_broadcast([NA, CB, 128])
                ITi = IT[:, 1, :].unsqueeze(1).to_broadcast([NA, CB, 128])
                xin = [[sb(stc, "hxin%s%d_%d" % (tag, s_, k), [NZ, CB, 128], BF16) for k in range(2)] for s_ in range(NSLOT)]
                PR = [sb(stc, "hPR%s%d" % (tag, s_), [128, 512], BF16) for s_ in range(NSLOT)]
                PI = [sb(stc, "hPI%s%d" % (tag, s_), [128, 512], BF16) for s_ in range(NSLOT)]

                def st_A(s_, x_):
                    for ch in range(CB):
                        for ri, pA in ((0, bankR[s_]), (1, bankI[s_])):
                            S.op("pe", lambda e, ch=ch, ri=ri, pA=pA, x_=x_: e.matmul(
                                pA[:, ch * NA:(ch + 1) * NA], lhsT=x_[:, ch, :], rhs=FA[:, ri, :], start=True, stop=True),
                                reads=[x_, FA], writes=[pA])

                Mq = [[sb(stc, "hMq%s%d_%d" % (tag, s_, k), [128, 512], BF16) for k in range(4)] for s_ in range(NSLOT)]
                cpI = [sb(stc, "hcpI%s%d" % (tag, s_), [128, 512]) for s_ in range(NSLOT)]

                def tw_products(s_, Cr, Ci, np_, shape3):
                    W_ = shape3[0] * shape3[1]

                    def v(buf):
                        return buf[0:np_, 0:W_].rearrange("p (c x) -> p c x", c=shape3[0])
                    Ar, Ai, c_, m_ = bankR[s_], bankI[s_], cpI[s_], Mq[s_]
                    S.op("act", lambda e: e.copy(out=c_[0:np_, 0:W_], in_=Ai[0:np_, 0:W_]), reads=[Ai], writes=[c_])
                    S.op("dve", lambda e: e.tensor_tensor(out=v(m_[0]), in0=v(Ar), in1=Cr, op=ALU.mult), reads=[Ar], writes=[m_[0]])
                    S.op("dve", lambda e: e.tensor_tensor(out=v(m_[2]), in0=v(Ar), in1=Ci, op=ALU.mult), reads=[Ar], writes=[m_[2]])
                    S.op("dve", lambda e: e.tensor_tensor(out=v(m_[1]), in0=v(Ai), in1=Ci, op=ALU.mult), reads=[Ai], writes=[m_[1]])
                    S.op("pool", lambda e: e.tensor_tensor(out=v(m_[3]), in0=v(c_), in1=Cr, op=ALU.mult), reads=[c_], writes=[m_[3]])

                def st_tw(s_):
                    tw_products(s_, TWr, TWi, 128, (CB, NA))

                def st_C(s_, need_re=True, need_im=True):
                    m_, pXr, pXi = Mq[s_], bankR[s_], bankI[s_]
                    if need_re:
                        for n_, (fi, mi) in enumerate(((0, 0), (3, 1), (2, 2), (2, 3))):
                            S.op("pe", lambda e, n_=n_, fi=fi, mi=mi: e.matmul(
                                pXr[:, 0:G], lhsT=F128[:, fi, :], rhs=m_[mi][:, 0:G], start=(n_ == 0), stop=(n_ == 3)),
                                reads=[F128, m_[mi]], writes=[pXr])
                    if need_im:
                        for n_, (fi, mi) in enumerate(((1, 0), (2, 1), (0, 2), (0, 3))):
                            S.op("pe", lambda e, n_=n_, fi=fi, mi=mi: e.matmul(
                                pXi[:, 0:G], lhsT=F128[:, fi, :], rhs=m_[mi][:, 0:G], start=(n_ == 0), stop=(n_ == 3)),
                                reads=[F128, m_[mi]], writes=[pXi])

                kout = [sb(stc, "hko%s%d" % (tag, k), [128, G], BF16) for k in range(NSLOT)]
                ktmp = [sb(stc, "hkt%s%d" % (tag, k), [128, G]) for k in range(NSLOT)]
                items = [(filt, sd, gr) for filt in range(2) for sd in range(2) for gr in range(NGR)]
                for w0 in range(0, len(items), NSLOT):
                    wave = items[w0:w0 + NSLOT]
                    par = (w0 // NSLOT) % 2
                    for s_, (filt, sd, gr) in enumerate(wave):
                        d0 = gr * CB
                        S.dma("sp", xin[s_][par][:, :, :],
                              Dd["KS"][filt, sd, d0 // 128, d0 % 128:d0 % 128 + CB, :].rearrange("c (a b) -> a c b", b=128),
                              writes=[xin[s_][par]])
                    for s_, it in enumerate(wave):
                        st_A(s_, xin[s_][par])
                    for s_, it in enumerate(wave):
                        st_tw(s_)
                    for s_, (filt, sd, gr) in enumerate(wave):
                        st_C(s_, need_re=(sd == 0), need_im=(sd == 1))
                    for s_, (filt, sd, gr) in enumerate(wave):
                        d0 = gr * CB
                        src = bankR[s_] if sd == 0 else bankI[s_]
                        ko, kt = kout[s_], ktmp[s_]
                        rn3 = rnb[:, filt, d0:d0 + CB].unsqueeze(2).to_broadcast([128, CB, NA])
                        if sd == 0:
                            S.op("dve", lambda e, src=src, rn3=rn3, kt=kt: e.tensor_tensor(
                                out=kt[:, 0:G].rearrange("p (c x) -> p c x", c=CB), in0=src[:, 0:G].rearrange("p (c x) -> p c x", c=CB),
                                in1=rn3, op=ALU.mult), reads=[src, rnb], writes=[kt])
                            b3 = bias_bc[:, filt, d0:d0 + CB].unsqueeze(2).to_broadcast([128, CB, NA])
                            S.op("pool", lambda e, ko=ko, b3=b3, kt=kt: e.tensor_tensor(
                                out=ko[:, :].rearrange("p (c x) -> p c x", c=CB), in0=kt[:, 0:G].rearrange("p (c x) -> p c x", c=CB),
                                in1=b3, op=ALU.add), reads=[kt, bias_bc], writes=[ko])
                        else:
                            S.op("dve", lambda e, src=src, rn3=rn3, ko=ko: e.tensor_tensor(
                                out=ko[:, :].rearrange("p (c x) -> p c x", c=CB), in0=src[:, 0:G].rearrange("p (c x) -> p c x", c=CB),
                                in1=rn3, op=ALU.mult), reads=[src, rnb], writes=[ko])
                        S.dma("pool", Dd["KF"][filt, sd, :, d0:d0 + CB, :], ko[:, :].rearrange("p (c x) -> p c x", c=CB), reads=[ko])
                S.barrier()

                x1b = [[sb(stc, "hx1%s%d_%d" % (tag, s_, k), [NZ, CB, 128]) for k in range(2)] for s_ in range(NSLOT)]
                x2b = [[sb(stc, "hx2%s%d_%d" % (tag, s_, k), [NZ, CB, 128]) for k in range(2)] for s_ in range(NSLOT)]
                kf = [[[sb(stc, "hkf%s%d_%d_%d" % (tag, s_, k, m), [128, G], BF16) for m in range(4)] for k in range(2)] for s_ in range(NSLOT)]
                zt = [sb(stc, "hzt%s%d" % (tag, s_), [NZ, CB, 128], BF16) for s_ in range(NSLOT)]
                yo = [sb(stc, "hyo%s%d" % (tag, s_), [NZ, CB, 128]) for s_ in range(NSLOT)]

                def blk(T, d0):
                    return T[d0 // 128, d0 % 128:d0 % 128 + CB, base:base + Lf].rearrange("c (a b) -> a c b", b=128)

                for w0 in range(0, NGR, NSLOT):
                    grs = list(range(w0, min(w0 + NSLOT, NGR)))
                    par = (w0 // NSLOT) % 2
                    for s_, gr in enumerate(grs):
                        d0 = gr * CB
                        S.dma("sp", xin[s_][par][:, :, :], blk(P.HV, d0), writes=[xin[s_][par]])
                        S.dma("sp", x1b[s_][par][:, :, :], blk(P.HX1, d0), writes=[x1b[s_][par]])
                        S.dma("sp", x2b[s_][par][:, :, :], blk(P.HX2, d0), writes=[x2b[s_][par]])
                        for m in range(4):
                            S.dma("sp", kf[s_][par][m][:, :].rearrange("p (c x) -> p c x", c=CB),
                                  Dd["KF"][m // 2, m % 2, :, d0:d0 + CB, :], writes=[kf[s_][par][m]])
                    for q in range(2):
                        for s_, gr in enumerate(grs):
                            st_A(s_, xin[s_][par] if q == 0 else zt[s_])
                        for s_, gr in enumerate(grs):
                            st_tw(s_)
                        for s_, gr in enumerate(grs):
                            st_C(s_)
                        for s_, gr in enumerate(grs):
                            kf_ = kf[s_][par]
                            cmul(mts[s_], bankR[s_], bankI[s_], kf_[2 * q][:, :].rearrange("p (c x) -> p c x", c=CB),
                                 kf_[2 * q + 1][:, :].rearrange("p (c x) -> p c x", c=CB), PR[s_], PI[s_],
                                 [kf_[2 * q], kf_[2 * q + 1]], 128, (CB, NA))
                        for s_, gr in enumerate(grs):
                            Yr, Yi, pPr, pPi = PR[s_], PI[s_], bankR[s_], bankI[s_]
                            for ch in range(CB):
                                S.op("pe", lambda e, ch=ch, Yr=Yr, pPr=pPr: e.matmul(
                                    pPr[0:NA, ch * 128:(ch + 1) * 128], lhsT=Yr[:, ch * NA:(ch + 1) * NA], rhs=F128[:, 0, :],
                                    start=True, stop=False), reads=[Yr, F128], writes=[pPr])
                                S.op("pe", lambda e, ch=ch, Yi=Yi, pPr=pPr: e.matmul(
                                    pPr[0:NA, ch * 128:(ch + 1) * 128], lhsT=Yi[:, ch * NA:(ch + 1) * NA], rhs=F128[:, 1, :],
                                    start=False, stop=True), reads=[Yi, F128], writes=[pPr])
                                S.op("pe", lambda e, ch=ch, Yr=Yr, pPi=pPi: e.matmul(
                                    pPi[0:NA, ch * 128:(ch + 1) * 128], lhsT=Yr[:, ch * NA:(ch + 1) * NA], rhs=F128[:, 2, :],
                                    start=True, stop=False), reads=[Yr, F128], writes=[pPi])
                                S.op("pe", lambda e, ch=ch, Yi=Yi, pPi=pPi: e.matmul(
                                    pPi[0:NA, ch * 128:(ch + 1) * 128], lhsT=Yi[:, ch * NA:(ch + 1) * NA], rhs=F128[:, 0, :],
                                    start=False, stop=True), reads=[Yi, F128], writes=[pPi])
                        for s_, gr in enumerate(grs):
                            tw_products(s_, ITr, ITi, NA, (CB, 128))
                        for s_, gr in enumerate(grs):
                            m_, pY = Mq[s_], bankR[s_]
                            for n_, (ci, mi) in enumerate(((0, 0), (2, 1), (1, 2), (1, 3))):
                                S.op("pe", lambda e, n_=n_, ci=ci, mi=mi, m_=m_, pY=pY: e.matmul(
                                    pY[0:NZ, 0:CB * 128], lhsT=CS[:, ci, :], rhs=m_[mi][0:NA, 0:CB * 128],
                                    start=(n_ == 0), stop=(n_ == 3)), reads=[CS, m_[mi]], writes=[pY])
                        for s_, gr in enumerate(grs):
                            mul_ = x1b[s_][par] if q == 0 else x2b[s_][par]
                            dst = zt[s_] if q == 0 else yo[s_]
                            pY = bankR[s_]
                            S.op("dve", lambda e, mul_=mul_, dst=dst, pY=pY: e.tensor_tensor(
                                out=dst[:, :, :], in0=pY[0:NZ, 0:CB * 128].rearrange("p (c x) -> p c x", c=CB), in1=mul_[:, :, :], op=ALU.mult),
                                reads=[pY, mul_], writes=[dst])
                    for s_, gr in enumerate(grs):
                        S.dma("pool", blk(P.HY, gr * CB), yo[s_][:, :, :], reads=[yo[s_]])
                S.barrier()

        def do_ctx(tag, Lf, base, Dd):
            with contextlib.ExitStack() as stc:
                DF = sb(stc, "cDF", [128, 2, 2, 512], BF16)
                DG = sb(stc, "cDG", [128, 2, 4, 256], BF16)
                S.dma("sp", DF[:, :, :, :], I["hy_DF"][:, :, :].rearrange("m (tt p) k -> p m tt k", p=128), writes=[DF])
                S.dma("sp", DG[:, :, :, :], I["hy_DG"][:, :, :].rearrange("m (kc p) t -> p m kc t", p=128), writes=[DG])
                identb = sb(stc, "cidb", [128, 128], BF16)
                S.op("dve", lambda e: e.tensor_copy(out=identb[:, :], in_=ident[:, :]), reads=[ident], writes=[identb])
                rnb = sb(stc, "crnb", [128, 2, D])
                for filt in range(2):
                    S.dma("sp", rnb[:, filt, :], Dd["RN"][filt, :].partition_broadcast(128), writes=[rnb])
                fm16 = [sb(stc, "cfm16_%d" % k, [128, KC, 256], BF16) for k in range(2)]
                fm32 = [sb(stc, "cfm32_%d" % k, [128, KC, 256]) for k in range(2)]
                tok16 = [sb(stc, "ctok16_%d" % k, [128, 2, D], BF16) for k in range(2)]
                x1t = sb(stc, "cx1t", [128, 2, D])
                x2t = sb(stc, "cx2t", [128, 2, D])
                ytk = sb(stc, "cytk", [128, 2, D])
                Kc = [sb(stc, "cKc%d" % k, [128, 4, D], BF16) for k in range(4)]
                Yr = sb(stc, "cYr", [128, 4, D], BF16)
                Yi = sb(stc, "cYi", [128, 4, D], BF16)
                ktmp = sb(stc, "cktmp", [128, 512])
                pTb = bankI[NSLOT - 1]
                pTb_ap = pTb[:, :].bitcast(BF16)

                def to_tok16(src_ap, dst, k):
                    f_ = fm16[k % 2]
                    S.dma("sp", f_[:, :, :], src_ap, writes=[f_])
                    for tt in range(2):
                        for dc in range(KC):
                            S.op("pe", lambda e, tt=tt, dc=dc, f_=f_: e.transpose(
                                pTb_ap[:, dc * 128:(dc + 1) * 128], f_[:, dc, tt * 128:(tt + 1) * 128], identb[:, :]),
                                reads=[f_, identb], writes=[pTb])
                        S.op("act", lambda e, tt=tt, dst=dst: e.copy(out=dst[:, tt, :], in_=pTb_ap[:, :]), reads=[pTb], writes=[dst])

                def to_tok32(src_ap, dst, k):
                    f_ = fm32[k % 2]
                    S.dma("sp", f_[:, :, :], src_ap, writes=[f_])
                    for tt in range(2):
                        for hf_ in range(2):
                            pb = bankR[hf_]
                            for dq in range(4):
                                dc = hf_ * 4 + dq
                                S.op("pe", lambda e, tt=tt, dc=dc, dq=dq, f_=f_, pb=pb: e.transpose(
                                    pb[:, dq * 128:(dq + 1) * 128], f_[:, dc, tt * 128:(tt + 1) * 128], ident[:, :]),
                                    reads=[f_, ident], writes=[pb])
                            S.op("act", lambda e, tt=tt, hf_=hf_, dst=dst, pb=pb: e.copy(
                                out=dst[:, tt, hf_ * 512:(hf_ + 1) * 512], in_=pb[:, :]), reads=[pb], writes=[dst])

                def ctx_src(T):
                    return T[:, :, base:base + Lf].rearrange("c p t -> p c t")

                n = 0
                for filt in range(2):
                    for sd in range(2):
                        tk = tok16[n % 2]
                        to_tok16(Dd["KS"][filt, sd, :, :, :].rearrange("c p t -> p c t"), tk, n)
                        for kc in range(4):
                            for hf_ in range(2):
                                pb = bankI[(kc * 2 + hf_) % NSLOT]
                                for tt in range(2):
                                    S.op("pe", lambda e, pb=pb, sd=sd, tt=tt, kc=kc, hf_=hf_, tk=tk: e.matmul(
                                        pb[:, :], lhsT=DF[:, sd, tt, kc * 128:(kc + 1) * 128], rhs=tk[:, tt, hf_ * 512:(hf_ + 1) * 512],
                                        start=(tt == 0), stop=(tt == 1)), reads=[DF, tk], writes=[pb])
                                dstK = Kc[filt * 2 + sd]
                                if sd == 0:
                                    S.op("dve", lambda e, pb=pb, filt=filt, hf_=hf_: e.tensor_tensor(
                                        out=ktmp[:, :], in0=pb[:, :], in1=rnb[:, filt, hf_ * 512:(hf_ + 1) * 512], op=ALU.mult),
                                        reads=[pb, rnb], writes=[ktmp])
                                    S.op("pool", lambda e, filt=filt, hf_=hf_, kc=kc, dstK=dstK: e.tensor_tensor(
                                        out=dstK[:, kc, hf_ * 512:(hf_ + 1) * 512], in0=ktmp[:, :],
                                        in1=bias_bc[:, filt, hf_ * 512:(hf_ + 1) * 512], op=ALU.add), reads=[ktmp, bias_bc], writes=[dstK])
                                else:
                                    S.op("dve", lambda e, pb=pb, filt=filt, hf_=hf_, kc=kc, dstK=dstK: e.tensor_tensor(
                                        out=dstK[:, kc, hf_ * 512:(hf_ + 1) * 512], in0=pb[:, :],
                                        in1=rnb[:, filt, hf_ * 512:(hf_ + 1) * 512], op=ALU.mult), reads=[pb, rnb], writes=[dstK])
                        n += 1
                to_tok16(ctx_src(P.HV), tok16[0], 0)
                to_tok32(ctx_src(P.HX1), x1t, 0)
                to_tok32(ctx_src(P.HX2), x2t, 1)
                for q in range(2):
                    xin_ = tok16[q]
                    mul_ = x1t if q == 0 else x2t
                    for kc in range(4):
                        for hf_ in range(2):
                            s_ = (kc * 2 + hf_) % NSLOT
                            pR, pI = bankR[s_], bankI[s_]
                            for ri, pb in ((0, pR), (1, pI)):
                                for tt in range(2):
                                    S.op("pe", lambda e, pb=pb, ri=ri, tt=tt, kc=kc, hf_=hf_, xin_=xin_: e.matmul(
                                        pb[:, :], lhsT=DF[:, ri, tt, kc * 128:(kc + 1) * 128], rhs=xin_[:, tt, hf_ * 512:(hf_ + 1) * 512],
                                        start=(tt == 0), stop=(tt == 1)), reads=[DF, xin_], writes=[pb])
                            Kr_, Ki_ = Kc[2 * q], Kc[2 * q + 1]
                            mt = mts[s_]
                            sl = slice(hf_ * 512, (hf_ + 1) * 512)
                            combos = ((pR, Kr_), (pI, Ki_), (pR, Ki_), (pI, Kr_))
                            for k_, (A_, C_) in enumerate(combos):
                                S.op("dve", lambda e, k_=k_, A_=A_, C_=C_, kc=kc, sl=sl, mt=mt: e.tensor_tensor(
                                    out=mt[k_][:, :], in0=A_[:, :], in1=C_[:, kc, sl], op=ALU.mult), reads=[A_, C_], writes=[mt[k_]])
                            S.op("pool", lambda e, mt=mt, kc=kc, sl=sl: e.tensor_tensor(
                                out=Yr[:, kc, sl], in0=mt[0][:, :], in1=mt[1][:, :], op=ALU.subtract), reads=[mt[0], mt[1]], writes=[Yr])
                            S.op("pool", lambda e, mt=mt, kc=kc, sl=sl: e.tensor_tensor(
                                out=Yi[:, kc, sl], in0=mt[2][:, :], in1=mt[3][:, :], op=ALU.add), reads=[mt[2], mt[3]], writes=[Yi])
                    for tt in range(2):
                        for hf_ in range(2):
                            pb = bankR[(tt * 2 + hf_) % NSLOT]
                            sl = slice(hf_ * 512, (hf_ + 1) * 512)
                            for kc in range(4):
                                S.op("pe", lambda e, pb=pb, kc=kc, tt=tt, sl=sl: e.matmul(
                                    pb[:, :], lhsT=DG[:, 0, kc, tt * 128:(tt + 1) * 128], rhs=Yr[:, kc, sl],
                                    start=(kc == 0), stop=False), reads=[DG, Yr], writes=[pb])
                                S.op("pe", lambda e, pb=pb, kc=kc, tt=tt, sl=sl: e.matmul(
                                    pb[:, :], lhsT=DG[:, 1, kc, tt * 128:(tt + 1) * 128], rhs=Yi[:, kc, sl],
                                    start=False, stop=(kc == 3)), reads=[DG, Yi], writes=[pb])
                            dst = tok16[1] if q == 0 else ytk
                            S.op("dve", lambda e, pb=pb, tt=tt, sl=sl, dst=dst, mul_=mul_: e.tensor_tensor(
                                out=dst[:, tt, sl], in0=pb[:, :], in1=mul_[:, tt, sl], op=ALU.mult), reads=[pb, mul_], writes=[dst])
                yfm = fm32[0]
                for dc in range(KC):
                    pb = bankI[dc % NSLOT]
                    for tt in range(2):
                        S.op("pe", lambda e, pb=pb, tt=tt, dc=dc: e.transpose(
                            pb[:, tt * 128:(tt + 1) * 128], ytk[:, tt, dc * 128:(dc + 1) * 128], ident[:, :]),
                            reads=[ytk, ident], writes=[pb])
                    S.op("act", lambda e, pb=pb, dc=dc: e.copy(out=yfm[:, dc, :], in_=pb[:, 0:256]), reads=[pb], writes=[yfm])
                S.dma("pool", ctx_src(P.HY), yfm[:, :, :], reads=[yfm])
                S.barrier()

        do_ctx(*classes[0])
        do_lat(*classes[1])

    with contextlib.ExitStack() as st:
        wo = sb(st, "hwo", [128, KC, D], BF16)
        with contextlib.ExitStack() as st2:
            stg = [sb(st2, "hostg%d" % k, [128, D]) for k in range(2)]
            for kc in range(KC):
                sg_ = stg[kc % 2]
                S.dma("sp", sg_[:, :], I["hy_w_out"][jj, kc * 128:(kc + 1) * 128, :], writes=[sg_])
                S.op("dve", lambda e, sg_=sg_, kc=kc: e.tensor_copy(out=wo[:, kc, :], in_=sg_[:, :]), reads=[sg_], writes=[wo])
            S.barrier()
        yT = [sb(st, "hyT%d" % k, [128, KC, 256]) for k in range(2)]
        yT16 = [sb(st, "hyT16%d" % k, [128, KC, 256], BF16) for k in range(2)]
        xb = [sb(st, "hxb%d" % k, [128, 2, D]) for k in range(2)]
        ep = [sb(st, "hep%d" % k, [128, 512]) for k in range(2)]
        m5bc = sb(st, "hm5", [128, D])
        bankX = [ps(st, "hX%d" % k) for k in range(4)]
        cur_j = None
        for b in range(NB):
            j = 1 if b == 0 else 0
            if j != cur_j:
                S.dma("sp", m5bc[:, :], P.Mrows[j, 5 * D:6 * D].partition_broadcast(128), reads=[P.Mb], writes=[m5bc])
                cur_j = j
            y_, y16, xt = yT[b % 2], yT16[b % 2], xb[b % 2]
            S.dma("sp", y_[:, :, :], P.HY[:, :, b * 256:(b + 1) * 256].rearrange("c p t -> p c t"), writes=[y_])
            S.dma("sp", xt[:], group_src(P, b, False), reads=[P.Xb[b]], writes=[xt])
            S.op("act", lambda e, y_=y_, y16=y16: e.copy(out=y16[:, :, :], in_=y_[:, :, :]), reads=[y_], writes=[y16])
            for s in range(2):
                for hh in range(2):
                    pb = bankX[s * 2 + hh]
                    for kc in range(KC):
                        S.op("pe", lambda e, pb=pb, s=s, hh=hh, kc=kc, y16=y16: e.matmul(
                            pb[:, :], lhsT=y16[:, kc, s * 128:(s + 1) * 128], rhs=wo[:, kc, hh * 512:(hh + 1) * 512],
                            start=(kc == 0), stop=(kc == KC - 1)), reads=[y16, wo], writes=[pb])
                    epb = ep[hh]
                    S.op("dve", lambda e, pb=pb, epb=epb, hh=hh: e.tensor_tensor(
                        out=epb[:, :], in0=pb[:, :], in1=m5bc[:, hh * 512:(hh + 1) * 512], op=ALU.mult), reads=[pb, m5bc], writes=[epb])
                    S.op("pool", lambda e, epb=epb, xt=xt, s=s, hh=hh: e.tensor_tensor(
                        out=xt[:, s, hh * 512:(hh + 1) * 512], in0=epb[:, :], in1=xt[:, s, hh * 512:(hh + 1) * 512], op=ALU.add),
                        reads=[epb, xt], writes=[xt])
            S.dma("pool", P.X[b * 256:(b + 1) * 256, :].rearrange("(s p) d -> p s d", p=128), xt[:], reads=[xt], writes=[P.Xb[b]])
        S.barrier()
```

```python
import contextlib
import math
import numpy as np
import concourse.bass as bass
import concourse.mybir as mybir
from concourse.bass_utils import run_bass_kernel_spmd

F32 = mybir.dt.float32
BF16 = mybir.dt.bfloat16
AF = mybir.ActivationFunctionType
ALU = mybir.AluOpType
AX = mybir.AxisListType

D = 1024
DFF = 2816
NMOD = 9
DEPTH = 4
CTX = 256
EPS = 1e-6
KC = 8
FC = 22
GRID_W = 64


class Buf:
    def __init__(self, t, name=""):
        self.t = t
        self.w = None
        self.r = {}
        self.name = name

    def __getitem__(self, idx):
        return self.t[idx]


class Sched:
    ENGS = ["pe", "act", "dve", "pool", "sp"]

    def __init__(self, nc, st, nds=40):
        self.nc = nc
        self.sem = {e: st.enter_context(nc.semaphore("sem_" + e)) for e in self.ENGS}
        self.cnt = {e: 0 for e in self.ENGS}
        self.prog = {e: [] for e in self.ENGS}
        self.known = {e: {} for e in self.ENGS}
        self.dsem = [st.enter_context(nc.semaphore("dsem%d" % i)) for i in range(nds)]
        self.dcnt = [0] * nds
        self.dnext = 0

    def _need(self, eng, dep, waits):
        if dep is None:
            return
        kind, a, v = dep
        if kind == "E":
            if a == eng and a == "pe":
                return
            key = ("E", a)
        else:
            key = ("D", a)
        if self.known[eng].get(key, 0) >= v:
            return
        self.known[eng][key] = v
        waits.append((key, v))

    def _deps(self, eng, reads, writes):
        waits = []
        for b in reads:
            self._need(eng, b.w, waits)
        for b in writes:
            if not (b.w is not None and b.w[0] == "E" and b.w[1] == eng):
                self._need(eng, b.w, waits)
            for d in b.r.values():
                self._need(eng, d, waits)
        return waits

    def op(self, eng, fn, reads=(), writes=()):
        waits = self._deps(eng, reads, writes)
        self.cnt[eng] += 1
        dep = ("E", eng, self.cnt[eng])
        self.prog[eng].append((waits, fn, dep))
        for b in writes:
            b.w = dep
            b.r = {}
        for b in reads:
            if b not in writes:
                b.r[eng] = dep

    def dma(self, q, out, in_, reads=(), writes=(), **kw):
        waits = self._deps(q, reads, writes)
        s = self.dnext
        self.dnext = (self.dnext + 1) % len(self.dsem)
        if self.dcnt[s] > 0:
            self._need(q, ("D", s, self.dcnt[s]), waits)
        self.dcnt[s] += 16
        dep = ("D", s, self.dcnt[s])
        self.prog[q].append((waits, (lambda e: e.dma_start(out=out, in_=in_, **kw)), dep))
        for b in writes:
            b.w = dep
            b.r = {}
        for b in reads:
            if b not in writes:
                b.r["dma%d" % s] = dep

    def barrier(self):
        for e in self.ENGS:
            waits = []
            for f in self.ENGS:
                if f != e and self.cnt[f] > 0:
                    self._need(e, ("E", f, self.cnt[f]), waits)
            for s in range(len(self.dsem)):
                if self.dcnt[s] > 0:
                    self._need(e, ("D", s, self.dcnt[s]), waits)
            if waits:
                self.prog[e].append((waits, None, None))

    def emit(self, block):
        decos = dict(pe=block.tensor, act=block.scalar, dve=block.vector, pool=block.gpsimd, sp=block.sync)
        for name in self.ENGS:
            prog = self.prog[name]

            def body(e, prog=prog, name=name):
                for waits, fn, dep in prog:
                    for key, v in waits:
                        sem = self.sem[key[1]] if key[0] == "E" else self.dsem[key[1]]
                        e.wait_ge(sem, v)
                    if fn is None:
                        continue
                    ins = fn(e)
                    if dep[0] == "E":
                        ins.then_inc(self.sem[name], 1)
                    else:
                        ins.then_inc(self.dsem[dep[1]], 16)

            decos[name](body)


class Ctx:
    pass


def build_program(L, layers=DEPTH, mixers=(0, 1, 2), dbg_ctx=False):
    NT = CTX + L
    NG = NT // 256
    nc = bass.Bass("TRN2", target_bir_lowering=False)
    P = Ctx()
    P.nc = nc
    P.dbg_ctx = dbg_ctx
    P.L = L
    P.NT = NT
    P.NG = NG

    def din(name, shape, dt=F32):
        return nc.dram_tensor(name, list(shape), dt, kind="ExternalInput").ap()

    def dscr(name, shape, dt=F32):
        return nc.dram_tensor(name, list(shape), dt, kind="Internal").ap()

    I = {}
    I["x"] = din("x", [L, D])
    I["ctx"] = din("ctx", [CTX, D])
    I["c"] = din("c", [KC, 128])
    I["c_ctx"] = din("c_ctx", [KC, 128])
    I["ada_w"] = din("ada_w", [DEPTH, D, NMOD * D])
    I["ada_b"] = din("ada_b", [DEPTH, NMOD * KC, 128])
    I["norm_g"] = din("norm_g", [DEPTH, 3 * KC, 128])
    I["norm_g_rows"] = din("norm_g_rows", [DEPTH, 3, D])
    I["ffn_w_gate"] = din("ffn_w_gate", [DEPTH, 2, D, DFF])
    I["ffn_w_up"] = din("ffn_w_up", [DEPTH, 2, D, DFF])
    I["ffn_w_down"] = din("ffn_w_down", [DEPTH, 2, DFF, D])
    I["final_g"] = din("final_g", [1, D])
    I["ident"] = din("ident", [128, 128])
    NK = L // 128
    I["poolP"] = din("poolP", [19, 128, 128], BF16)
    I["poolPc"] = din("poolPc", [16, 128, 128], BF16)
    I["pool_inv"] = din("pool_inv", [128, NK, 4])
    I["pool_invc"] = din("pool_invc", [128, 2, 4])
    I["pool_w"] = din("pool_w", [1, 4, 256, 256])
    I["pool_b"] = din("pool_b", [1, D])
    I["pool_scale"] = din("pool_scale", [1, D])
    I["ssd_w_in"] = din("ssd_w_in", [2, D, SSD_IN])
    I["ssd_conv_w"] = din("ssd_conv_w", [2, 5, 3072])
    I["ssd_conv_b"] = din("ssd_conv_b", [2, 3072])
    I["ssd_a_log"] = din("ssd_a_log", [2, 64])
    I["ssd_dt_bias"] = din("ssd_dt_bias", [2, 64])
    I["ssd_d"] = din("ssd_d", [2, 32])
    I["ssd_norm_g"] = din("ssd_norm_g", [2, 2048])
    I["ssd_w_out"] = din("ssd_w_out", [2, 2048, D])
    I["ssd_U"] = din("ssd_U", [6, 128, 128])
    NCH = NT // 128
    P.UT = dscr("UT", [KC, 128, NT], BF16)
    P.SZ = dscr("SZ", [NT, 2048])
    P.YP = dscr("YP", [NT, 2048])
    P.SBs = dscr("SBs", [NCH, 128, 2048])
    P.CT = dscr("CT", [NCH, 128, 512], BF16)
    P.EB = dscr("EB", [NCH, 128, 64])
    I["hy_w_in"] = din("hy_w_in", [1, D, 3072])
    I["hy_conv_w"] = din("hy_conv_w", [1, 3, 3072])
    I["hy_conv_b"] = din("hy_conv_b", [1, 3072])
    I["hy_filt_w1"] = din("hy_filt_w1", [1, 33, 64])
    I["hy_filt_w2"] = din("hy_filt_w2", [1, 64, 64])
    I["hy_filt_w3"] = din("hy_filt_w3", [1, 64, 4096])
    I["hy_filt_fb"] = din("hy_filt_fb", [1, 4, 128])
    I["hy_bias"] = din("hy_bias", [1, 2, D])
    I["hy_w_out"] = din("hy_w_out", [1, D, D])
    I["hy_F128"] = din("hy_F128", [4, 128, 128], BF16)
    I["hy_ndel"] = din("hy_ndel", [128, KC])
    I["hy_DF"] = din("hy_DF", [2, CTX, 2 * CTX], BF16)
    I["hy_DG"] = din("hy_DG", [2, 2 * CTX, CTX], BF16)
    P.HV = dscr("HV", [KC, 128, NT], BF16)
    P.HX1 = dscr("HX1", [KC, 128, NT])
    P.HX2 = dscr("HX2", [KC, 128, NT])
    P.HY = dscr("HY", [KC, 128, NT])
    for tag, Lf in (("c", CTX), ("l", L)):
        NA_ = 2 * Lf // 128
        NZ_ = Lf // 128
        dd = {}
        dd["FA"] = din("hy_FA" + tag, [2, NZ_, NA_], BF16)
        dd["TW"] = din("hy_TW" + tag, [2, 128, NA_])
        dd["IT"] = din("hy_IT" + tag, [2, NA_, 128])
        dd["CS"] = din("hy_CS" + tag, [3, NA_, NZ_], BF16)
        dd["zT"] = din("hy_zT" + tag, [33, Lf])
        dd["trow"] = din("hy_trow" + tag, [1, Lf])
        dd["KS"] = dscr("hy_KS" + tag, [2, 2, KC, 128, Lf], BF16)
        dd["RN"] = dscr("hy_RN" + tag, [2, D])
        dd["KF"] = dscr("hy_KF" + tag, [2, 2, 128, D, NA_], BF16)
        setattr(P, "hyc_" + tag, dd)
    P.U32 = dscr("U32", [NT, D])
    P.U16 = dscr("U16", [NT, D], BF16)
    out = nc.dram_tensor("out", [L, D], F32, kind="ExternalOutput").ap()
    X = dscr("X", [NT, D])
    Mrows = dscr("Mrows", [2, NMOD * D])
    P.I = I
    P.X = X
    P.Mrows = Mrows
    P.out = out

    with contextlib.ExitStack() as st0:
        S = Sched(nc, st0)
        P.S = S

        uid = [0]

        def sb(st, name, shape, dt=F32):
            uid[0] += 1
            nm = "s%d_%s" % (uid[0], name)
            return Buf(st.enter_context(nc.sbuf_tensor(nm, list(shape), dt)), nm)

        def ps(st, name, dt=F32, cols=512):
            uid[0] += 1
            nm = "p%d_%s" % (uid[0], name)
            return Buf(st.enter_context(nc.psum_tensor(nm, [128, cols], dt)), nm)

        P.sb = sb
        P.ps = ps
        ident = sb(st0, "ident", [128, 128])
        scT = sb(st0, "scT", [128, KC, 2])
        modc = sb(st0, "modc", [128, NMOD * KC, 2])
        gcol = sb(st0, "gcol", [128, 3 * KC])
        neghalf = sb(st0, "neghalf", [128, 16])
        P.ident, P.scT, P.modc, P.gcol, P.neghalf = ident, scT, modc, gcol, neghalf
        P.neghalf16 = neghalf
        P.Xb = [Buf(None, "X%d" % g) for g in range(NG)]
        P.Mb = Buf(None, "Mrows")

        S.dma("sp", ident[:], I["ident"][:, :], writes=[ident])
        S.op("dve", lambda e: e.memset(neghalf[:], -0.5), writes=[neghalf])

        with contextlib.ExitStack() as st:
            crow = sb(st, "crow", [8, 2, 128])
            pT = ps(st, "pT")
            S.dma("sp", crow[:, 0, :], I["c"][:, :], writes=[crow])
            S.dma("sp", crow[:, 1, :], I["c_ctx"][:, :], writes=[crow])
            for j in range(2):
                S.op("pe", lambda e, j=j: e.transpose(pT[:, j * 8:(j + 1) * 8], crow[:, j, :], ident[0:8, 0:8]),
                     reads=[crow, ident], writes=[pT])
            for j in range(2):
                S.op("act", lambda e, j=j: e.activation(out=scT[:, :, j], in_=pT[:, j * 8:(j + 1) * 8], func=AF.Silu),
                     reads=[pT], writes=[scT])
            S.barrier()

        for i in range(layers):
            kind = i % 3
            jj = i // 3
            last = i == DEPTH - 1
            adaln_phase(P, i)
            ffn_phase(P, i, 0, first=(i == 0), do_ctx=True)
            if kind == 0 and 0 in mixers:
                ssd_phase(P, i, jj, last)
            if kind == 1 and 1 in mixers:
                pool_phase(P, i, jj)
            if kind == 2 and 2 in mixers:
                hyena_phase(P, i, jj)
            ffn_phase(P, i, 1, first=False, do_ctx=not last)
        final_phase(P)
        S.barrier()
        with nc.Block() as block:
            S.emit(block)
    return nc


def adaln_phase(P, i):
    nc, S, I = P.nc, P.S, P.I
    ident, scT, modc, gcol = P.ident, P.scT, P.modc, P.gcol
    with contextlib.ExitStack() as st:
        brow = P.sb(st, "brow", [72, 128])
        grow = P.sb(st, "grow", [24, 128])
        bcol = P.sb(st, "bcol", [128, 72])
        wbuf = [P.sb(st, "adaw%d" % k, [128, KC, 1024]) for k in range(2)]
        mrow = P.sb(st, "mrow", [72, 2, 128])
        pT = P.ps(st, "pTa")
        pM = P.ps(st, "pMa")
        pR = P.ps(st, "pRa")
        S.dma("sp", brow[:], I["ada_b"][i, :, :], writes=[brow])
        S.dma("sp", grow[:], I["norm_g"][i, :, :], writes=[grow])
        S.op("pe", lambda e: e.transpose(pT[:, 0:72], brow[:], ident[0:72, 0:72]), reads=[brow, ident], writes=[pT])
        S.op("pe", lambda e: e.transpose(pT[:, 128:152], grow[:], ident[0:24, 0:24]), reads=[grow, ident], writes=[pT])
        S.op("dve", lambda e: e.tensor_copy(out=bcol[:], in_=pT[:, 0:72]), reads=[pT], writes=[bcol])
        S.op("dve", lambda e: e.tensor_copy(out=gcol[:], in_=pT[:, 128:152]), reads=[pT], writes=[gcol])
        for m in range(NMOD):
            wb = wbuf[m % 2]
            S.dma("sp", wb[:], I["ada_w"][i, :, m * 1024:(m + 1) * 1024].rearrange("(kc p) n -> p kc n", p=128),
                  writes=[wb])
            for cc in range(KC):
                col = m * KC + cc
                for kc in range(KC):
                    S.op("pe", lambda e, wb=wb, cc=cc, kc=kc, col=col: e.matmul(
                        pM[:, col * 2:col * 2 + 2], lhsT=wb[:, kc, cc * 128:(cc + 1) * 128], rhs=scT[:, kc, :],
                        start=(kc == 0), stop=(kc == KC - 1)), reads=[wb, scT], writes=[pM])
        for j in range(2):
            S.op("dve", lambda e, j=j: e.tensor_tensor(
                out=modc[:, :, j], in0=pM[:, :].rearrange("p (c j) -> p c j", j=2)[:, 0:72, j], in1=bcol[:], op=ALU.add),
                reads=[pM, bcol], writes=[modc])
        for j in range(2):
            S.op("pe", lambda e, j=j: e.transpose(pR[0:72, j * 128:(j + 1) * 128], modc[:, :, j], ident[:, :]),
                 reads=[modc, ident], writes=[pR])
        S.op("dve", lambda e: e.tensor_copy(out=mrow[:, :, :], in_=pR[0:72, 0:256].rearrange("p (j c) -> p j c", j=2)),
             reads=[pR], writes=[mrow])
        for j in range(2):
            S.dma("sp", P.Mrows[j, :].rearrange("(c p) -> c p", p=128), mrow[:, j, :], reads=[mrow], writes=[P.Mb])
        S.barrier()


def rstd_ops(P, xt, nsub, stats, mv, t1, t2, rstd):
    S = P.S
    for s in range(nsub):
        for hh in range(2):
            S.op("dve", lambda e, s=s, hh=hh: e.bn_stats(out=stats[:, s, hh, :], in_=xt[:, s, hh * 512:(hh + 1) * 512]),
                 reads=[xt], writes=[stats])
        S.op("dve", lambda e, s=s: e.bn_aggr(out=mv[:, s, :], in_=stats[:, s, :, :].rearrange("p a b -> p (a b)")),
             reads=[stats], writes=[mv])
    S.op("dve", lambda e: e.tensor_tensor(out=t1[:, 0:nsub], in0=mv[:, 0:nsub, 0], in1=mv[:, 0:nsub, 0], op=ALU.mult),
         reads=[mv], writes=[t1])
    S.op("dve", lambda e: e.scalar_tensor_tensor(out=t2[:, 0:nsub], in0=t1[:, 0:nsub], scalar=EPS, in1=mv[:, 0:nsub, 1],
                                                 op0=ALU.add, op1=ALU.add), reads=[t1, mv], writes=[t2])
    S.op("pool", lambda e: e.tensor_tensor(out=rstd[:, 0:nsub], in0=t2[:, 0:nsub], in1=P.neghalf[:, 0:nsub], op=ALU.pow),
         reads=[t2, P.neghalf], writes=[rstd])


def group_src(P, g, first):
    if first:
        if g == 0:
            return P.I["ctx"][:, :].rearrange("(s p) d -> p s d", p=128)
        return P.I["x"][(g - 1) * 256:g * 256, :].rearrange("(s p) d -> p s d", p=128)
    return P.X[g * 256:(g + 1) * 256, :].rearrange("(s p) d -> p s d", p=128)


def ffn_phase(P, i, which, first, do_ctx):
    nc, S, I = P.nc, P.S, P.I
    ident, modc, gcol = P.ident, P.modc, P.gcol
    nidx = 0 if which == 0 else 2
    m_shift, m_scale, m_gate = (0, 1, 2) if which == 0 else (6, 7, 8)
    with contextlib.ExitStack() as st:
        sb, ps = P.sb, P.ps
        wg = sb(st, "wg", [128, KC, DFF], BF16)
        wu = sb(st, "wu", [128, KC, DFF], BF16)
        wd = sb(st, "wd", [128, FC, D], BF16)
        stg = [sb(st, "stg%d" % k, [128, 1408]) for k in range(2)]
        xb = [sb(st, "xt%d" % k, [128, 2, D]) for k in range(2)]
        xn = sb(st, "xn", [128, 2, D])
        uT = [sb(st, "uT%d" % k, [128, KC, 256], BF16) for k in range(2)]
        hT = sb(st, "hT", [128, FC, 256], BF16)
        sg = [sb(st, "sg%d" % k, [128, 256]) for k in range(2)]
        ep = [sb(st, "ep%d" % k, [128, 512]) for k in range(2)]
        gbc = sb(st, "gbc", [128, D])
        acol = sb(st, "acol", [128, KC, 2])
        stats = sb(st, "stats", [128, 2, 2, 6])
        mv = sb(st, "mv", [128, 2, 2])
        t1 = sb(st, "t1", [128, 2])
        t2 = sb(st, "t2", [128, 2])
        rstd = sb(st, "rstd", [128, 2])
        psT = [ps(st, "psT%d" % k) for k in range(2)]
        psGU = [ps(st, "psGU%d" % k) for k in range(2)]
        psD = [ps(st, "psD%d" % k) for k in range(4)]

        for j in range(2):
            S.op("dve", lambda e, j=j: e.scalar_tensor_tensor(
                out=acol[:, :, j], in0=modc[:, m_scale * KC:(m_scale + 1) * KC, j], scalar=1.0,
                in1=gcol[:, nidx * KC:(nidx + 1) * KC], op0=ALU.add, op1=ALU.mult), reads=[modc, gcol], writes=[acol])

        conv_engs = ["pool", "dve", "act"]
        k = 0
        for (wsrc, wdst) in ((I["ffn_w_gate"], wg), (I["ffn_w_up"], wu)):
            for kc in range(KC):
                for hh in range(2):
                    sg_ = stg[k % 2]
                    S.dma("sp", sg_[:, :], wsrc[i, which, kc * 128:(kc + 1) * 128, hh * 1408:(hh + 1) * 1408], writes=[sg_])
                    eng = conv_engs[k % 3]
                    dst = wdst
                    if eng == "act":
                        S.op(eng, lambda e, sg_=sg_, dst=dst, kc=kc, hh=hh: e.copy(
                            out=dst[:, kc, hh * 1408:(hh + 1) * 1408], in_=sg_[:, :]), reads=[sg_], writes=[dst])
                    else:
                        S.op(eng, lambda e, sg_=sg_, dst=dst, kc=kc, hh=hh: e.tensor_copy(
                            out=dst[:, kc, hh * 1408:(hh + 1) * 1408], in_=sg_[:, :]), reads=[sg_], writes=[dst])
                    k += 1
        for fc in range(FC):
            sg_ = stg[k % 2]
            S.dma("sp", sg_[:, 0:D], I["ffn_w_down"][i, which, fc * 128:(fc + 1) * 128, :], writes=[sg_])
            eng = conv_engs[k % 3]
            if eng == "act":
                S.op(eng, lambda e, sg_=sg_, fc=fc: e.copy(out=wd[:, fc, :], in_=sg_[:, 0:D]), reads=[sg_], writes=[wd])
            else:
                S.op(eng, lambda e, sg_=sg_, fc=fc: e.tensor_copy(out=wd[:, fc, :], in_=sg_[:, 0:D]), reads=[sg_], writes=[wd])
            k += 1

        groups = list(range(P.NG)) if do_ctx else list(range(1, P.NG))

        def prep(gi, g):
            j = 1 if g == 0 else 0
            xt = xb[gi % 2]
            S.dma("sp", xt[:], group_src(P, g, first), reads=[P.Xb[g]], writes=[xt])
            rstd_ops(P, xt, 2, stats, mv, t1, t2, rstd)
            for s in range(2):
                S.op("pool", lambda e, s=s, xt=xt: e.tensor_scalar(
                    out=xn[:, s, :], in0=xt[:, s, :], scalar1=rstd[:, s:s + 1], scalar2=0.0, op0=ALU.mult, op1=ALU.add),
                    reads=[xt, rstd], writes=[xn])
            u = uT[gi % 2]
            for kc in range(KC):
                pb = psT[(kc // 2) % 2]
                for s in range(2):
                    c0 = (kc % 2) * 256 + s * 128
                    S.op("pe", lambda e, pb=pb, c0=c0, s=s, kc=kc: e.transpose(
                        pb[:, c0:c0 + 128], xn[:, s, kc * 128:(kc + 1) * 128], ident[:, :]), reads=[xn, ident], writes=[pb])
                c1 = (kc % 2) * 256
                S.op("act", lambda e, pb=pb, c1=c1, kc=kc, u=u, j=j: e.activation(
                    out=u[:, kc, :], in_=pb[:, c1:c1 + 256], func=AF.Identity,
                    scale=acol[:, kc, j:j + 1], bias=modc[:, m_shift * KC + kc, j:j + 1]),
                    reads=[pb, acol, modc], writes=[u])

        cur_j = None
        prep(0, groups[0])
        for gi, g in enumerate(groups):
            j = 1 if g == 0 else 0
            if j != cur_j:
                S.dma("sp", gbc[:, :], P.Mrows[j, m_gate * D:(m_gate + 1) * D].partition_broadcast(128),
                      reads=[P.Mb], writes=[gbc])
                cur_j = j
            xt = xb[gi % 2]
            u = uT[gi % 2]
            for fc in range(FC):
                pb = psGU[fc % 2]
                for (w_, c0) in ((wg, 0), (wu, 256)):
                    for kc in range(KC):
                        S.op("pe", lambda e, pb=pb, w_=w_, c0=c0, kc=kc, fc=fc, u=u: e.matmul(
                            pb[:, c0:c0 + 256], lhsT=w_[:, kc, fc * 128:(fc + 1) * 128], rhs=u[:, kc, :],
                            start=(kc == 0), stop=(kc == KC - 1)), reads=[w_, u], writes=[pb])
                sgb = sg[fc % 2]
                S.op("act", lambda e, pb=pb, sgb=sgb: e.activation(out=sgb[:, :], in_=pb[:, 0:256], func=AF.Silu),
                     reads=[pb], writes=[sgb])
                S.op("dve", lambda e, pb=pb, sgb=sgb, fc=fc: e.tensor_tensor(
                    out=hT[:, fc, :], in0=sgb[:, :], in1=pb[:, 256:512], op=ALU.mult), reads=[pb, sgb], writes=[hT])
            if gi + 1 < len(groups):
                prep(gi + 1, groups[gi + 1])
            for s in range(2):
                for hh in range(2):
                    pb = psD[s * 2 + hh]
                    for fc in range(FC):
                        S.op("pe", lambda e, pb=pb, s=s, hh=hh, fc=fc: e.matmul(
                            pb[:, :], lhsT=hT[:, fc, s * 128:(s + 1) * 128], rhs=wd[:, fc, hh * 512:(hh + 1) * 512],
                            start=(fc == 0), stop=(fc == FC - 1)), reads=[hT, wd], writes=[pb])
                    epb = ep[hh]
                    S.op("dve", lambda e, pb=pb, epb=epb, hh=hh: e.scalar_tensor_tensor(
                        out=epb[:, :], in0=pb[:, :], scalar=0.5, in1=gbc[:, hh * 512:(hh + 1) * 512],
                        op0=ALU.mult, op1=ALU.mult), reads=[pb, gbc], writes=[epb])
                    S.op("pool", lambda e, epb=epb, xt=xt, s=s, hh=hh: e.tensor_tensor(
                        out=xt[:, s, hh * 512:(hh + 1) * 512], in0=epb[:, :], in1=xt[:, s, hh * 512:(hh + 1) * 512],
                        op=ALU.add), reads=[epb, xt], writes=[xt])
            S.dma("pool", P.X[g * 256:(g + 1) * 256, :].rearrange("(s p) d -> p s d", p=128), xt[:],
                  reads=[xt], writes=[P.Xb[g]])
        S.barrier()


def final_phase(P):
    nc, S, I = P.nc, P.S, P.I
    with contextlib.ExitStack() as st:
        sb = P.sb
        xb = [sb(st, "fx%d" % k, [128, 2, D]) for k in range(2)]
        yb = [sb(st, "fy%d" % k, [128, 2, D]) for k in range(2)]
        gbc = sb(st, "fg", [128, D])
        stats = sb(st, "fstats", [128, 2, 2, 6])
        mv = sb(st, "fmv", [128, 2, 2])
        t1 = sb(st, "ft1", [128, 2])
        t2 = sb(st, "ft2", [128, 2])
        rstd = sb(st, "frstd", [128, 2])
        S.dma("sp", gbc[:, :], I["final_g"][0, :].partition_broadcast(128), writes=[gbc])
        glist = list(range(1, P.NG))
        if getattr(P, "dbg_ctx", False):
            glist = [0] + glist[1:]
        for g in glist:
            xt = xb[g % 2]
            yt = yb[g % 2]
            S.dma("sp", xt[:], group_src(P, g, False), reads=[P.Xb[g]], writes=[xt])
            rstd_ops(P, xt, 2, stats, mv, t1, t2, rstd)
            for s in range(2):
                S.op("dve", lambda e, s=s, xt=xt, yt=yt: e.scalar_tensor_tensor(
                    out=yt[:, s, :], in0=xt[:, s, :], scalar=rstd[:, s:s + 1], in1=gbc[:, :], op0=ALU.mult, op1=ALU.mult),
                    reads=[xt, rstd, gbc], writes=[yt])
            go = max(g - 1, 0)
            S.dma("pool", P.out[go * 256:(go + 1) * 256, :].rearrange("(s p) d -> p s d", p=128), yt[:], reads=[yt])
        S.barrier()


_CACHE = {}


def host_inputs(inputs, b, L, shared):
    f = np.float32
    m = dict(shared)
    m["x"] = np.ascontiguousarray(inputs["x"][b, :L], dtype=f)
    m["ctx"] = np.ascontiguousarray(inputs["ctx"][b], dtype=f)
    m["c"] = np.ascontiguousarray(inputs["c"][b].reshape(KC, 128), dtype=f)
    return m


def shared_inputs(inputs, L):
    f = np.float32
    m = {}
    m["c_ctx"] = np.ascontiguousarray(inputs["c_ctx"].reshape(KC, 128), dtype=f)
    m["ada_w"] = np.ascontiguousarray(inputs["ada_w"], dtype=f)
    m["ada_b"] = np.ascontiguousarray(inputs["ada_b"].reshape(DEPTH, NMOD * KC, 128), dtype=f)
    m["norm_g"] = np.ascontiguousarray(inputs["norm_g"].reshape(DEPTH, 3 * KC, 128), dtype=f)
    m["norm_g_rows"] = np.ascontiguousarray(inputs["norm_g"], dtype=f)
    m["ffn_w_gate"] = np.ascontiguousarray(inputs["ffn_w_gate"], dtype=f)
    m["ffn_w_up"] = np.ascontiguousarray(inputs["ffn_w_up"], dtype=f)
    m["ffn_w_down"] = np.ascontiguousarray(inputs["ffn_w_down"], dtype=f)
    m["final_g"] = np.ascontiguousarray(inputs["final_g"].reshape(1, D), dtype=f)
    m["ident"] = np.eye(128, dtype=f)
    pm, pinv, pcm, pcinv = pool_consts(L)
    m["poolP"], m["pool_inv"], m["poolPc"], m["pool_invc"] = pm, pinv, pcm, pcinv
    m["pool_w"] = np.ascontiguousarray(inputs["pool_w"], dtype=f)
    m["pool_b"] = np.ascontiguousarray(inputs["pool_b"].reshape(1, D), dtype=f)
    m["pool_scale"] = np.ascontiguousarray(inputs["pool_scale"].reshape(1, D), dtype=f)
    m["ssd_w_in"] = np.ascontiguousarray(inputs["ssd_w_in"], dtype=f)
    m["ssd_conv_w"] = np.ascontiguousarray(inputs["ssd_conv_w"], dtype=f)
    m["ssd_conv_b"] = np.ascontiguousarray(inputs["ssd_conv_b"], dtype=f)
    m["ssd_a_log"] = np.ascontiguousarray(inputs["ssd_a_log"].reshape(2, 64), dtype=f)
    m["ssd_dt_bias"] = np.ascontiguousarray(inputs["ssd_dt_bias"].reshape(2, 64), dtype=f)
    m["ssd_d"] = np.ascontiguousarray(inputs["ssd_d"], dtype=f)
    m["ssd_norm_g"] = np.ascontiguousarray(inputs["ssd_norm_g"], dtype=f)
    m["ssd_w_out"] = np.ascontiguousarray(inputs["ssd_w_out"], dtype=f)
    m["ssd_U"] = ssd_consts()
    m["hy_w_in"] = np.ascontiguousarray(inputs["hy_w_in"], dtype=f)
    m["hy_conv_w"] = np.ascontiguousarray(inputs["hy_conv_w"], dtype=f)
    m["hy_conv_b"] = np.ascontiguousarray(inputs["hy_conv_b"], dtype=f)
    m["hy_filt_w1"] = np.ascontiguousarray(inputs["hy_filt_w1"], dtype=f)
    m["hy_filt_w2"] = np.ascontiguousarray(inputs["hy_filt_w2"], dtype=f)
    m["hy_filt_w3"] = np.ascontiguousarray(inputs["hy_filt_w3"], dtype=f)
    fbm = np.zeros((1, 4, 128), f)
    fbm[0, 0, :64] = inputs["hy_filt_freq"][0, 0]
    fbm[0, 1, :64] = inputs["hy_filt_freq"][0, 1]
    fbm[0, 2, :64] = inputs["hy_filt_b1"][0]
    fbm[0, 3, :64] = inputs["hy_filt_b2"][0]
    m["hy_filt_fb"] = fbm
    m["hy_bias"] = np.ascontiguousarray(inputs["hy_bias"], dtype=f)
    m["hy_w_out"] = np.ascontiguousarray(inputs["hy_w_out"], dtype=f)
    F128, ndel = hy_shared_consts()
    m["hy_F128"], m["hy_ndel"] = F128, ndel
    m["hy_DF"], m["hy_DG"] = hy_dense_consts()
    for tag, Lf in (("c", CTX), ("l", L)):
        hc = hy_consts(Lf)
        for k_ in ("FA", "TW", "IT", "CS", "zT", "trow"):
            m["hy_" + k_ + tag] = hc[k_]
    return m


def run(inputs, L, layers=DEPTH, mixers=(0, 1, 2), trace=False, ncores=8, dbg_ctx=False):
    mixers = tuple(mixers)
    key = (L, layers, mixers, dbg_ctx)
    if key not in _CACHE:
        _CACHE[key] = build_program(L, layers, mixers, dbg_ctx)
    nc = _CACHE[key]
    B = inputs["x"].shape[0]
    shared = shared_inputs(inputs, L)
    in_maps = [host_inputs(inputs, c % B, L, shared) for c in range(ncores)]
    res = run_bass_kernel_spmd(nc, in_maps, core_ids=list(range(ncores)))
    out = np.stack([np.asarray(res.results[b]["out"]) for b in range(min(B, ncores))], axis=0)
    return out.astype(np.float32)


def kernel(**inputs):
    return run(inputs, inputs["x"].shape[1])


POOL_W = (2, 4, 8, 16)
POOL_RANGES = ((-1, 0), (-1, 1), (-2, 2), (-4, 4))
POOL_BASE = (0, 2, 5, 10)


def pool_consts(L):
    import ml_dtypes
    rows = L // GRID_W
    NK = L // 128
    p = np.arange(128)
    rl, cc = p // 64, p % 64
    mats = np.zeros((19, 128, 128), np.float32)
    for wi, w in enumerate(POOL_W):
        lo, hi = POOL_RANGES[wi]
        for dl in range(lo, hi + 1):
            dr = 2 * dl + rl[:, None] - rl[None, :]
            dc = cc[:, None] - cc[None, :]
            ok = (dr >= -(w // 2)) & (dr <= w // 2 - 1) & (dc >= -(w // 2)) & (dc <= w // 2 - 1)
            mats[POOL_BASE[wi] + dl - lo] = ok
    inv = np.zeros((128, NK, 4), np.float32)
    for wi, w in enumerate(POOL_W):
        for k in range(NK):
            r = 2 * k + rl
            cr = np.minimum(r + w - w // 2, rows) - np.maximum(r - w // 2, 0)
            ccnt = np.minimum(cc + w - w // 2, GRID_W) - np.maximum(cc - w // 2, 0)
            inv[:, k, wi] = 1.0 / (cr * ccnt).astype(np.float32)
    cm = np.zeros((16, 128, 128), np.float32)
    cinv = np.zeros((128, 2, 4), np.float32)
    for wi, w in enumerate(POOL_W):
        for kin in range(2):
            for ko in range(2):
                tin = kin * 128 + p[:, None]
                to = ko * 128 + p[None, :]
                dt_ = tin - to
                cm[wi * 4 + kin * 2 + ko] = (dt_ >= -(w // 2)) & (dt_ <= w // 2 - 1)
        for ko in range(2):
            t = ko * 128 + p
            cnt = np.minimum(t + w - w // 2, CTX) - np.maximum(t - w // 2, 0)
            cinv[:, ko, wi] = 1.0 / cnt.astype(np.float32)
    bf = ml_dtypes.bfloat16
    return mats.astype(bf), inv, cm.astype(bf), cinv


def mixer_u_pass(P, i, st, U32, U16, Ub):
    S, I = P.S, P.I
    sb = P.sb
    abc = sb(st, "abc", [128, D])
    bbc = sb(st, "bbc", [128, D])
    gb = sb(st, "gb", [128, D])
    xb = [sb(st, "ux%d" % k, [128, 2, D]) for k in range(2)]
    ub = [sb(st, "uu%d" % k, [128, 2, D]) for k in range(2)]
    u16 = [sb(st, "uh%d" % k, [128, 2, D], BF16) for k in range(2)]
    stats = sb(st, "ustats", [128, 2, 2, 6])
    mv = sb(st, "umv", [128, 2, 2])
    t1 = sb(st, "ut1", [128, 2])
    t2 = sb(st, "ut2", [128, 2])
    rstd = sb(st, "urstd", [128, 2])
    S.dma("sp", gb[:, :], I["norm_g_rows"][i, 1, :].partition_broadcast(128), writes=[gb])
    cur_j = None
    for g in range(P.NG):
        j = 1 if g == 0 else 0
        if j != cur_j:
            S.dma("sp", abc[:, :], P.Mrows[j, 4 * D:5 * D].partition_broadcast(128), reads=[P.Mb], writes=[abc])
            S.dma("sp", bbc[:, :], P.Mrows[j, 3 * D:4 * D].partition_broadcast(128), reads=[P.Mb], writes=[bbc])
            S.op("pool", lambda e: e.tensor_scalar(
                out=abc[:, :], in0=abc[:, :], scalar1=1.0, scalar2=1.0, op0=ALU.add, op1=ALU.mult),
                reads=[abc], writes=[abc])
            S.op("pool", lambda e: e.tensor_tensor(out=abc[:, :], in0=abc[:, :], in1=gb[:, :], op=ALU.mult),
                 reads=[abc, gb], writes=[abc])
            cur_j = j
        xt, ut, uh = xb[g % 2], ub[g % 2], u16[g % 2]
        S.dma("sp", xt[:], group_src(P, g, False), reads=[P.Xb[g]], writes=[xt])
        rstd_ops(P, xt, 2, stats, mv, t1, t2, rstd)
        for s in range(2):
            S.op("dve", lambda e, s=s, xt=xt, ut=ut: e.scalar_tensor_tensor(
                out=ut[:, s, :], in0=xt[:, s, :], scalar=rstd[:, s:s + 1], in1=abc[:, :], op0=ALU.mult, op1=ALU.mult),
                reads=[xt, rstd, abc], writes=[ut])
            S.op("pool", lambda e, s=s, ut=ut: e.tensor_tensor(out=ut[:, s, :], in0=ut[:, s, :], in1=bbc[:, :], op=ALU.add),
                 reads=[ut, bbc], writes=[ut])
            if U16 is not None:
                S.op("act", lambda e, s=s, ut=ut, uh=uh: e.copy(out=uh[:, s, :], in_=ut[:, s, :]), reads=[ut], writes=[uh])
        if U32 is not None:
            S.dma("pool", U32[g * 256:(g + 1) * 256, :].rearrange("(s p) d -> p s d", p=128), ut[:], reads=[ut], writes=[Ub[g]])
        if U16 is not None:
            S.dma("pool", U16[g * 256:(g + 1) * 256, :].rearrange("(s p) d -> p s d", p=128), uh[:], reads=[uh], writes=[Ub[g]])


def pool_phase(P, i, jj):
    nc, S, I = P.nc, P.S, P.I
    ident = P.ident
    NK = P.L // 128
    Ub = [Buf(None, "U%d" % g) for g in range(P.NG)]
    with contextlib.ExitStack() as st:
        mixer_u_pass(P, i, st, P.U32, P.U16, Ub)
        S.barrier()
    with contextlib.ExitStack() as st:
        sb, ps = P.sb, P.ps
        NR = 10
        ring = [sb(st, "ring%d" % k, [128, D], BF16) for k in range(NR)]
        u32 = [sb(st, "pu%d" % k, [128, D]) for k in range(2)]
        xb = [sb(st, "px%d" % k, [128, D]) for k in range(2)]
        pooled = sb(st, "pooled", [128, D])
        pT16 = sb(st, "pT16", [128, KC, 128], BF16)
        ep = [sb(st, "pep%d" % k, [128, 512]) for k in range(2)]
        Pm = sb(st, "Pm", [128, 19, 128], BF16)
        Pc = sb(st, "Pc", [128, 16, 128], BF16)
        invl = sb(st, "invl", [128, NK, 4])
        invc = sb(st, "invc", [128, 2, 4])
        pw32 = sb(st, "pw32", [128, 4, 2, 256])
        pw = sb(st, "pw", [128, 4, 2, 256], BF16)
        sbc = sb(st, "sbc", [128, D])
        bsbc = sb(st, "bsbc", [128, D])
        psc = sb(st, "psc", [128, D])
        pP = [ps(st, "pP%d" % k) for k in range(2)]
        psT = [ps(st, "ppT%d" % k) for k in range(2)]
        psY = [ps(st, "pY%d" % k) for k in range(2)]
        S.dma("sp", Pm[:], I["poolP"][:, :, :].rearrange("m p q -> p m q"), writes=[Pm])
        S.dma("sp", Pc[:], I["poolPc"][:, :, :].rearrange("m p q -> p m q"), writes=[Pc])
        S.dma("sp", invl[:], I["pool_inv"][:, :, :], writes=[invl])
        S.dma("sp", invc[:], I["pool_invc"][:, :, :], writes=[invc])
        S.dma("sp", pw32[:], I["pool_w"][jj, :, :, :].rearrange("g (cc p) d -> p g cc d", p=128), writes=[pw32])
        S.op("dve", lambda e: e.tensor_copy(out=pw[:], in_=pw32[:]), reads=[pw32], writes=[pw])
        S.dma("sp", psc[:, :], I["pool_scale"][jj, :].partition_broadcast(128), writes=[psc])

        def run_tiles(ntiles, row0, j, mats_of, inv_t, Mt):
            S.dma("sp", sbc[:, :], P.Mrows[j, 5 * D:6 * D].partition_broadcast(128), reads=[P.Mb], writes=[sbc])
            S.dma("sp", bsbc[:, :], I["pool_b"][jj, :].partition_broadcast(128), writes=[bsbc])
            S.op("pool", lambda e: e.tensor_tensor(out=sbc[:, :], in0=sbc[:, :], in1=psc[:, :], op=ALU.mult),
                 reads=[sbc, psc], writes=[sbc])
            S.op("pool", lambda e: e.tensor_tensor(out=bsbc[:, :], in0=bsbc[:, :], in1=sbc[:, :], op=ALU.mult),
                 reads=[sbc, bsbc], writes=[bsbc])
            loaded = set()

            def ensure(t):
                if t in loaded or t < 0 or t >= ntiles:
                    return
                slot = ring[t % NR]
                r0 = row0 + t * 128
                S.dma("sp", slot[:, :], P.U16[r0:r0 + 128, :], reads=[Ub[r0 // 256]], writes=[slot])
                loaded.add(t)

            for k in range(ntiles):
                for t in range(k - 4, k + 5):
                    ensure(t)
                r0 = row0 + k * 128
                ut, xt = u32[k % 2], xb[k % 2]
                S.dma("sp", ut[:, :], P.U32[r0:r0 + 128, :], reads=[Ub[r0 // 256]], writes=[ut])
                S.dma("sp", xt[:, :], P.X[r0:r0 + 128, :], reads=[P.Xb[r0 // 256]], writes=[xt])
                for gi in range(4):
                    pb = pP[gi // 2]
                    c0 = (gi % 2) * 256
                    lst = [(mi, t) for (mi, t) in mats_of(k, gi) if 0 <= t < ntiles]
                    for n, (mi, t) in enumerate(lst):
                        slot = ring[t % NR]
                        S.op("pe", lambda e, pb=pb, c0=c0, mi=mi, slot=slot, gi=gi, n=n, ln=len(lst): e.matmul(
                            pb[:, c0:c0 + 256], lhsT=Mt[:, mi, :], rhs=slot[:, gi * 256:(gi + 1) * 256],
                            start=(n == 0), stop=(n == ln - 1)), reads=[Mt, slot], writes=[pb])
                    S.op("dve", lambda e, pb=pb, c0=c0, gi=gi, k=k, ut=ut: e.scalar_tensor_tensor(
                        out=pooled[:, gi * 256:(gi + 1) * 256], in0=pb[:, c0:c0 + 256], scalar=inv_t[:, k, gi:gi + 1],
                        in1=ut[:, gi * 256:(gi + 1) * 256], op0=ALU.mult, op1=ALU.subtract),
                        reads=[pb, inv_t, ut], writes=[pooled])
                for cb in range(KC):
                    pb = psT[cb // 4]
                    S.op("pe", lambda e, pb=pb, cb=cb: e.transpose(
                        pb[:, (cb % 4) * 128:(cb % 4 + 1) * 128], pooled[:, cb * 128:(cb + 1) * 128], ident[:, :]),
                        reads=[pooled, ident], writes=[pb])
                for hh in range(2):
                    S.op("act", lambda e, hh=hh: e.copy(
                        out=pT16[:, hh * 4:(hh + 1) * 4, :], in_=psT[hh][:, :].rearrange("p (a b) -> p a b", a=4)),
                        reads=[psT[hh]], writes=[pT16])
                for gi in range(4):
                    pb = psY[gi // 2]
                    c0 = (gi % 2) * 256
                    for cc in range(2):
                        S.op("pe", lambda e, pb=pb, c0=c0, gi=gi, cc=cc: e.matmul(
                            pb[:, c0:c0 + 256], lhsT=pT16[:, gi * 2 + cc, :], rhs=pw[:, gi, cc, :],
                            start=(cc == 0), stop=(cc == 1)), reads=[pT16, pw], writes=[pb])
                for hh in range(2):
                    epb = ep[hh]
                    S.op("dve", lambda e, hh=hh, epb=epb: e.tensor_tensor(
                        out=epb[:, :], in0=psY[hh][:, :], in1=sbc[:, hh * 512:(hh + 1) * 512], op=ALU.mult),
                        reads=[psY[hh], sbc], writes=[epb])
                    S.op("pool", lambda e, hh=hh, epb=epb, xt=xt: e.tensor_tensor(
                        out=xt[:, hh * 512:(hh + 1) * 512], in0=epb[:, :], in1=xt[:, hh * 512:(hh + 1) * 512], op=ALU.add),
                        reads=[epb, xt], writes=[xt])
                S.op("pool", lambda e, xt=xt: e.tensor_tensor(out=xt[:, :], in0=xt[:, :], in1=bsbc[:, :], op=ALU.add),
                     reads=[xt, bsbc], writes=[xt])
                S.dma("pool", P.X[r0:r0 + 128, :], xt[:, :], reads=[xt], writes=[P.Xb[r0 // 256]])

        def mats_ctx(k, gi):
            return [(gi * 4 + kin * 2 + k, kin) for kin in range(2)]

        def mats_lat(k, gi):
            lo, hi = POOL_RANGES[gi]
            return [(POOL_BASE[gi] + dl - lo, k + dl) for dl in range(lo, hi + 1)]

        run_tiles(2, 0, 1, mats_ctx, invc, Pc)
        S.barrier()
        run_tiles(NK, CTX, 0, mats_lat, invl, Pm)
        S.barrier()


SSD_H = 32
SSD_IN = 5184


def ssd_consts():
    j = np.arange(128)
    f = np.float32
    c = np.zeros((6, 128, 128), f)
    c[0] = (j[:, None] <= j[None, :])
    c[1] = (j[:, None] >= j[None, :])
    c[2] = (j[:, None] > j[None, :])
    c[3] = (j[:, None] < j[None, :])
    c[4] = 1.0
    return c


def mixer_uT_pass(P, i, st, UT, UTb):
    S, I = P.S, P.I
    sb, ps = P.sb, P.ps
    ident, modc, gcol = P.ident, P.modc, P.gcol
    xb = [sb(st, "tx%d" % k, [128, 2, D]) for k in range(2)]
    xn = sb(st, "txn", [128, 2, D])
    uT = [sb(st, "tuT%d" % k, [128, KC, 256], BF16) for k in range(2)]
    acol = sb(st, "tacol", [128, KC, 2])
    stats = sb(st, "tstats", [128, 2, 2, 6])
    mv = sb(st, "tmv", [128, 2, 2])
    t1 = sb(st, "tt1", [128, 2])
    t2 = sb(st, "tt2", [128, 2])
    rstd = sb(st, "trstd", [128, 2])
    psT = [ps(st, "tpsT%d" % k) for k in range(2)]
    for j in range(2):
        S.op("dve", lambda e, j=j: e.scalar_tensor_tensor(
            out=acol[:, :, j], in0=modc[:, 4 * KC:5 * KC, j], scalar=1.0, in1=gcol[:, KC:2 * KC],
            op0=ALU.add, op1=ALU.mult), reads=[modc, gcol], writes=[acol])
    for g in range(P.NG):
        j = 1 if g == 0 else 0
        xt = xb[g % 2]
        S.dma("sp", xt[:], group_src(P, g, False), reads=[P.Xb[g]], writes=[xt])
        rstd_ops(P, xt, 2, stats, mv, t1, t2, rstd)
        for s in range(2):
            S.op("pool", lambda e, s=s, xt=xt: e.tensor_scalar(
                out=xn[:, s, :], in0=xt[:, s, :], scalar1=rstd[:, s:s + 1], scalar2=0.0, op0=ALU.mult, op1=ALU.add),
                reads=[xt, rstd], writes=[xn])
        u = uT[g % 2]
        for kc in range(KC):
            pb = psT[(kc // 2) % 2]
            for s in range(2):
                c0 = (kc % 2) * 256 + s * 128
                S.op("pe", lambda e, pb=pb, c0=c0, s=s, kc=kc: e.transpose(
                    pb[:, c0:c0 + 128], xn[:, s, kc * 128:(kc + 1) * 128], ident[:, :]), reads=[xn, ident], writes=[pb])
            c1 = (kc % 2) * 256
            S.op("act", lambda e, pb=pb, c1=c1, kc=kc, u=u, j=j: e.activation(
                out=u[:, kc, :], in_=pb[:, c1:c1 + 256], func=AF.Identity,
                scale=acol[:, kc, j:j + 1], bias=modc[:, 3 * KC + kc, j:j + 1]),
                reads=[pb, acol, modc], writes=[u])
        S.dma("pool", UT[:, :, g * 256:(g + 1) * 256].rearrange("kc p t -> p kc t"), u[:, :, :], reads=[u], writes=[UTb[g]])


def load_cols(P, st, pbank, dram_rows, nrows, name):
    S = P.S
    rows = P.sb(st, name + "_r", [nrows, 128])
    cols = P.sb(st, name + "_c", [128, nrows])
    S.dma("sp", rows[:, :], dram_rows, writes=[rows])
    S.op("pe", lambda e: e.transpose(pbank[:, 0:nrows], rows[:, :], P.ident[0:nrows, 0:nrows]),
         reads=[rows, P.ident], writes=[pbank])
    S.op("dve", lambda e: e.tensor_copy(out=cols[:, :], in_=pbank[:, 0:nrows]), reads=[pbank], writes=[cols])
    return cols


def ssd_phase(P, i, jj, last):
    nc, S, I = P.nc, P.S, P.I
    sb, ps = P.sb, P.ps
    ident = P.ident
    NB = P.NG
    NCH = 2 * NB
    UTb = [Buf(None, "UT%d" % g) for g in range(NB)]
    SZb = [Buf(None, "SZ%d" % c) for c in range(NCH)]
    YPb = [Buf(None, "YP%d" % c) for c in range(NCH)]
    SBb = [Buf(None, "SB%d" % c) for c in range(NCH)]
    CTb = [Buf(None, "CT%d" % c) for c in range(NCH)]
    EBb = [Buf(None, "EB%d" % c) for c in range(NCH)]
    with contextlib.ExitStack() as st:
        mixer_uT_pass(P, i, st, P.UT, UTb)
        S.barrier()

    with contextlib.ExitStack() as st:
        win = sb(st, "win", [128, KC, SSD_IN], BF16)
        with contextlib.ExitStack() as st2:
            stg = [sb(st2, "wstg%d" % k, [128, 1296]) for k in range(2)]
            k = 0
            engs = ["pool", "dve", "act"]
            for kc in range(KC):
                for q in range(4):
                    sg_ = stg[k % 2]
                    S.dma("sp", sg_[:, :], I["ssd_w_in"][jj, kc * 128:(kc + 1) * 128, q * 1296:(q + 1) * 1296], writes=[sg_])
                    eng = engs[k % 3]
                    if eng == "act":
                        S.op(eng, lambda e, sg_=sg_, kc=kc, q=q: e.copy(out=win[:, kc, q * 1296:(q + 1) * 1296], in_=sg_[:, :]),
                             reads=[sg_], writes=[win])
                    else:
                        S.op(eng, lambda e, sg_=sg_, kc=kc, q=q: e.tensor_copy(out=win[:, kc, q * 1296:(q + 1) * 1296], in_=sg_[:, :]),
                             reads=[sg_], writes=[win])
                    k += 1
            S.barrier()
        segb = [ps(st, "bA%d" % k) for k in range(4)]
        bankA = segb
        bankY = ps(st, "bY")
        bankO = ps(st, "bO")
        bankC = ps(st, "bC")
        bankS = ps(st, "bS")
        cw = load_cols(P, st, bankS, I["ssd_conv_w"][jj, :, :].rearrange("k (c p) -> (k c) p", p=128), 120, "cw")
        cbias = load_cols(P, st, bankC, I["ssd_conv_b"][jj, :].rearrange("(c p) -> c p", p=128), 24, "cb")
        U = sb(st, "Uc", [128, 5, 128])
        S.dma("sp", U[:, :, :], I["ssd_U"][0:5, :, :].rearrange("m p q -> p m q"), writes=[U])
        Abc = sb(st, "Abc", [128, 64])
        dtb = sb(st, "dtb", [128, 64])
        Dbc = sb(st, "Dbc", [128, 32])
        S.dma("sp", Abc[:, :], I["ssd_a_log"][jj, :].partition_broadcast(128), writes=[Abc])
        S.dma("sp", dtb[:, :], I["ssd_dt_bias"][jj, :].partition_broadcast(128), writes=[dtb])
        S.dma("sp", Dbc[:, :], I["ssd_d"][jj, :].partition_broadcast(128), writes=[Dbc])
        S.op("act", lambda e: e.activation(out=Abc[:, :], in_=Abc[:, :], func=AF.Exp), reads=[Abc], writes=[Abc])
        S.op("dve", lambda e: e.tensor_scalar(out=Abc[:, :], in0=Abc[:, :], scalar1=-1.0, scalar2=None, op0=ALU.mult),
             reads=[Abc], writes=[Abc])

        uw = [sb(st, "uw%d" % k, [128, KC, 260], BF16) for k in range(2)]
        acc = [sb(st, "acc%d" % k, [128, 256]) for k in range(4)]
        Bf = sb(st, "Bf", [128, 4, 256], BF16)
        Cf = sb(st, "Cf", [128, 4, 256], BF16)
        xtok = sb(st, "xtok", [128, 2, 2048])
        Btok = sb(st, "Btok", [128, 2, 512], BF16)
        szb = [sb(st, "szb%d" % k, [128, 512]) for k in range(2)]
        dtt = sb(st, "dtt", [128, 2, 64])
        dt1 = sb(st, "dt1", [128, 2, 64])
        dt2 = sb(st, "dt2", [128, 2, 64])
        aa = sb(st, "aa", [128, 2, 64])
        acum = sb(st, "acum", [128, 64])
        eacum = sb(st, "eacum", [128, 64])
        dec = sb(st, "dec", [128, 64])
        eatot = sb(st, "eatot", [128, 64])
        ebst = sb(st, "ebst", [128, 64])
        xdt = [sb(st, "xdt%d" % d, [128, 2048], BF16) for d in range(2)]
        xdd = [sb(st, "xdd%d" % d, [128, 2048], BF16) for d in range(2)]
        lseg = [sb(st, "lseg%d" % k, [128, 512]) for k in range(4)]
        Eb = [sb(st, "Eb%d" % k, [128, 512]) for k in range(3)]
        MT = [sb(st, "MT%d" % k, [128, 4, 128], BF16) for k in range(16)]
        cbm = sb(st, "cbm", [128, 2, 4, 128])
        ysb = [sb(st, "ysb%d" % k, [128, 512]) for k in range(2)]
        ytmp = [sb(st, "ytmp%d" % k, [128, 512]) for k in range(2)]
        hf = sb(st, "hf", [128, 4, 512])
        hf16 = sb(st, "hf16", [128, 4, 512], BF16)
        sbst = [sb(st, "sbst%d" % k, [128, 512]) for k in range(2)]
        S.op("pool", lambda e: e.memset(hf[:], 0.0), writes=[hf])
        S.op("pool", lambda e: e.memset(hf16[:], 0.0), writes=[hf16])

        for b in range(NB):
            t0 = b * 256
            seq_lo, seq_hi = (0, CTX) if b == 0 else (CTX, P.NT)
            lo, hi = max(t0 - 2, seq_lo), min(t0 + 258, seq_hi)
            w_ = uw[b % 2]
            if lo != t0 - 2 or hi != t0 + 258:
                S.op("pool", lambda e, w_=w_: e.memset(w_[:], 0.0), writes=[w_])
            S.dma("sp", w_[:, :, lo - (t0 - 2):hi - (t0 - 2)], P.UT[:, :, lo:hi].rearrange("kc p t -> p kc t"),
                  reads=[UTb[g] for g in range(max(b - 1, 0), min(b + 2, NB))], writes=[w_])
            for part in range(3):
                for c8 in range(8):
                    cch = part * 8 + c8
                    pb = bankA[cch % 4]
                    for kc in range(KC):
                        S.op("pe", lambda e, pb=pb, kc=kc, cch=cch, w_=w_: e.matmul(
                            pb[:, 0:260], lhsT=win[:, kc, 2048 + cch * 128:2048 + (cch + 1) * 128], rhs=w_[:, kc, :],
                            start=(kc == 0), stop=(kc == KC - 1)), reads=[win, w_], writes=[pb])
                    pre = pb
                    a_ = acc[cch % 4]
                    S.op("dve", lambda e, a_=a_, pre=pre, cch=cch: e.tensor_scalar(
                        out=a_[:, :], in0=pre[:, 0:256], scalar1=cw[:, cch:cch + 1], scalar2=cbias[:, cch:cch + 1],
                        op0=ALU.mult, op1=ALU.add), reads=[pre, cw, cbias], writes=[a_])
                    for kk in range(1, 5):
                        S.op("dve", lambda e, a_=a_, pre=pre, cch=cch, kk=kk: e.scalar_tensor_tensor(
                            out=a_[:, :], in0=pre[:, kk:kk + 256], scalar=cw[:, kk * 24 + cch:kk * 24 + cch + 1], in1=a_[:, :],
                            op0=ALU.mult, op1=ALU.add), reads=[pre, cw, a_], writes=[a_])
                    if cch < 16:
                        for s in range(2):
                            S.op("pe", lambda e, a_=a_, s=s: e.transpose(
                                bankS[:, s * 128:(s + 1) * 128], a_[:, s * 128:(s + 1) * 128], ident[:, :]),
                                reads=[a_, ident], writes=[bankS])
                        S.op("act", lambda e, cch=cch: e.activation(
                            out=xtok[:, :, cch * 128:(cch + 1) * 128], in_=bankS[:, 0:256].rearrange("p (s c) -> p s c", s=2),
                            func=AF.Silu), reads=[bankS], writes=[xtok])
                    elif cch < 20:
                        gq = cch - 16
                        S.op("act", lambda e, a_=a_, gq=gq: e.activation(out=Bf[:, gq, :], in_=a_[:, :], func=AF.Silu),
                             reads=[a_], writes=[Bf])
                        for s in range(2):
                            S.op("pe", lambda e, a_=a_, s=s: e.transpose(
                                bankS[:, s * 128:(s + 1) * 128], a_[:, s * 128:(s + 1) * 128], ident[:, :]),
                                reads=[a_, ident], writes=[bankS])
                        S.op("act", lambda e, gq=gq: e.activation(
                            out=Btok[:, :, gq * 128:(gq + 1) * 128], in_=bankS[:, 0:256].rearrange("p (s c) -> p s c", s=2),
                            func=AF.Silu), reads=[bankS], writes=[Btok])
                    else:
                        gq = cch - 20
                        S.op("act", lambda e, a_=a_, gq=gq: e.activation(out=Cf[:, gq, :], in_=a_[:, :], func=AF.Silu),
                             reads=[a_], writes=[Cf])
            for s in range(2):
                for kc in range(KC):
                    S.op("pe", lambda e, s=s, kc=kc, w_=w_: e.matmul(
                        bankS[:, 256 + s * 64:256 + (s + 1) * 64], lhsT=w_[:, kc, 2 + s * 128:2 + (s + 1) * 128],
                        rhs=win[:, kc, 5120:5184], start=(kc == 0), stop=(kc == KC - 1)), reads=[win, w_], writes=[bankS])
            S.op("dve", lambda e: e.tensor_tensor(
                out=dtt[:, :, :], in0=bankS[:, 256:384].rearrange("p (s c) -> p s c", s=2),
                in1=dtb[:, :].unsqueeze(1).to_broadcast([128, 2, 64]), op=ALU.add), reads=[bankS, dtb], writes=[dtt])
            S.op("dve", lambda e: e.scalar_tensor_tensor(out=dt1[:, :, :], in0=dtt[:, :, :], scalar=-1.0, in1=dtt[:, :, :],
                                                         op0=ALU.mult, op1=ALU.max), reads=[dtt], writes=[dt1])
            S.op("act", lambda e: e.activation(out=dt1[:, :, :], in_=dt1[:, :, :], func=AF.Exp, scale=-1.0), reads=[dt1], writes=[dt1])
            S.op("act", lambda e: e.activation(out=dt1[:, :, :], in_=dt1[:, :, :], func=AF.Ln, bias=1.0), reads=[dt1], writes=[dt1])
            S.op("dve", lambda e: e.scalar_tensor_tensor(out=dt2[:, :, :], in0=dtt[:, :, :], scalar=0.0, in1=dt1[:, :, :],
                                                         op0=ALU.max, op1=ALU.add), reads=[dtt, dt1], writes=[dt2])
            S.op("dve", lambda e: e.tensor_tensor(out=aa[:, :, :], in0=dt2[:, :, :],
                                                  in1=Abc[:, :].unsqueeze(1).to_broadcast([128, 2, 64]), op=ALU.mult),
                 reads=[dt2, Abc], writes=[aa])
            for s in range(2):
                c = 2 * b + s
                for q in range(4):
                    pb = bankA[q % 4]
                    for kc in range(KC):
                        S.op("pe", lambda e, pb=pb, s=s, kc=kc, q=q, w_=w_: e.matmul(
                            pb[:, :], lhsT=w_[:, kc, 2 + s * 128:2 + (s + 1) * 128], rhs=win[:, kc, q * 512:(q + 1) * 512],
                            start=(kc == 0), stop=(kc == KC - 1)), reads=[win, w_], writes=[pb])
                    zb = szb[q % 2]
                    S.op("act", lambda e, pb=pb, zb=zb: e.activation(out=zb[:, :], in_=pb[:, :], func=AF.Silu), reads=[pb], writes=[zb])
                    S.dma("pool", P.SZ[c * 128:(c + 1) * 128, q * 512:(q + 1) * 512], zb[:, :], reads=[zb], writes=[SZb[c]])
            for s in range(2):
                c = 2 * b + s
                a_s = aa[:, s, :]
                S.op("pe", lambda e, a_s=a_s: e.matmul(bankC[:, 0:32], lhsT=U[:, 0, :], rhs=a_s[:, 0:32], start=True, stop=True),
                     reads=[U, aa], writes=[bankC])
                S.op("pe", lambda e, a_s=a_s: e.matmul(bankC[:, 32:64], lhsT=U[:, 1, :], rhs=a_s[:, 32:64], start=True, stop=True),
                     reads=[U, aa], writes=[bankC])
                S.op("pe", lambda e, a_s=a_s: e.matmul(bankC[:, 64:128], lhsT=U[:, 4, :], rhs=a_s[:, 0:64], start=True, stop=True),
                     reads=[U, aa], writes=[bankC])
                S.op("dve", lambda e: e.tensor_copy(out=acum[:, :], in_=bankC[:, 0:64]), reads=[bankC], writes=[acum])
                S.op("dve", lambda e: e.tensor_tensor(out=dec[:, :], in0=bankC[:, 64:128], in1=acum[:, :], op=ALU.subtract),
                     reads=[bankC, acum], writes=[dec])
                S.op("act", lambda e: e.activation(out=dec[:, :], in_=dec[:, :], func=AF.Exp), reads=[dec], writes=[dec])
                S.op("act", lambda e: e.activation(out=eacum[:, :], in_=acum[:, :], func=AF.Exp), reads=[acum], writes=[eacum])
                S.op("act", lambda e: e.activation(out=eatot[:, :], in_=bankC[:, 64:128], func=AF.Exp), reads=[bankC], writes=[eatot])
                S.op("pool", lambda e: e.tensor_copy(out=ebst[:, 0:32], in_=eacum[:, 32:64]), reads=[eacum], writes=[ebst])
                S.op("pool", lambda e: e.tensor_copy(out=ebst[:, 32:64], in_=eatot[:, 32:64]), reads=[eatot], writes=[ebst])
                S.dma("pool", P.EB[c, :, :], ebst[:, :], reads=[ebst], writes=[EBb[c]])
                S.dma("pool", P.CT[c, :, :].rearrange("p (g l) -> p g l", g=4), Cf[:, :, s * 128:(s + 1) * 128], reads=[Cf], writes=[CTb[c]])
                for d in range(2):
                    S.op("dve", lambda e, d=d, s=s: e.tensor_tensor(
                        out=xdt[d][:, :].rearrange("p (h q) -> p h q", h=32), in0=xtok[:, s, :].rearrange("p (h q) -> p h q", h=32),
                        in1=dt2[:, s, d * 32:(d + 1) * 32].unsqueeze(2).to_broadcast([128, 32, 64]), op=ALU.mult),
                        reads=[xtok, dt2], writes=[xdt[d]])
                    S.op("pool", lambda e, d=d: e.tensor_tensor(
                        out=xdd[d][:, :].rearrange("p (h q) -> p h q", h=32), in0=xdt[d][:, :].rearrange("p (h q) -> p h q", h=32),
                        in1=dec[:, d * 32:(d + 1) * 32].unsqueeze(2).to_broadcast([128, 32, 64]), op=ALU.mult),
                        reads=[xdt[d], dec], writes=[xdd[d]])
                for gq in range(4):
                    S.op("pe", lambda e, gq=gq, s=s: e.matmul(
                        bankS[:, gq * 128:(gq + 1) * 128],
                        lhsT=Bf[:, gq, s * 128:(s + 1) * 128], rhs=Cf[:, gq, s * 128:(s + 1) * 128], start=True, stop=True),
                        reads=[Bf, Cf], writes=[bankS])
                for d in range(2):
                    S.op("dve", lambda e, d=d: e.tensor_tensor(
                        out=cbm[:, d, :, :], in0=bankS[:, :].rearrange("p (g l) -> p g l", g=4),
                        in1=U[:, d, :].unsqueeze(1).to_broadcast([128, 4, 128]), op=ALU.mult), reads=[bankS, U], writes=[cbm])
                for k16 in range(16):
                    u, d = k16 // 2, k16 % 2
                    gq, hq = u // 2, u % 2
                    h0 = gq * 8 + hq * 4
                    R_ = lseg[k16 % 4]
                    ba = segb[k16 % 4]
                    S.op("dve", lambda e, R_=R_, d=d, h0=h0, s=s: e.tensor_tensor(
                        out=R_[:, :].rearrange("p (h l) -> p h l", h=4),
                        in0=U[:, d, :].unsqueeze(1).to_broadcast([128, 4, 128]),
                        in1=aa[:, s, d * 32 + h0:d * 32 + h0 + 4].unsqueeze(2).to_broadcast([128, 4, 128]), op=ALU.mult),
                        reads=[U, aa], writes=[R_])
                    S.op("pe", lambda e, ba=ba, R_=R_, d=d: e.matmul(
                        ba[:, :], lhsT=U[:, 2 + d, :], rhs=R_[:, :], start=True, stop=True), reads=[R_, U], writes=[ba])
                    E_ = Eb[k16 % 3]
                    M_ = MT[k16]
                    S.op("act", lambda e, ba=ba, E_=E_: e.activation(out=E_[:, :], in_=ba[:, :], func=AF.Exp),
                         reads=[ba], writes=[E_])
                    S.op("pool", lambda e, E_=E_, M_=M_, d=d, gq=gq: e.tensor_tensor(
                        out=M_[:, :, :], in0=E_[:, :].rearrange("p (h l) -> p h l", h=4),
                        in1=cbm[:, d, gq, :].unsqueeze(1).to_broadcast([128, 4, 128]), op=ALU.mult),
                        reads=[E_, cbm], writes=[M_])

                def S2(u):
                    gq, hq = u // 2, u % 2
                    by = bankY
                    for hh in range(4):
                        h = gq * 8 + hq * 4 + hh
                        col = (hq * 4 + hh) * 64
                        for d in range(2):
                            M_ = MT[u * 2 + d]
                            S.op("pe", lambda e, col=col, d=d, hh=hh, h=h, M_=M_: e.matmul(
                                by[:, col:col + 64], lhsT=M_[:, hh, :], rhs=xdt[d][:, h * 64:(h + 1) * 64],
                                start=(d == 0), stop=(d == 1)), reads=[M_, xdt[d]], writes=[by])
                    if hq == 0:
                        return
                    bo = bankO
                    S.op("pe", lambda e, gq=gq, s=s: e.matmul(
                        bo[:, :], lhsT=Cf[:, gq, s * 128:(s + 1) * 128], rhs=hf16[:, gq, :], start=True, stop=True),
                        reads=[Cf, hf16], writes=[bo])
                    yt_, ys_ = ytmp[gq % 2], ysb[gq % 2]
                    S.op("dve", lambda e, yt_=yt_, gq=gq: e.tensor_tensor(
                        out=yt_[:, :].rearrange("p (h q) -> p h q", h=8), in0=bo[:, :].rearrange("p (h q) -> p h q", h=8),
                        in1=eacum[:, gq * 8:(gq + 1) * 8].unsqueeze(2).to_broadcast([128, 8, 64]), op=ALU.mult),
                        reads=[bo, eacum], writes=[yt_])
                    S.op("dve", lambda e, yt_=yt_: e.tensor_tensor(out=yt_[:, :], in0=yt_[:, :], in1=by[:, :], op=ALU.add),
                         reads=[by, yt_], writes=[yt_])
                    S.op("pool", lambda e, ys_=ys_, gq=gq, s=s: e.tensor_tensor(
                        out=ys_[:, :].rearrange("p (h q) -> p h q", h=8),
                        in0=xtok[:, s, gq * 512:(gq + 1) * 512].rearrange("p (h q) -> p h q", h=8),
                        in1=Dbc[:, gq * 8:(gq + 1) * 8].unsqueeze(2).to_broadcast([128, 8, 64]), op=ALU.mult),
                        reads=[xtok, Dbc], writes=[ys_])
                    S.op("pool", lambda e, ys_=ys_, yt_=yt_: e.tensor_tensor(out=ys_[:, :], in0=ys_[:, :], in1=yt_[:, :], op=ALU.add),
                         reads=[ys_, yt_], writes=[ys_])
                    S.dma("pool", P.YP[c * 128:(c + 1) * 128, gq * 512:(gq + 1) * 512], ys_[:, :], reads=[ys_], writes=[YPb[c]])
                    for d in range(2):
                        S.op("pe", lambda e, d=d, gq=gq, s=s: e.matmul(
                            bankC[:, :], lhsT=Btok[:, s, gq * 128:(gq + 1) * 128], rhs=xdd[d][:, gq * 512:(gq + 1) * 512],
                            start=True, stop=True), reads=[Btok, xdd[d]], writes=[bankC])
                        if d == 0:
                            S.op("dve", lambda e, gq=gq: e.tensor_tensor(
                                out=hf[:, gq, :].rearrange("p (h q) -> p h q", h=8), in0=hf[:, gq, :].rearrange("p (h q) -> p h q", h=8),
                                in1=eatot[:, gq * 8:(gq + 1) * 8].unsqueeze(2).to_broadcast([128, 8, 64]), op=ALU.mult),
                                reads=[hf, eatot], writes=[hf])
                            S.op("dve", lambda e, gq=gq: e.tensor_tensor(out=hf[:, gq, :], in0=hf[:, gq, :], in1=bankC[:, :], op=ALU.add),
                                 reads=[hf, bankC], writes=[hf])
                            S.op("act", lambda e, gq=gq: e.copy(out=hf16[:, gq, :], in_=hf[:, gq, :]), reads=[hf], writes=[hf16])
                        else:
                            sb_ = sbst[gq % 2]
                            S.op("act", lambda e, sb_=sb_: e.copy(out=sb_[:, :], in_=bankC[:, :]), reads=[bankC], writes=[sb_])
                            S.dma("pool", P.SBs[c, :, gq * 512:(gq + 1) * 512], sb_[:, :], reads=[sb_], writes=[SBb[c]])

                for u in range(8):
                    S2(u)
        S.barrier()

    with contextlib.ExitStack() as st:
        wout = sb(st, "wout", [128, 16, D], BF16)
        with contextlib.ExitStack() as st2:
            stg = [sb(st2, "ostg%d" % k, [128, D]) for k in range(2)]
            engs = ["pool", "dve", "act"]
            for kc in range(16):
                sg_ = stg[kc % 2]
                S.dma("sp", sg_[:, :], I["ssd_w_out"][jj, kc * 128:(kc + 1) * 128, :], writes=[sg_])
                eng = engs[kc % 3]
                if eng == "act":
                    S.op(eng, lambda e, sg_=sg_, kc=kc: e.copy(out=wout[:, kc, :], in_=sg_[:, :]), reads=[sg_], writes=[wout])
                else:
                    S.op(eng, lambda e, sg_=sg_, kc=kc: e.tensor_copy(out=wout[:, kc, :], in_=sg_[:, :]), reads=[sg_], writes=[wout])
            S.barrier()
        bankO = [ps(st, "qO%d" % k) for k in range(2)]
        bankT = [ps(st, "qT%d" % k) for k in range(2)]
        bankX = [ps(st, "qX%d" % k) for k in range(2)]
        hb = sb(st, "hb", [128, 4, 512])
        hb16 = sb(st, "hb16", [128, 4, 512], BF16)
        ct = [sb(st, "ct%d" % k, [128, 4, 128], BF16) for k in range(2)]
        eb = [sb(st, "eb%d" % k, [128, 64]) for k in range(2)]
        sbl = [sb(st, "sbl%d" % k, [128, 2048]) for k in range(2)]
        yp = [sb(st, "yp%d" % k, [128, 2048]) for k in range(2)]
        szl = [sb(st, "szl%d" % k, [128, 2048]) for k in range(2)]
        xl = [sb(st, "xl%d" % k, [128, D]) for k in range(2)]
        ytmp = [sb(st, "qyt%d" % k, [128, 512]) for k in range(2)]
        gyT = sb(st, "gyT", [128, 16, 128], BF16)
        gnb = sb(st, "gnb", [128, 2048])
        m5bc = sb(st, "m5bc", [128, D])
        ep = [sb(st, "qep%d" % k, [128, 512]) for k in range(2)]
        stats = sb(st, "qstats", [128, 4, 6])
        mv = sb(st, "qmv", [128, 4, 2])
        t1 = sb(st, "qt1", [128, 4])
        t2 = sb(st, "qt2", [128, 4])
        rstd = sb(st, "qrstd", [128, 4])
        S.dma("sp", gnb[:, :], I["ssd_norm_g"][jj, :].partition_broadcast(128), writes=[gnb])
        S.op("pool", lambda e: e.memset(hb[:], 0.0), writes=[hb])
        S.op("pool", lambda e: e.memset(hb16[:], 0.0), writes=[hb16])
        order = [1, 0] + list(range(NCH - 1, 1, -1))
        cur_j = None
        for n, c in enumerate(order):
            j = 1 if c < 2 else 0
            skip_out = last and c < 2
            if j != cur_j:
                S.dma("sp", m5bc[:, :], P.Mrows[j, 5 * D:6 * D].partition_broadcast(128), reads=[P.Mb], writes=[m5bc])
                cur_j = j
            ct_, eb_, sbl_, yp_, szl_, xl_ = ct[n % 2], eb[n % 2], sbl[n % 2], yp[n % 2], szl[n % 2], xl[n % 2]
            S.dma("sp", eb_[:, :], P.EB[c, :, :], reads=[EBb[c]], writes=[eb_])
            S.dma("sp", sbl_[:, :], P.SBs[c, :, :], reads=[SBb[c]], writes=[sbl_])
            if not skip_out:
                S.dma("sp", ct_[:, :, :], P.CT[c, :, :].rearrange("p (g l) -> p g l", g=4), reads=[CTb[c]], writes=[ct_])
                S.dma("sp", yp_[:, :], P.YP[c * 128:(c + 1) * 128, :], reads=[YPb[c]], writes=[yp_])
                S.dma("sp", szl_[:, :], P.SZ[c * 128:(c + 1) * 128, :], reads=[SZb[c]], writes=[szl_])
                S.dma("sp", xl_[:, :], P.X[c * 128:(c + 1) * 128, :], reads=[P.Xb[c // 2]], writes=[xl_])
                for gq in range(4):
                    bo = bankO[gq % 2]
                    S.op("pe", lambda e, bo=bo, gq=gq, ct_=ct_: e.matmul(
                        bo[:, :], lhsT=ct_[:, gq, :], rhs=hb16[:, gq, :], start=True, stop=True), reads=[ct_, hb16], writes=[bo])
                    yt_ = ytmp[gq % 2]
                    S.op("dve", lambda e, bo=bo, yt_=yt_, gq=gq, eb_=eb_: e.tensor_tensor(
                        out=yt_[:, :].rearrange("p (h q) -> p h q", h=8), in0=bo[:, :].rearrange("p (h q) -> p h q", h=8),
                        in1=eb_[:, gq * 8:(gq + 1) * 8].unsqueeze(2).to_broadcast([128, 8, 64]), op=ALU.mult),
                        reads=[bo, eb_], writes=[yt_])
                    S.op("pool", lambda e, yt_=yt_, yp_=yp_, gq=gq: e.tensor_tensor(
                        out=yp_[:, gq * 512:(gq + 1) * 512], in0=yp_[:, gq * 512:(gq + 1) * 512], in1=yt_[:, :], op=ALU.add),
                        reads=[yt_, yp_], writes=[yp_])
                S.op("dve", lambda e, yp_=yp_, szl_=szl_: e.tensor_tensor(out=yp_[:, :], in0=yp_[:, :], in1=szl_[:, :], op=ALU.mult),
                     reads=[yp_, szl_], writes=[yp_])
                for gq in range(4):
                    S.op("dve", lambda e, gq=gq, yp_=yp_: e.bn_stats(out=stats[:, gq, :], in_=yp_[:, gq * 512:(gq + 1) * 512]),
                         reads=[yp_], writes=[stats])
                    S.op("dve", lambda e, gq=gq: e.bn_aggr(out=mv[:, gq, :], in_=stats[:, gq, :]), reads=[stats], writes=[mv])
                S.op("dve", lambda e: e.tensor_tensor(out=t1[:, :], in0=mv[:, :, 0], in1=mv[:, :, 0], op=ALU.mult), reads=[mv], writes=[t1])
                S.op("dve", lambda e: e.scalar_tensor_tensor(out=t2[:, :], in0=t1[:, :], scalar=EPS, in1=mv[:, :, 1],
                                                             op0=ALU.add, op1=ALU.add), reads=[t1, mv], writes=[t2])
                S.op("pool", lambda e: e.tensor_tensor(out=rstd[:, :], in0=t2[:, :], in1=P.neghalf[:, 0:4], op=ALU.pow),
                     reads=[t2, P.neghalf], writes=[rstd])
                for gq in range(4):
                    S.op("dve", lambda e, gq=gq, yp_=yp_: e.scalar_tensor_tensor(
                        out=yp_[:, gq * 512:(gq + 1) * 512], in0=yp_[:, gq * 512:(gq + 1) * 512], scalar=rstd[:, gq:gq + 1],
                        in1=gnb[:, gq * 512:(gq + 1) * 512], op0=ALU.mult, op1=ALU.mult), reads=[yp_, rstd, gnb], writes=[yp_])
                for kc in range(16):
                    pb = bankT[(kc // 4) % 2]
                    S.op("pe", lambda e, pb=pb, kc=kc, yp_=yp_: e.transpose(
                        pb[:, (kc % 4) * 128:(kc % 4 + 1) * 128], yp_[:, kc * 128:(kc + 1) * 128], ident[:, :]),
                        reads=[yp_, ident], writes=[pb])
                    if kc % 4 == 3:
                        S.op("act", lambda e, pb=pb, kc=kc: e.copy(
                            out=gyT[:, kc - 3:kc + 1, :], in_=pb[:, :].rearrange("p (a b) -> p a b", a=4)), reads=[pb], writes=[gyT])
                for hh in range(2):
                    pb = bankX[hh]
                    for kc in range(16):
                        S.op("pe", lambda e, pb=pb, kc=kc, hh=hh: e.matmul(
                            pb[:, :], lhsT=gyT[:, kc, :], rhs=wout[:, kc, hh * 512:(hh + 1) * 512],
                            start=(kc == 0), stop=(kc == 15)), reads=[gyT, wout], writes=[pb])
                    epb = ep[hh]
                    S.op("dve", lambda e, pb=pb, epb=epb, hh=hh: e.tensor_tensor(
                        out=epb[:, :], in0=pb[:, :], in1=m5bc[:, hh * 512:(hh + 1) * 512], op=ALU.mult), reads=[pb, m5bc], writes=[epb])
                    S.op("pool", lambda e, epb=epb, xl_=xl_, hh=hh: e.tensor_tensor(
                        out=xl_[:, hh * 512:(hh + 1) * 512], in0=epb[:, :], in1=xl_[:, hh * 512:(hh + 1) * 512], op=ALU.add),
                        reads=[epb, xl_], writes=[xl_])
                S.dma("pool", P.X[c * 128:(c + 1) * 128, :], xl_[:, :], reads=[xl_], writes=[P.Xb[c // 2]])
            S.op("dve", lambda e, eb_=eb_: e.tensor_tensor(
                out=hb[:, :, :].rearrange("p g (h q) -> p (g h) q", h=8), in0=hb[:, :, :].rearrange("p g (h q) -> p (g h) q", h=8),
                in1=eb_[:, 32:64].unsqueeze(2).to_broadcast([128, 32, 64]), op=ALU.mult), reads=[hb, eb_], writes=[hb])
            S.op("pool", lambda e, sbl_=sbl_: e.tensor_tensor(
                out=hb[:, :, :].rearrange("p g c -> p (g c)"), in0=hb[:, :, :].rearrange("p g c -> p (g c)"), in1=sbl_[:, :], op=ALU.add),
                reads=[hb, sbl_], writes=[hb])
            S.op("act", lambda e: e.copy(out=hb16[:, :, :], in_=hb[:, :, :]), reads=[hb], writes=[hb16])
        S.barrier()


HY_BANDS = 16
HY_MAX_DECAY = math.log(1e-2) / 0.3
HY_MIN_DECAY = math.log(1e-2) / 1.5


def hy_consts(Lf):
    import ml_dtypes
    bf = ml_dtypes.bfloat16
    f = np.float32
    N = 2 * Lf
    NA = N // 128
    NZ = Lf // 128
    a = np.arange(NZ)[:, None].astype(np.float64)
    c = np.arange(NA)[None, :].astype(np.float64)
    b = np.arange(128)[:, None].astype(np.float64)
    th = 2 * np.pi * a * c / NA
    FA = np.stack([np.cos(th), -np.sin(th)]).astype(bf)
    tw = 2 * np.pi * b * c / N
    TW = np.stack([np.cos(tw), -np.sin(tw)]).astype(f)
    IT = np.stack([np.cos(tw).T, np.sin(tw).T]).astype(f)
    CS = np.stack([np.cos(th).T / N, -np.sin(th).T / N, -np.cos(th).T / N]).astype(bf)
    pos = np.arange(Lf, dtype=f)
    t = np.linspace(0.0, 1.0, Lf, dtype=f)
    wpos = (f(2.0 * math.pi) * pos / f(Lf)).astype(f)
    fr = np.linspace(1e-4, HY_BANDS - 1, HY_BANDS, dtype=f)
    ang = wpos[:, None] * fr[None, :]
    z = np.concatenate([t[:, None], np.cos(ang), -np.sin(ang)], axis=-1).astype(f)
    return dict(FA=FA, TW=TW, IT=IT, CS=CS, zT=np.ascontiguousarray(z.T), trow=t.reshape(1, Lf).copy())


def hy_shared_consts():
    import ml_dtypes
    bf = ml_dtypes.bfloat16
    b = np.arange(128)[:, None].astype(np.float64)
    e = np.arange(128)[None, :].astype(np.float64)
    th = 2 * np.pi * b * e / 128
    F = np.stack([np.cos(th), -np.sin(th), np.sin(th), -np.cos(th)]).astype(bf)
    deltas = np.abs(np.linspace(HY_MIN_DECAY, HY_MAX_DECAY, D, dtype=np.float32))
    ndel = np.ascontiguousarray((-deltas).reshape(KC, 128).T.astype(np.float32))
    return F, ndel


def hy_dense_consts():
    import ml_dtypes
    bf = ml_dtypes.bfloat16
    N = 2 * CTX
    t = np.arange(CTX)[:, None].astype(np.float64)
    k = np.arange(N)[None, :].astype(np.float64)
    th = 2 * np.pi * t * k / N
    DF = np.stack([np.cos(th), -np.sin(th)]).astype(bf)
    DG = np.stack([np.cos(th).T / N, -np.sin(th).T / N]).astype(bf)
    return DF, DG


def hyena_phase(P, i, jj):
    nc, S, I = P.nc, P.S, P.I
    sb, ps = P.sb, P.ps
    ident = P.ident
    NB = P.NG
    UTb = [Buf(None, "hUT%d" % g) for g in range(NB)]
    Vb = Buf(None, "hV")
    with contextlib.ExitStack() as st:
        mixer_uT_pass(P, i, st, P.UT, UTb)
        S.barrier()

    with contextlib.ExitStack() as st:
        win = sb(st, "hwin", [128, KC, 3072], BF16)
        with contextlib.ExitStack() as st2:
            stg = [sb(st2, "hstg%d" % k, [128, 1536]) for k in range(2)]
            engs = ["pool", "dve", "act"]
            k = 0
            for kc in range(KC):
                for q in range(2):
                    sg_ = stg[k % 2]
                    S.dma("sp", sg_[:, :], I["hy_w_in"][jj, kc * 128:(kc + 1) * 128, q * 1536:(q + 1) * 1536], writes=[sg_])
                    eng = engs[k % 3]
                    if eng == "act":
                        S.op(eng, lambda e, sg_=sg_, kc=kc, q=q: e.copy(out=win[:, kc, q * 1536:(q + 1) * 1536], in_=sg_[:, :]),
                             reads=[sg_], writes=[win])
                    else:
                        S.op(eng, lambda e, sg_=sg_, kc=kc, q=q: e.tensor_copy(out=win[:, kc, q * 1536:(q + 1) * 1536], in_=sg_[:, :]),
                             reads=[sg_], writes=[win])
                    k += 1
            S.barrier()
        bankA = [ps(st, "hA%d" % k) for k in range(2)]
        bankS = ps(st, "hS")
        cw = load_cols(P, st, bankS, I["hy_conv_w"][jj, :, :].rearrange("k (c p) -> (k c) p", p=128), 72, "hcw")
        cbias = load_cols(P, st, bankS, I["hy_conv_b"][jj, :].rearrange("(c p) -> c p", p=128), 24, "hcb")
        uw = [sb(st, "huw%d" % k, [128, KC, 258], BF16) for k in range(2)]
        pre = [sb(st, "hpre%d" % k, [128, 258]) for k in range(2)]
        acc = [sb(st, "hacc%d" % k, [128, 256]) for k in range(2)]
        vb16 = [sb(st, "hvb%d" % k, [128, 256], BF16) for k in range(2)]
        for b in range(NB):
            t0 = b * 256
            seq_lo, seq_hi = (0, CTX) if b == 0 else (CTX, P.NT)
            lo, hi = max(t0 - 1, seq_lo), min(t0 + 257, seq_hi)
            w_ = uw[b % 2]
            if lo != t0 - 1 or hi != t0 + 257:
                S.op("pool", lambda e, w_=w_: e.memset(w_[:], 0.0), writes=[w_])
            S.dma("sp", w_[:, :, lo - (t0 - 1):hi - (t0 - 1)], P.UT[:, :, lo:hi].rearrange("kc p t -> p kc t"),
                  reads=[UTb[g] for g in range(max(b - 1, 0), min(b + 2, NB))], writes=[w_])
            for cch in range(24):
                pb = bankA[cch % 2]
                for kc in range(KC):
                    S.op("pe", lambda e, pb=pb, kc=kc, cch=cch, w_=w_: e.matmul(
                        pb[:, 0:258], lhsT=win[:, kc, cch * 128:(cch + 1) * 128], rhs=w_[:, kc, :],
                        start=(kc == 0), stop=(kc == KC - 1)), reads=[win, w_], writes=[pb])
                pr, a_ = pre[cch % 2], acc[cch % 2]
                S.op("act", lambda e, pb=pb, pr=pr: e.copy(out=pr[:, :], in_=pb[:, 0:258]), reads=[pb], writes=[pr])
                S.op("dve", lambda e, a_=a_, pr=pr, cch=cch: e.tensor_scalar(
                    out=a_[:, :], in0=pr[:, 0:256], scalar1=cw[:, cch:cch + 1], scalar2=cbias[:, cch:cch + 1],
                    op0=ALU.mult, op1=ALU.add), reads=[pr, cw, cbias], writes=[a_])
                for kk in range(1, 3):
                    S.op("dve", lambda e, a_=a_, pr=pr, cch=cch, kk=kk: e.scalar_tensor_tensor(
                        out=a_[:, :], in0=pr[:, kk:kk + 256], scalar=cw[:, kk * 24 + cch:kk * 24 + cch + 1], in1=a_[:, :],
                        op0=ALU.mult, op1=ALU.add), reads=[pr, cw, a_], writes=[a_])
                if cch < 8:
                    v_ = vb16[cch % 2]
                    S.op("act", lambda e, a_=a_, v_=v_: e.copy(out=v_[:, :], in_=a_[:, :]), reads=[a_], writes=[v_])
                    S.dma("pool", P.HV[cch, :, t0:t0 + 256], v_[:, :], reads=[v_])
                elif cch < 16:
                    S.dma("pool", P.HX1[cch - 8, :, t0:t0 + 256], a_[:, :], reads=[a_])
                else:
                    S.dma("pool", P.HX2[cch - 16, :, t0:t0 + 256], a_[:, :], reads=[a_])
        S.barrier()

    classes = [("c", CTX, 0, P.hyc_c), ("l", P.L, CTX, P.hyc_l)]

    with contextlib.ExitStack() as st:
        w1 = sb(st, "fw1", [33, 64])
        w2 = sb(st, "fw2", [64, 64])
        w3f = sb(st, "fw3f", [64, 4096])
        w3 = sb(st, "fw3", [64, 4096], BF16)
        ndel = sb(st, "ndel", [128, KC])
        bank1 = ps(st, "f1")
        bank2 = ps(st, "f2")
        bankH = [ps(st, "fH%d" % k) for k in range(2)]
        S.dma("sp", w1[:, :], I["hy_filt_w1"][jj, :, :], writes=[w1])
        S.dma("sp", w2[:, :], I["hy_filt_w2"][jj, :, :], writes=[w2])
        S.dma("sp", w3f[:, :], I["hy_filt_w3"][jj, :, :], writes=[w3f])
        S.op("dve", lambda e: e.tensor_copy(out=w3[:, :], in_=w3f[:, :]), reads=[w3f], writes=[w3])
        S.dma("sp", ndel[:, :], I["hy_ndel"][:, :], writes=[ndel])
        fcol = load_cols(P, st, bank1, I["hy_filt_fb"][jj, :, :], 4, "fcol")
        fb = sb(st, "ffb", [64, 2])
        S.op("dve", lambda e: e.tensor_tensor(out=fb[:, :], in0=fcol[0:64, 0:2], in1=fcol[0:64, 2:4], op=ALU.mult),
             reads=[fcol], writes=[fb])
        TWO_PI = 2.0 * math.pi

        def gen_filters(tag, Lf, base, Dd):
            nblk = max(Lf // 512, 1)
            bw = min(Lf, 512)
            zT = sb(st, "zT" + tag, [33, Lf])
            tbc = sb(st, "tbc" + tag, [128, Lf])
            S.dma("sp", zT[:, :], Dd["zT"][:, :], writes=[zT])
            S.dma("sp", tbc[:, :], Dd["trow"][0, :].partition_broadcast(128), writes=[tbc])
            stats = sb(st, "fstats" + tag, [128, 2, KC, 2 * nblk, 6])
            mv = sb(st, "fmv" + tag, [128, 2, KC, 2])
            arg = sb(st, "farg" + tag, [64, bw])
            wr = sb(st, "fwr" + tag, [64, bw])
            hid = sb(st, "fhid" + tag, [64, bw])
            hid2 = sb(st, "fhid2" + tag, [64, bw], BF16)
            dcy = sb(st, "fdcy" + tag, [128, KC, bw])
            h0 = [sb(st, "fh0%s%d" % (tag, k), [128, bw]) for k in range(2)]
            h1 = [sb(st, "fh1%s%d" % (tag, k), [128, bw]) for k in range(2)]
            so = [sb(st, "fso%s%d" % (tag, k), [128, bw], BF16) for k in range(2)]
            do = [sb(st, "fdo%s%d" % (tag, k), [128, bw], BF16) for k in range(2)]

            def sin_layer(src_bank, fi, dst):
                S.op("dve", lambda e: e.tensor_scalar(out=arg[:, :], in0=src_bank[0:64, 0:bw], scalar1=fcol[0:64, fi:fi + 1],
                                                      scalar2=fb[:, fi:fi + 1], op0=ALU.mult, op1=ALU.add),
                     reads=[src_bank, fcol, fb], writes=[arg])
                for _ in range(2):
                    S.op("dve", lambda e: e.tensor_scalar(out=wr[:, :], in0=arg[:, :], scalar1=math.pi, scalar2=-TWO_PI,
                                                          op0=ALU.is_gt, op1=ALU.mult), reads=[arg], writes=[wr])
                    S.op("dve", lambda e: e.tensor_tensor(out=arg[:, :], in0=arg[:, :], in1=wr[:, :], op=ALU.add),
                         reads=[arg, wr], writes=[arg])
                    S.op("dve", lambda e: e.tensor_scalar(out=wr[:, :], in0=arg[:, :], scalar1=-math.pi, scalar2=TWO_PI,
                                                          op0=ALU.is_lt, op1=ALU.mult), reads=[arg], writes=[wr])
                    S.op("dve", lambda e: e.tensor_tensor(out=arg[:, :], in0=arg[:, :], in1=wr[:, :], op=ALU.add),
                         reads=[arg, wr], writes=[arg])
                S.op("act", lambda e: e.activation(out=dst[:, :], in_=arg[:, :], func=AF.Sin), reads=[arg], writes=[dst])

            for lb in range(nblk):
                l0 = lb * bw
                S.op("pe", lambda e, l0=l0: e.matmul(bank1[0:64, 0:bw], lhsT=w1[:, :], rhs=zT[:, l0:l0 + bw], start=True, stop=True),
                     reads=[w1, zT], writes=[bank1])
                sin_layer(bank1, 0, hid)
                S.op("pe", lambda e: e.matmul(bank2[0:64, 0:bw], lhsT=w2[:, :], rhs=hid[:, :], start=True, stop=True),
                     reads=[w2, hid], writes=[bank2])
                sin_layer(bank2, 1, hid2)
                for dc in range(KC):
                    S.op("act", lambda e, dc=dc, l0=l0: e.activation(out=dcy[:, dc, :], in_=tbc[:, l0:l0 + bw], func=AF.Exp,
                                                                      scale=ndel[:, dc:dc + 1]), reads=[tbc, ndel], writes=[dcy])
                n = 0
                for filt in range(2):
                    for dc in range(KC):
                        hh = [h0[n % 2], h1[n % 2]]
                        for dr in range(2):
                            col = ((filt * 2 + dr) * KC + dc) * 128
                            pb = bankH[dr]
                            S.op("pe", lambda e, pb=pb, col=col: e.matmul(
                                pb[:, 0:bw], lhsT=w3[:, col:col + 128], rhs=hid2[:, :], start=True, stop=True),
                                reads=[w3, hid2], writes=[pb])
                            S.op("dve", lambda e, pb=pb, dr=dr, dc=dc, hh=hh: e.tensor_tensor(
                                out=hh[dr][:, :], in0=pb[:, 0:bw], in1=dcy[:, dc, :], op=ALU.mult), reads=[pb, dcy], writes=[hh[dr]])
                            S.op("dve", lambda e, dr=dr, dc=dc, hh=hh, filt=filt, lb=lb: e.bn_stats(
                                out=stats[:, filt, dc, dr * nblk + lb, :], in_=hh[dr][:, :]), reads=[hh[dr]], writes=[stats])
                        if lb == 0:
                            S.op("pool", lambda e, hh=hh: e.memset(hh[1][:, 0:1], 0.0), reads=[hh[1]], writes=[hh[1]])
                        s_, d_ = so[n % 2], do[n % 2]
                        S.op("pool", lambda e, hh=hh, s_=s_: e.tensor_tensor(out=s_[:, :], in0=hh[0][:, :], in1=hh[1][:, :], op=ALU.add),
                             reads=hh, writes=[s_])
                        S.op("pool", lambda e, hh=hh, d_=d_: e.tensor_tensor(out=d_[:, :], in0=hh[0][:, :], in1=hh[1][:, :], op=ALU.subtract),
                             reads=hh, writes=[d_])
                        S.dma("sp", Dd["KS"][filt, 0, dc, :, l0:l0 + bw], s_[:, :], reads=[s_])
                        S.dma("sp", Dd["KS"][filt, 1, dc, :, l0:l0 + bw], d_[:, :], reads=[d_])
                        n += 1
            for filt in range(2):
                for dc in range(KC):
                    S.op("dve", lambda e, filt=filt, dc=dc: e.bn_aggr(
                        out=mv[:, filt, dc, :], in_=stats[:, filt, dc, :, :].rearrange("p a b -> p (a b)")),
                        reads=[stats], writes=[mv])
            ssq = sb(st, "fssq" + tag, [128, 16])
            rn = sb(st, "frn" + tag, [128, 16])
            rrow = sb(st, "frrow" + tag, [16, 128])
            mvv = mv[:, :, :, :].rearrange("p f c t -> p (f c) t")
            S.op("dve", lambda e: e.tensor_tensor(out=ssq[:, :], in0=mvv[:, :, 0], in1=mvv[:, :, 0], op=ALU.mult), reads=[mv], writes=[ssq])
            S.op("dve", lambda e: e.tensor_tensor(out=ssq[:, :], in0=ssq[:, :], in1=mvv[:, :, 1], op=ALU.add), reads=[ssq, mv], writes=[ssq])
            S.op("dve", lambda e: e.tensor_scalar(out=ssq[:, :], in0=ssq[:, :], scalar1=float(2 * Lf), scalar2=EPS,
                                                  op0=ALU.mult, op1=ALU.add), reads=[ssq], writes=[ssq])
            S.op("pool", lambda e: e.tensor_tensor(out=rn[:, :], in0=ssq[:, :], in1=P.neghalf16[:, 0:16], op=ALU.pow),
                 reads=[ssq, P.neghalf16], writes=[rn])
            S.op("pe", lambda e: e.transpose(bank1[0:16, 0:128], rn[:, :], ident[:, :]), reads=[rn, ident], writes=[bank1])
            S.op("dve", lambda e: e.tensor_copy(out=rrow[:, :], in_=bank1[0:16, 0:128]), reads=[bank1], writes=[rrow])
            S.dma("sp", Dd["RN"][:, :].rearrange("f (c p) -> (f c) p", p=128), rrow[:, :], reads=[rrow])

        for cl in classes:
            gen_filters(*cl)
        S.barrier()

    NSLOT = 4
    with contextlib.ExitStack() as st:
        F128 = sb(st, "F128", [128, 4, 128], BF16)
        S.dma("sp", F128[:, :, :], I["hy_F128"][:, :, :].rearrange("m p q -> p m q"), writes=[F128])
        bias_bc = sb(st, "hbias", [128, 2, D])
        for filt in range(2):
            S.dma("sp", bias_bc[:, filt, :], I["hy_bias"][jj, filt, :].partition_broadcast(128), writes=[bias_bc])
        bankR = [ps(st, "hpR%d" % k) for k in range(NSLOT)]
        bankI = [ps(st, "hpI%d" % k) for k in range(NSLOT)]
        mts = [[sb(st, "hm%d_%d" % (s_, k), [128, 512]) for k in range(4)] for s_ in range(NSLOT)]

        def cmul(mt, Ar, Ai, Cr, Ci, Or, Oi, rA, np_, shape3):
            W_ = shape3[0] * shape3[1]

            def v(buf):
                return buf[0:np_, 0:W_].rearrange("p (c x) -> p c x", c=shape3[0])
            combos = ((Ar, Cr), (Ai, Ci), (Ar, Ci), (Ai, Cr))
            for k_, (A_, C_) in enumerate(combos):
                S.op("dve", lambda e, k_=k_, A_=A_, C_=C_: e.tensor_tensor(out=v(mt[k_]), in0=v(A_), in1=C_, op=ALU.mult),
                     reads=[A_] + rA, writes=[mt[k_]])
            S.op("pool", lambda e: e.tensor_tensor(out=v(Or), in0=v(mt[0]), in1=v(mt[1]), op=ALU.subtract),
                 reads=[mt[0], mt[1]], writes=[Or])
            S.op("pool", lambda e: e.tensor_tensor(out=v(Oi), in0=v(mt[2]), in1=v(mt[3]), op=ALU.add),
                 reads=[mt[2], mt[3]], writes=[Oi])

        def do_lat(tag, Lf, base, Dd):
            N = 2 * Lf
            NA = N // 128
            NZ = Lf // 128
            CB = 4
            G = CB * NA
            NGR = D // CB
            with contextlib.ExitStack() as stc:
                FA = sb(stc, "FA" + tag, [NZ, 2, NA], BF16)
                TW = sb(stc, "TW" + tag, [128, 2, NA])
                IT = sb(stc, "IT" + tag, [NA, 2, 128])
                CS = sb(stc, "CS" + tag, [NA, 3, NZ], BF16)
                rnb = sb(stc, "rnb" + tag, [128, 2, D])
                S.dma("sp", FA[:, :, :], Dd["FA"][:, :, :].rearrange("m p q -> p m q"), writes=[FA])
                S.dma("sp", TW[:, :, :], Dd["TW"][:, :, :].rearrange("m p q -> p m q"), writes=[TW])
                S.dma("sp", IT[:, :, :], Dd["IT"][:, :, :].rearrange("m p q -> p m q"), writes=[IT])
                S.dma("sp", CS[:, :, :], Dd["CS"][:, :, :].rearrange("m p q -> p m q"), writes=[CS])
                for filt in range(2):
                    S.dma("sp", rnb[:, filt, :], Dd["RN"][filt, :].partition_broadcast(128), writes=[rnb])
                TWr = TW[:, 0, :].unsqueeze(1).to_broadcast([128, CB, NA])
                TWi = TW[:, 1, :].unsqueeze(1).to_broadcast([128, CB, NA])
                ITr = IT[:, 0, :].unsqueeze(1).to_broadcast([NA, CB, 128])
                ITi = IT[:, 1, :].unsqueeze(1).to_broadcast([NA, CB, 128])
                xin = [[sb(stc, "hxin%s%d_%d" % (tag, s_, k), [NZ, CB, 128], BF16) for k in range(2)] for s_ in range(NSLOT)]
                PR = [sb(stc, "hPR%s%d" % (tag, s_), [128, 512], BF16) for s_ in range(NSLOT)]
                PI = [sb(stc, "hPI%s%d" % (tag, s_), [128, 512], BF16) for s_ in range(NSLOT)]

                def st_A(s_, x_):
                    for ch in range(CB):
                        for ri, pA in ((0, bankR[s_]), (1, bankI[s_])):
                            S.op("pe", lambda e, ch=ch, ri=ri, pA=pA, x_=x_: e.matmul(
                                pA[:, ch * NA:(ch + 1) * NA], lhsT=x_[:, ch, :], rhs=FA[:, ri, :], start=True, stop=True),
                                reads=[x_, FA], writes=[pA])

                Mq = [[sb(stc, "hMq%s%d_%d" % (tag, s_, k), [128, 512], BF16) for k in range(4)] for s_ in range(NSLOT)]
                cpI = [sb(stc, "hcpI%s%d" % (tag, s_), [128, 512]) for s_ in range(NSLOT)]

                def tw_products(s_, Cr, Ci, np_, shape3):
                    W_ = shape3[0] * shape3[1]

                    def v(buf):
                        return buf[0:np_, 0:W_].rearrange("p (c x) -> p c x", c=shape3[0])
                    Ar, Ai, c_, m_ = bankR[s_], bankI[s_], cpI[s_], Mq[s_]
                    S.op("act", lambda e: e.copy(out=c_[0:np_, 0:W_], in_=Ai[0:np_, 0:W_]), reads=[Ai], writes=[c_])
                    S.op("dve", lambda e: e.tensor_tensor(out=v(m_[0]), in0=v(Ar), in1=Cr, op=ALU.mult), reads=[Ar], writes=[m_[0]])
                    S.op("dve", lambda e: e.tensor_tensor(out=v(m_[2]), in0=v(Ar), in1=Ci, op=ALU.mult), reads=[Ar], writes=[m_[2]])
                    S.op("dve", lambda e: e.tensor_tensor(out=v(m_[1]), in0=v(Ai), in1=Ci, op=ALU.mult), reads=[Ai], writes=[m_[1]])
                    S.op("pool", lambda e: e.tensor_tensor(out=v(m_[3]), in0=v(c_), in1=Cr, op=ALU.mult), reads=[c_], writes=[m_[3]])

                def st_tw(s_):
                    tw_products(s_, TWr, TWi, 128, (CB, NA))

                def st_C(s_, need_re=True, need_im=True):
                    m_, pXr, pXi = Mq[s_], bankR[s_], bankI[s_]
                    if need_re:
                        for n_, (fi, mi) in enumerate(((0, 0), (3, 1), (2, 2), (2, 3))):
                            S.op("pe", lambda e, n_=n_, fi=fi, mi=mi: e.matmul(
                                pXr[:, 0:G], lhsT=F128[:, fi, :], rhs=m_[mi][:, 0:G], start=(n_ == 0), stop=(n_ == 3)),
                                reads=[F128, m_[mi]], writes=[pXr])
                    if need_im:
                        for n_, (fi, mi) in enumerate(((1, 0), (2, 1), (0, 2), (0, 3))):
                            S.op("pe", lambda e, n_=n_, fi=fi, mi=mi: e.matmul(
                                pXi[:, 0:G], lhsT=F128[:, fi, :], rhs=m_[mi][:, 0:G], start=(n_ == 0), stop=(n_ == 3)),
                                reads=[F128, m_[mi]], writes=[pXi])

                kout = [sb(stc, "hko%s%d" % (tag, k), [128, G], BF16) for k in range(NSLOT)]
                ktmp = [sb(stc, "hkt%s%d" % (tag, k), [128, G]) for k in range(NSLOT)]
                items = [(filt, sd, gr) for filt in range(2) for sd in range(2) for gr in range(NGR)]
                for w0 in range(0, len(items), NSLOT):
                    wave = items[w0:w0 + NSLOT]
                    par = (w0 // NSLOT) % 2
                    for s_, (filt, sd, gr) in enumerate(wave):
                        d0 = gr * CB
                        S.dma("sp", xin[s_][par][:, :, :],
                              Dd["KS"][filt, sd, d0 // 128, d0 % 128:d0 % 128 + CB, :].rearrange("c (a b) -> a c b", b=128),
                              writes=[xin[s_][par]])
                    for s_, it in enumerate(wave):
                        st_A(s_, xin[s_][par])
                    for s_, it in enumerate(wave):
                        st_tw(s_)
                    for s_, (filt, sd, gr) in enumerate(wave):
                        st_C(s_, need_re=(sd == 0), need_im=(sd == 1))
                    for s_, (filt, sd, gr) in enumerate(wave):
                        d0 = gr * CB
                        src = bankR[s_] if sd == 0 else bankI[s_]
                        ko, kt = kout[s_], ktmp[s_]
                        rn3 = rnb[:, filt, d0:d0 + CB].unsqueeze(2).to_broadcast([128, CB, NA])
                        if sd == 0:
                            S.op("dve", lambda e, src=src, rn3=rn3, kt=kt: e.tensor_tensor(
                                out=kt[:, 0:G].rearrange("p (c x) -> p c x", c=CB), in0=src[:, 0:G].rearrange("p (c x) -> p c x", c=CB),
                                in1=rn3, op=ALU.mult), reads=[src, rnb], writes=[kt])
                            b3 = bias_bc[:, filt, d0:d0 + CB].unsqueeze(2).to_broadcast([128, CB, NA])
                            S.op("pool", lambda e, ko=ko, b3=b3, kt=kt: e.tensor_tensor(
                                out=ko[:, :].rearrange("p (c x) -> p c x", c=CB), in0=kt[:, 0:G].rearrange("p (c x) -> p c x", c=CB),
                                in1=b3, op=ALU.add), reads=[kt, bias_bc], writes=[ko])
                        else:
                            S.op("dve", lambda e, src=src, rn3=rn3, ko=ko: e.tensor_tensor(
                                out=ko[:, :].rearrange("p (c x) -> p c x", c=CB), in0=src[:, 0:G].rearrange("p (c x) -> p c x", c=CB),
                                in1=rn3, op=ALU.mult), reads=[src, rnb], writes=[ko])
                        S.dma("pool", Dd["KF"][filt, sd, :, d0:d0 + CB, :], ko[:, :].rearrange("p (c x) -> p c x", c=CB), reads=[ko])
                S.barrier()

                x1b = [[sb(stc, "hx1%s%d_%d" % (tag, s_, k), [NZ, CB, 128]) for k in range(2)] for s_ in range(NSLOT)]
                x2b = [[sb(stc, "hx2%s%d_%d" % (tag, s_, k), [NZ, CB, 128]) for k in range(2)] for s_ in range(NSLOT)]
                kf = [[[sb(stc, "hkf%s%d_%d_%d" % (tag, s_, k, m), [128, G], BF16) for m in range(4)] for k in range(2)] for s_ in range(NSLOT)]
                zt = [sb(stc, "hzt%s%d" % (tag, s_), [NZ, CB, 128], BF16) for s_ in range(NSLOT)]
                yo = [sb(stc, "hyo%s%d" % (tag, s_), [NZ, CB, 128]) for s_ in range(NSLOT)]

                def blk(T, d0):
                    return T[d0 // 128, d0 % 128:d0 % 128 + CB, base:base + Lf].rearrange("c (a b) -> a c b", b=128)

                for w0 in range(0, NGR, NSLOT):
                    grs = list(range(w0, min(w0 + NSLOT, NGR)))
                    par = (w0 // NSLOT) % 2
                    for s_, gr in enumerate(grs):
                        d0 = gr * CB
                        S.dma("sp", xin[s_][par][:, :, :], blk(P.HV, d0), writes=[xin[s_][par]])
                        S.dma("sp", x1b[s_][par][:, :, :], blk(P.HX1, d0), writes=[x1b[s_][par]])
                        S.dma("sp", x2b[s_][par][:, :, :], blk(P.HX2, d0), writes=[x2b[s_][par]])
                        for m in range(4):
                            S.dma("sp", kf[s_][par][m][:, :].rearrange("p (c x) -> p c x", c=CB),
                                  Dd["KF"][m // 2, m % 2, :, d0:d0 + CB, :], writes=[kf[s_][par][m]])
                    for q in range(2):
                        for s_, gr in enumerate(grs):
                            st_A(s_, xin[s_][par] if q == 0 else zt[s_])
                        for s_, gr in enumerate(grs):
                            st_tw(s_)
                        for s_, gr in enumerate(grs):
                            st_C(s_)
                        for s_, gr in enumerate(grs):
                            kf_ = kf[s_][par]
                            cmul(mts[s_], bankR[s_], bankI[s_], kf_[2 * q][:, :].rearrange("p (c x) -> p c x", c=CB),
                                 kf_[2 * q + 1][:, :].rearrange("p (c x) -> p c x", c=CB), PR[s_], PI[s_],
                                 [kf_[2 * q], kf_[2 * q + 1]], 128, (CB, NA))
                        for s_, gr in enumerate(grs):
                            Yr, Yi, pPr, pPi = PR[s_], PI[s_], bankR[s_], bankI[s_]
                            for ch in range(CB):
                                S.op("pe", lambda e, ch=ch, Yr=Yr, pPr=pPr: e.matmul(
                                    pPr[0:NA, ch * 128:(ch + 1) * 128], lhsT=Yr[:, ch * NA:(ch + 1) * NA], rhs=F128[:, 0, :],
                                    start=True, stop=False), reads=[Yr, F128], writes=[pPr])
                                S.op("pe", lambda e, ch=ch, Yi=Yi, pPr=pPr: e.matmul(
                                    pPr[0:NA, ch * 128:(ch + 1) * 128], lhsT=Yi[:, ch * NA:(ch + 1) * NA], rhs=F128[:, 1, :],
                                    start=False, stop=True), reads=[Yi, F128], writes=[pPr])
                                S.op("pe", lambda e, ch=ch, Yr=Yr, pPi=pPi: e.matmul(
                                    pPi[0:NA, ch * 128:(ch + 1) * 128], lhsT=Yr[:, ch * NA:(ch + 1) * NA], rhs=F128[:, 2, :],
                                    start=True, stop=False), reads=[Yr, F128], writes=[pPi])
                                S.op("pe", lambda e, ch=ch, Yi=Yi, pPi=pPi: e.matmul(
                                    pPi[0:NA, ch * 128:(ch + 1) * 128], lhsT=Yi[:, ch * NA:(ch + 1) * NA], rhs=F128[:, 0, :],
                                    start=False, stop=True), reads=[Yi, F128], writes=[pPi])
                        for s_, gr in enumerate(grs):
                            tw_products(s_, ITr, ITi, NA, (CB, 128))
                        for s_, gr in enumerate(grs):
                            m_, pY = Mq[s_], bankR[s_]
                            for n_, (ci, mi) in enumerate(((0, 0), (2, 1), (1, 2), (1, 3))):
                                S.op("pe", lambda e, n_=n_, ci=ci, mi=mi, m_=m_, pY=pY: e.matmul(
                                    pY[0:NZ, 0:CB * 128], lhsT=CS[:, ci, :], rhs=m_[mi][0:NA, 0:CB * 128],
                                    start=(n_ == 0), stop=(n_ == 3)), reads=[CS, m_[mi]], writes=[pY])
                        for s_, gr in enumerate(grs):
                            mul_ = x1b[s_][par] if q == 0 else x2b[s_][par]
                            dst = zt[s_] if q == 0 else yo[s_]
                            pY = bankR[s_]
                            S.op("dve", lambda e, mul_=mul_, dst=dst, pY=pY: e.tensor_tensor(
                                out=dst[:, :, :], in0=pY[0:NZ, 0:CB * 128].rearrange("p (c x) -> p c x", c=CB), in1=mul_[:, :, :], op=ALU.mult),
                                reads=[pY, mul_], writes=[dst])
                    for s_, gr in enumerate(grs):
                        S.dma("pool", blk(P.HY, gr * CB), yo[s_][:, :, :], reads=[yo[s_]])
                S.barrier()

        def do_ctx(tag, Lf, base, Dd):
            with contextlib.ExitStack() as stc:
                DF = sb(stc, "cDF", [128, 2, 2, 512], BF16)
                DG = sb(stc, "cDG", [128, 2, 4, 256], BF16)
                S.dma("sp", DF[:, :, :, :], I["hy_DF"][:, :, :].rearrange("m (tt p) k -> p m tt k", p=128), writes=[DF])
                S.dma("sp", DG[:, :, :, :], I["hy_DG"][:, :, :].rearrange("m (kc p) t -> p m kc t", p=128), writes=[DG])
                identb = sb(stc, "cidb", [128, 128], BF16)
                S.op("dve", lambda e: e.tensor_copy(out=identb[:, :], in_=ident[:, :]), reads=[ident], writes=[identb])
                rnb = sb(stc, "crnb", [128, 2, D])
                for filt in range(2):
                    S.dma("sp", rnb[:, filt, :], Dd["RN"][filt, :].partition_broadcast(128), writes=[rnb])
                fm16 = [sb(stc, "cfm16_%d" % k, [128, KC, 256], BF16) for k in range(2)]
                fm32 = [sb(stc, "cfm32_%d" % k, [128, KC, 256]) for k in range(2)]
                tok16 = [sb(stc, "ctok16_%d" % k, [128, 2, D], BF16) for k in range(2)]
                x1t = sb(stc, "cx1t", [128, 2, D])
                x2t = sb(stc, "cx2t", [128, 2, D])
                ytk = sb(stc, "cytk", [128, 2, D])
                Kc = [sb(stc, "cKc%d" % k, [128, 4, D], BF16) for k in range(4)]
                Yr = sb(stc, "cYr", [128, 4, D], BF16)
                Yi = sb(stc, "cYi", [128, 4, D], BF16)
                ktmp = sb(stc, "cktmp", [128, 512])
                pTb = bankI[NSLOT - 1]
                pTb_ap = pTb[:, :].bitcast(BF16)

                def to_tok16(src_ap, dst, k):
                    f_ = fm16[k % 2]
                    S.dma("sp", f_[:, :, :], src_ap, writes=[f_])
                    for tt in range(2):
                        for dc in range(KC):
                            S.op("pe", lambda e, tt=tt, dc=dc, f_=f_: e.transpose(
                                pTb_ap[:, dc * 128:(dc + 1) * 128], f_[:, dc, tt * 128:(tt + 1) * 128], identb[:, :]),
                                reads=[f_, identb], writes=[pTb])
                        S.op("act", lambda e, tt=tt, dst=dst: e.copy(out=dst[:, tt, :], in_=pTb_ap[:, :]), reads=[pTb], writes=[dst])

                def to_tok32(src_ap, dst, k):
                    f_ = fm32[k % 2]
                    S.dma("sp", f_[:, :, :], src_ap, writes=[f_])
                    for tt in range(2):
                        for hf_ in range(2):
                            pb = bankR[hf_]
                            for dq in range(4):
                                dc = hf_ * 4 + dq
                                S.op("pe", lambda e, tt=tt, dc=dc, dq=dq, f_=f_, pb=pb: e.transpose(
                                    pb[:, dq * 128:(dq + 1) * 128], f_[:, dc, tt * 128:(tt + 1) * 128], ident[:, :]),
                                    reads=[f_, ident], writes=[pb])
                            S.op("act", lambda e, tt=tt, hf_=hf_, dst=dst, pb=pb: e.copy(
                                out=dst[:, tt, hf_ * 512:(hf_ + 1) * 512], in_=pb[:, :]), reads=[pb], writes=[dst])

                def ctx_src(T):
                    return T[:, :, base:base + Lf].rearrange("c p t -> p c t")

                n = 0
                for filt in range(2):
                    for sd in range(2):
                        tk = tok16[n % 2]
                        to_tok16(Dd["KS"][filt, sd, :, :, :].rearrange("c p t -> p c t"), tk, n)
                        for kc in range(4):
                            for hf_ in range(2):
                                pb = bankI[(kc * 2 + hf_) % NSLOT]
                                for tt in range(2):
                                    S.op("pe", lambda e, pb=pb, sd=sd, tt=tt, kc=kc, hf_=hf_, tk=tk: e.matmul(
                                        pb[:, :], lhsT=DF[:, sd, tt, kc * 128:(kc + 1) * 128], rhs=tk[:, tt, hf_ * 512:(hf_ + 1) * 512],
                                        start=(tt == 0), stop=(tt == 1)), reads=[DF, tk], writes=[pb])
                                dstK = Kc[filt * 2 + sd]
                                if sd == 0:
                                    S.op("dve", lambda e, pb=pb, filt=filt, hf_=hf_: e.tensor_tensor(
                                        out=ktmp[:, :], in0=pb[:, :], in1=rnb[:, filt, hf_ * 512:(hf_ + 1) * 512], op=ALU.mult),
                                        reads=[pb, rnb], writes=[ktmp])
                                    S.op("pool", lambda e, filt=filt, hf_=hf_, kc=kc, dstK=dstK: e.tensor_tensor(
                                        out=dstK[:, kc, hf_ * 512:(hf_ + 1) * 512], in0=ktmp[:, :],
                                        in1=bias_bc[:, filt, hf_ * 512:(hf_ + 1) * 512], op=ALU.add), reads=[ktmp, bias_bc], writes=[dstK])
                                else:
                                    S.op("dve", lambda e, pb=pb, filt=filt, hf_=hf_, kc=kc, dstK=dstK: e.tensor_tensor(
                                        out=dstK[:, kc, hf_ * 512:(hf_ + 1) * 512], in0=pb[:, :],
                                        in1=rnb[:, filt, hf_ * 512:(hf_ + 1) * 512], op=ALU.mult), reads=[pb, rnb], writes=[dstK])
                        n += 1
                to_tok16(ctx_src(P.HV), tok16[0], 0)
                to_tok32(ctx_src(P.HX1), x1t, 0)
                to_tok32(ctx_src(P.HX2), x2t, 1)
                for q in range(2):
                    xin_ = tok16[q]
                    mul_ = x1t if q == 0 else x2t
                    for kc in range(4):
                        for hf_ in range(2):
                            s_ = (kc * 2 + hf_) % NSLOT
                            pR, pI = bankR[s_], bankI[s_]
                            for ri, pb in ((0, pR), (1, pI)):
                                for tt in range(2):
                                    S.op("pe", lambda e, pb=pb, ri=ri, tt=tt, kc=kc, hf_=hf_, xin_=xin_: e.matmul(
                                        pb[:, :], lhsT=DF[:, ri, tt, kc * 128:(kc + 1) * 128], rhs=xin_[:, tt, hf_ * 512:(hf_ + 1) * 512],
                                        start=(tt == 0), stop=(tt == 1)), reads=[DF, xin_], writes=[pb])
                            Kr_, Ki_ = Kc[2 * q], Kc[2 * q + 1]
                            mt = mts[s_]
                            sl = slice(hf_ * 512, (hf_ + 1) * 512)
                            combos = ((pR, Kr_), (pI, Ki_), (pR, Ki_), (pI, Kr_))
                            for k_, (A_, C_) in enumerate(combos):
                                S.op("dve", lambda e, k_=k_, A_=A_, C_=C_, kc=kc, sl=sl, mt=mt: e.tensor_tensor(
                                    out=mt[k_][:, :], in0=A_[:, :], in1=C_[:, kc, sl], op=ALU.mult), reads=[A_, C_], writes=[mt[k_]])
                            S.op("pool", lambda e, mt=mt, kc=kc, sl=sl: e.tensor_tensor(
                                out=Yr[:, kc, sl], in0=mt[0][:, :], in1=mt[1][:, :], op=ALU.subtract), reads=[mt[0], mt[1]], writes=[Yr])
                            S.op("pool", lambda e, mt=mt, kc=kc, sl=sl: e.tensor_tensor(
                                out=Yi[:, kc, sl], in0=mt[2][:, :], in1=mt[3][:, :], op=ALU.add), reads=[mt[2], mt[3]], writes=[Yi])
                    for tt in range(2):
                        for hf_ in range(2):
                            pb = bankR[(tt * 2 + hf_) % NSLOT]
                            sl = slice(hf_ * 512, (hf_ + 1) * 512)
                            for kc in range(4):
                                S.op("pe", lambda e, pb=pb, kc=kc, tt=tt, sl=sl: e.matmul(
                                    pb[:, :], lhsT=DG[:, 0, kc, tt * 128:(tt + 1) * 128], rhs=Yr[:, kc, sl],
                                    start=(kc == 0), stop=False), reads=[DG, Yr], writes=[pb])
                                S.op("pe", lambda e, pb=pb, kc=kc, tt=tt, sl=sl: e.matmul(
                                    pb[:, :], lhsT=DG[:, 1, kc, tt * 128:(tt + 1) * 128], rhs=Yi[:, kc, sl],
                                    start=False, stop=(kc == 3)), reads=[DG, Yi], writes=[pb])
                            dst = tok16[1] if q == 0 else ytk
                            S.op("dve", lambda e, pb=pb, tt=tt, sl=sl, dst=dst, mul_=mul_: e.tensor_tensor(
                                out=dst[:, tt, sl], in0=pb[:, :], in1=mul_[:, tt, sl], op=ALU.mult), reads=[pb, mul_], writes=[dst])
                yfm = fm32[0]
                for dc in range(KC):
                    pb = bankI[dc % NSLOT]
                    for tt in range(2):
                        S.op("pe", lambda e, pb=pb, tt=tt, dc=dc: e.transpose(
                            pb[:, tt * 128:(tt + 1) * 128], ytk[:, tt, dc * 128:(dc + 1) * 128], ident[:, :]),
                            reads=[ytk, ident], writes=[pb])
                    S.op("act", lambda e, pb=pb, dc=dc: e.copy(out=yfm[:, dc, :], in_=pb[:, 0:256]), reads=[pb], writes=[yfm])
                S.dma("pool", ctx_src(P.HY), yfm[:, :, :], reads=[yfm])
                S.barrier()

        do_ctx(*classes[0])
        do_lat(*classes[1])

    with contextlib.ExitStack() as st:
        wo = sb(st, "hwo", [128, KC, D], BF16)
        with contextlib.ExitStack() as st2:
            stg = [sb(st2, "hostg%d" % k, [128, D]) for k in range(2)]
            for kc in range(KC):
                sg_ = stg[kc % 2]
                S.dma("sp", sg_[:, :], I["hy_w_out"][jj, kc * 128:(kc + 1) * 128, :], writes=[sg_])
                S.op("dve", lambda e, sg_=sg_, kc=kc: e.tensor_copy(out=wo[:, kc, :], in_=sg_[:, :]), reads=[sg_], writes=[wo])
            S.barrier()
        yT = [sb(st, "hyT%d" % k, [128, KC, 256]) for k in range(2)]
        yT16 = [sb(st, "hyT16%d" % k, [128, KC, 256], BF16) for k in range(2)]
        xb = [sb(st, "hxb%d" % k, [128, 2, D]) for k in range(2)]
        ep = [sb(st, "hep%d" % k, [128, 512]) for k in range(2)]
        m5bc = sb(st, "hm5", [128, D])
        bankX = [ps(st, "hX%d" % k) for k in range(4)]
        cur_j = None
        for b in range(NB):
            j = 1 if b == 0 else 0
            if j != cur_j:
                S.dma("sp", m5bc[:, :], P.Mrows[j, 5 * D:6 * D].partition_broadcast(128), reads=[P.Mb], writes=[m5bc])
                cur_j = j
            y_, y16, xt = yT[b % 2], yT16[b % 2], xb[b % 2]
            S.dma("sp", y_[:, :, :], P.HY[:, :, b * 256:(b + 1) * 256].rearrange("c p t -> p c t"), writes=[y_])
            S.dma("sp", xt[:], group_src(P, b, False), reads=[P.Xb[b]], writes=[xt])
            S.op("act", lambda e, y_=y_, y16=y16: e.copy(out=y16[:, :, :], in_=y_[:, :, :]), reads=[y_], writes=[y16])
            for s in range(2):
                for hh in range(2):
                    pb = bankX[s * 2 + hh]
                    for kc in range(KC):
                        S.op("pe", lambda e, pb=pb, s=s, hh=hh, kc=kc, y16=y16: e.matmul(
                            pb[:, :], lhsT=y16[:, kc, s * 128:(s + 1) * 128], rhs=wo[:, kc, hh * 512:(hh + 1) * 512],
                            start=(kc == 0), stop=(kc == KC - 1)), reads=[y16, wo], writes=[pb])
                    epb = ep[hh]
                    S.op("dve", lambda e, pb=pb, epb=epb, hh=hh: e.tensor_tensor(
                        out=epb[:, :], in0=pb[:, :], in1=m5bc[:, hh * 512:(hh + 1) * 512], op=ALU.mult), reads=[pb, m5bc], writes=[epb])
                    S.op("pool", lambda e, epb=epb, xt=xt, s=s, hh=hh: e.tensor_tensor(
                        out=xt[:, s, hh * 512:(hh + 1) * 512], in0=epb[:, :], in1=xt[:, s, hh * 512:(hh + 1) * 512], op=ALU.add),
                        reads=[epb, xt], writes=[xt])
            S.dma("pool", P.X[b * 256:(b + 1) * 256, :].rearrange("(s p) d -> p s d", p=128), xt[:], reads=[xt], writes=[P.Xb[b]])
        S.barrier()
```
